# Optimizing a Trainium2 kernel written in Bass

```python
import math
import jax, jax.numpy as jnp
from jax import lax
import numpy as np

D_MODEL = 1024
BATCH = 16
SEQ = 4096
DEPTH = 2

CHUNK = 64
N_LEFT_CHUNKS = 8
D_S5 = D_MODEL // 4
S5_GROUP = 16
S5_GROUPS = D_S5 // S5_GROUP
S5_STATE = 64
D_ATT = D_MODEL // 2
ATT_HEAD_DIM = 64
ATT_HEADS = D_ATT // ATT_HEAD_DIM
MAX_REL = 128
D_CONV = D_MODEL // 4
CONV_WIDTH = 31
N_BRANCHES = 3
D_FF = 2816
EPS = 1e-6
DT_MIN = 1e-3
DT_MAX = 1e-1

SPLIT_POINTS = [D_S5, D_S5 + D_ATT, D_S5 + 2 * D_ATT, D_S5 + 3 * D_ATT,
                D_S5 + 3 * D_ATT + 2 * D_CONV]
IN_COLS = D_S5 + 3 * D_ATT + 2 * D_CONV + N_BRANCHES * D_MODEL

kernel_name = "hybrid_s5_chunkattn_conformer_conv_gated"


def rmsnorm(x, g):
    xf = x.astype(jnp.float32)
    y = xf * lax.rsqrt(jnp.mean(xf * xf, axis=-1, keepdims=True) + EPS)
    return (y * g.astype(jnp.float32)).astype(x.dtype)


def layernorm(x, g, b):
    xf = x.astype(jnp.float32)
    mu = jnp.mean(xf, axis=-1, keepdims=True)
    var = jnp.mean(jnp.square(xf - mu), axis=-1, keepdims=True)
    y = (xf - mu) * lax.rsqrt(var + EPS)
    return (y * g.astype(jnp.float32) + b.astype(jnp.float32)).astype(x.dtype)


def swiglu_ffn(x, w_up, w_down):
    a, b = jnp.split(x @ w_up, 2, axis=-1)
    return (jax.nn.silu(a) * b) @ w_down


def _complex_linear_combine(e1, e2):
    a1r, a1i, b1r, b1i = e1
    a2r, a2i, b2r, b2i = e2
    ar = a2r * a1r - a2i * a1i
    ai = a2r * a1i + a2i * a1r
    br = a2r * b1r - a2i * b1i + b2r
    bi = a2r * b1i + a2i * b1r + b2i
    return (ar, ai, br, bi)


def s5_mixer(u, lambda_re, lambda_im, log_dt, b_re, b_im, c_re, c_im, d_skip, w_glu):
    bsz, seq, _ = u.shape
    f32 = jnp.float32
    uf = u.astype(f32).reshape(bsz, seq, S5_GROUPS, S5_GROUP)
    lr = jnp.minimum(lambda_re.astype(f32), -1e-4)
    li = lambda_im.astype(f32)
    dt = jnp.exp(log_dt.astype(f32))[:, None]
    mag = jnp.exp(lr * dt)
    ar = mag * jnp.cos(li * dt)
    ai = mag * jnp.sin(li * dt)
    den = lr * lr + li * li
    coef_r = ((ar - 1.0) * lr + ai * li) / den
    coef_i = (ai * lr - (ar - 1.0) * li) / den
    br = b_re.astype(f32)
    bi = b_im.astype(f32)
    bbar_r = coef_r[..., None] * br - coef_i[..., None] * bi
    bbar_i = coef_r[..., None] * bi + coef_i[..., None] * br
    bu_r = jnp.einsum('bsgc,gpc->bsgp', uf, bbar_r)
    bu_i = jnp.einsum('bsgc,gpc->bsgp', uf, bbar_i)
    a_r = jnp.broadcast_to(ar, bu_r.shape)
    a_i = jnp.broadcast_to(ai, bu_i.shape)
    _, _, xr, xi = lax.associative_scan(_complex_linear_combine, (a_r, a_i, bu_r, bu_i), axis=1)
    y = (jnp.einsum('bsgp,gcp->bsgc', xr, c_re.astype(f32))
         - jnp.einsum('bsgp,gcp->bsgc', xi, c_im.astype(f32))
         + d_skip.astype(f32).reshape(S5_GROUPS, S5_GROUP) * uf)
    y = jax.nn.gelu(y.reshape(bsz, seq, D_S5)).astype(u.dtype)
    a, g = jnp.split(y @ w_glu, 2, axis=-1)
    return a * jax.nn.sigmoid(g)


def chunked_attention(q, k, v, q_gain, k_gain, rel_bias):
    bsz, seq, _ = q.shape
    n_chunks = seq // CHUNK
    q = rmsnorm(q.reshape(bsz, seq, ATT_HEADS, ATT_HEAD_DIM), q_gain)
    k = rmsnorm(k.reshape(bsz, seq, ATT_HEADS, ATT_HEAD_DIM), k_gain)
    v = v.reshape(bsz, seq, ATT_HEADS, ATT_HEAD_DIM)
    pad = N_LEFT_CHUNKS * CHUNK
    band = (N_LEFT_CHUNKS + 1) * CHUNK
    k_pad = jnp.pad(k, ((0, 0), (pad, 0), (0, 0), (0, 0)))
    v_pad = jnp.pad(v, ((0, 0), (pad, 0), (0, 0), (0, 0)))
    qi = jnp.arange(CHUNK)[:, None]
    kj = jnp.arange(band)[None, :]
    rel = jnp.clip(pad + qi - kj, -MAX_REL, MAX_REL) + MAX_REL
    bias = rel_bias.astype(jnp.float32)[:, rel]
    scale = ATT_HEAD_DIM ** -0.5

    def one_chunk(c):
        start = c * CHUNK
        qc = lax.dynamic_slice_in_dim(q, start, CHUNK, axis=1)
        kc = lax.dynamic_slice_in_dim(k_pad, start, band, axis=1)
        vc = lax.dynamic_slice_in_dim(v_pad, start, band, axis=1)
        s = jnp.einsum('bqhd,bkhd->bhqk', qc, kc).astype(jnp.float32) * scale + bias[None]
        valid = kj >= (pad - start)
        s = jnp.where(valid[None, None], s, -1e30)
        p = jax.nn.softmax(s, axis=-1).astype(vc.dtype)
        return jnp.einsum('bhqk,bkhd->bqhd', p, vc)

    out = lax.map(one_chunk, jnp.arange(n_chunks))
    return out.transpose(1, 0, 2, 3, 4).reshape(bsz, seq, D_ATT)


def conv_module(z, w_dw, b_dw, ln_g, ln_b, w_pw):
    a, g = jnp.split(z, 2, axis=-1)
    h = a * jax.nn.sigmoid(g)
    h = lax.conv_general_dilated(
        h, w_dw[:, None, :], window_strides=(1,), padding=[(CONV_WIDTH - 1, 0)],
        dimension_numbers=('NWC', 'WIO', 'NWC'), feature_group_count=D_CONV) + b_dw
    h = jax.nn.silu(layernorm(h, ln_g, ln_b))
    return h @ w_pw


def setup_inputs(seed: int = 0) -> dict:
    key = jax.random.key(seed)
    ks = jax.random.split(key, 32)
    L = DEPTH

    def nrm(k, shape, scale):
        return jax.random.normal(k, shape, jnp.float32) * scale

    n_idx = jnp.arange(S5_STATE, dtype=jnp.float32)
    lam_im = jnp.broadcast_to(math.pi * n_idx, (L, S5_GROUPS, S5_STATE))
    return {
        "x": nrm(ks[0], (BATCH, SEQ, D_MODEL), 1.0),
        "ffn1_norm": 1.0 + nrm(ks[1], (L, D_MODEL), 0.02),
        "ffn1_w_up": nrm(ks[2], (L, D_MODEL, 2 * D_FF), D_MODEL ** -0.5),
        "ffn1_w_down": nrm(ks[3], (L, D_FF, D_MODEL), D_FF ** -0.5),
        "mix_norm": 1.0 + nrm(ks[4], (L, D_MODEL), 0.02),
        "w_in": nrm(ks[5], (L, D_MODEL, IN_COLS), D_MODEL ** -0.5),
        "b_gate": nrm(ks[6], (L, N_BRANCHES * D_MODEL), 0.01),
        "s5_lambda_re": -0.5 + nrm(ks[7], (L, S5_GROUPS, S5_STATE), 0.01),
        "s5_lambda_im": lam_im + nrm(ks[8], (L, S5_GROUPS, S5_STATE), 0.01),
        "s5_log_dt": jax.random.uniform(ks[9], (L, S5_GROUPS), jnp.float32,
                                        math.log(DT_MIN), math.log(DT_MAX)),
        "s5_b_re": nrm(ks[10], (L, S5_GROUPS, S5_STATE, S5_GROUP), (2 * S5_GROUP) ** -0.5),
        "s5_b_im": nrm(ks[11], (L, S5_GROUPS, S5_STATE, S5_GROUP), (2 * S5_GROUP) ** -0.5),
        "s5_c_re": nrm(ks[12], (L, S5_GROUPS, S5_GROUP, S5_STATE), (2 * S5_STATE) ** -0.5),
        "s5_c_im": nrm(ks[13], (L, S5_GROUPS, S5_GROUP, S5_STATE), (2 * S5_STATE) ** -0.5),
        "s5_d": nrm(ks[14], (L, D_S5), 1.0),
        "s5_w_glu": nrm(ks[15], (L, D_S5, 2 * D_S5), D_S5 ** -0.5),
        "w_br_s5": nrm(ks[16], (L, D_S5, D_MODEL), D_S5 ** -0.5),
        "attn_q_gain": 1.0 + nrm(ks[17], (L, ATT_HEAD_DIM), 0.02),
        "attn_k_gain": 1.0 + nrm(ks[18], (L, ATT_HEAD_DIM), 0.02),
        "attn_rel_bias": nrm(ks[19], (L, ATT_HEADS, 2 * MAX_REL + 1), 0.1),
        "w_br_attn": nrm(ks[20], (L, D_ATT, D_MODEL), D_ATT ** -0.5),
        "conv_w_dw": nrm(ks[21], (L, CONV_WIDTH, D_CONV), CONV_WIDTH ** -0.5),
        "conv_b_dw": nrm(ks[22], (L, D_CONV), 0.01),
        "conv_ln_g": 1.0 + nrm(ks[23], (L, D_CONV), 0.02),
        "conv_ln_b": nrm(ks[24], (L, D_CONV), 0.01),
        "w_br_conv": nrm(ks[25], (L, D_CONV, D_MODEL), D_CONV ** -0.5),
        "w_out": nrm(ks[26], (L, D_MODEL, D_MODEL), D_MODEL ** -0.5),
        "ffn2_norm": 1.0 + nrm(ks[27], (L, D_MODEL), 0.02),
        "ffn2_w_up": nrm(ks[28], (L, D_MODEL, 2 * D_FF), D_MODEL ** -0.5),
        "ffn2_w_down": nrm(ks[29], (L, D_FF, D_MODEL), D_FF ** -0.5),
    }


def reference(x, ffn1_norm, ffn1_w_up, ffn1_w_down, mix_norm, w_in, b_gate,
              s5_lambda_re, s5_lambda_im, s5_log_dt, s5_b_re, s5_b_im, s5_c_re, s5_c_im,
              s5_d, s5_w_glu, w_br_s5, attn_q_gain, attn_k_gain, attn_rel_bias, w_br_attn,
              conv_w_dw, conv_b_dw, conv_ln_g, conv_ln_b, w_br_conv, w_out,
              ffn2_norm, ffn2_w_up, ffn2_w_down):
    for l in range(DEPTH):
        x = x + 0.5 * swiglu_ffn(rmsnorm(x, ffn1_norm[l]), ffn1_w_up[l], ffn1_w_down[l])

        h = rmsnorm(x, mix_norm[l])
        proj = h @ w_in[l]
        u_s5, q, k, v, z_conv, gate_logits = jnp.split(proj, SPLIT_POINTS, axis=-1)

        y_s5 = s5_mixer(u_s5, s5_lambda_re[l], s5_lambda_im[l], s5_log_dt[l],
                        s5_b_re[l], s5_b_im[l], s5_c_re[l], s5_c_im[l],
                        s5_d[l], s5_w_glu[l]) @ w_br_s5[l]
        y_attn = chunked_attention(q, k, v, attn_q_gain[l], attn_k_gain[l],
                                   attn_rel_bias[l]) @ w_br_attn[l]
        y_conv = conv_module(z_conv, conv_w_dw[l], conv_b_dw[l], conv_ln_g[l],
                             conv_ln_b[l], w_br_conv[l])

        gates = jax.nn.sigmoid(gate_logits + b_gate[l])
        g_s5, g_attn, g_conv = jnp.split(gates, N_BRANCHES, axis=-1)
        merged = g_s5 * y_s5 + g_attn * y_attn + g_conv * y_conv
        x = x + merged @ w_out[l]

        x = x + 0.5 * swiglu_ffn(rmsnorm(x, ffn2_norm[l]), ffn2_w_up[l], ffn2_w_down[l])
    return x
```

```python
import math
import numpy as np
from contextlib import ExitStack
import concourse.bass as bass
import concourse.mybir as mybir
from concourse.bass_utils import run_bass_kernel_spmd

F32 = mybir.dt.float32
BF16 = mybir.dt.bfloat16
AF = mybir.ActivationFunctionType
ALU = mybir.AluOpType

COMPUTE = ("pe", "act", "dve", "pool")
ALL_ENG = COMPUTE + ("sp",)


class Reg:
    __slots__ = ("name", "w", "readers", "excl")

    def __init__(self, name, excl=False):
        self.name = name
        self.w = None
        self.readers = {}
        self.excl = excl


class Op:
    __slots__ = ("eng", "fn", "waits", "tok", "needed", "dma")

    def __init__(self, eng, fn, waits, tok, dma):
        self.eng = eng
        self.fn = fn
        self.waits = waits
        self.tok = tok
        self.needed = False
        self.dma = dma


class _Rec:
    def __init__(self):
        self.calls = []

    def __getattr__(self, name):
        def f(*a, **k):
            self.calls.append((name, a, k))
        return f


class Plan:
    def __init__(self):
        self.ops = {e: [] for e in ALL_ENG}
        self.cnt = {e: 0 for e in COMPUTE}
        self.seen = {e: {} for e in ALL_ENG}
        self.dma_cnt = {}
        self.tokmap = {}
        self.allregs = {}
        self.ext = {}

    def _need(self, eng, waits, tok, same_ok):
        if tok is None:
            return
        key, val, teng = tok
        if teng == eng and same_ok and eng == "pe":
            return
        if self.seen[eng].get(key, 0) >= val:
            return
        if waits.get(key, 0) < val:
            waits[key] = val

    def op(self, eng, fn, reads=(), writes=(), dma_sem=None):
        waits = {}
        for r in reads:
            self.allregs[id(r)] = r
        for r in writes:
            self.allregs[id(r)] = r
        for r in reads:
            if r.excl:
                self._need(eng, waits, r.w, True)
                for k, (v, e) in r.readers.items():
                    self._need(eng, waits, (k, v, e), True)
            else:
                self._need(eng, waits, r.w, False)
        for w in writes:
            self._need(eng, waits, w.w, True)
            for k, (v, e) in w.readers.items():
                self._need(eng, waits, (k, v, e), True)
        if dma_sem is not None:
            val = self.dma_cnt.get(dma_sem, 0) + 16
            self.dma_cnt[dma_sem] = val
            tok = (dma_sem, val, "dma:" + dma_sem)
        else:
            self.cnt[eng] += 1
            tok = (eng, self.cnt[eng], eng)
        rec = _Rec()
        fn(rec)
        assert len(rec.calls) == 1
        o = Op(eng, rec.calls[0], waits, tok, dma_sem is not None)
        self.tokmap[(tok[0], tok[1])] = o
        for k, v in waits.items():
            self.seen[eng][k] = v
            t = self.tokmap.get((k, v))
            if t is not None:
                t.needed = True
        self.ops[eng].append(o)
        for r in reads:
            old = r.readers.get(tok[0])
            if old is None or old[0] < tok[1]:
                r.readers[tok[0]] = (tok[1], tok[2])
        for w in writes:
            w.w = tok
            w.readers = {}
        return o

    def final_wait(self, eng, regs):
        waits = {}
        for r in regs:
            self._need(eng, waits, r.w, True)
            for k, (v, e) in r.readers.items():
                self._need(eng, waits, (k, v, e), True)
        o = Op(eng, None, waits, None, False)
        for k, v in waits.items():
            self.seen[eng][k] = max(self.seen[eng].get(k, 0), v)
            t = self.tokmap.get((k, v))
            if t is not None:
                t.needed = True
        self.ops[eng].append(o)

    def barrier(self):
        regs = list(self.allregs.values())
        for e_ in ALL_ENG:
            self.final_wait(e_, regs)

    def emit(self, nc, M=3000):
        remap = {}
        nep = {}
        for e in COMPUTE:
            c = 0
            m = {}
            for o in self.ops[e]:
                if o.tok is not None and not o.dma and o.needed:
                    m[o.tok[1]] = (c // M, c % M + 1)
                    c += 1
            remap[e] = m
            nep[e] = (c + M - 1) // M
        with ExitStack() as es:
            sems = {}
            for e in COMPUTE:
                for k in range(max(1, nep[e])):
                    sems[(e, k)] = es.enter_context(nc.semaphore("s_%s%d" % (e, k)))
            for k, tot in self.dma_cnt.items():
                if k in self.ext:
                    sems[(k, 0)] = self.ext[k]
                    continue
                n = (tot // 16 + M - 1) // M
                for j in range(max(1, n)):
                    sems[(k, j)] = es.enter_context(nc.semaphore("d_%s_%d" % (k, j)))
            self.nsems = len(sems)
            block = es.enter_context(nc.Block())
            plan = self

            def sv(k, v):
                if k in remap:
                    ep, val = remap[k][v]
                    return sems[(k, ep)], val
                n = v // 16 - 1
                return sems[(k, n // M)], (n % M + 1) * 16

            def run(engname):
                def body(eng):
                    for o in plan.ops[engname]:
                        for k, v in o.waits.items():
                            s_, v_ = sv(k, v)
                            eng.wait_ge(s_, v_)
                        if o.fn is None:
                            continue
                        name_, a_, k_ = o.fn
                        ins = getattr(eng, name_)(*a_, **k_)
                        if o.dma:
                            s_, _ = sv(o.tok[0], o.tok[1])
                            ins.then_inc(s_, 16)
                        elif o.needed:
                            s_, _ = sv(o.tok[0], o.tok[1])
                            ins.then_inc(s_, 1)
                return body

            block.tensor(run("pe"))
            block.scalar(run("act"))
            block.vector(run("dve"))
            block.gpsimd(run("pool"))
            block.sync(run("sp"))


D = 1024
KC = 8
DFF = 2816
HC = 22
T = 512
L = 2
EPS = 1e-6
IN_COLS = 5376
NPV = 144
BIAS_PE = False
SLOTW = 4096
PV_N1, PV_NM, PV_N2, PV_BG, PV_D, PV_CB, PV_LG, PV_LB, PV_CW, PV_QG, PV_KG, PV_LR, PV_LI, PV_DT = (
    0, 8, 16, 24, 48, 50, 52, 54, 56, 118, 119, 120, 128, 136)


def layer_pieces():
    p = []
    for f in (1, 2):
        pass
    ffn = lambda f: [("f%du%d" % (f, j), 4096) for j in range(11)] + \
        [("f%dd%d" % (f, j), 4096 if j < 5 else 2048) for j in range(6)]
    mix = [("inA", 4096), ("cw0", 3968), ("cw1", 3968), ("inQ", 4096), ("inK", 4096), ("inV", 4096), ("inU", 2048),
           ("bbar", 2048), ("cmat", 2048)]
    for hc in range(4):
        mix += [("me%d" % hc, 1280), ("tab%d" % (2 * hc), 2048), ("tab%d" % (2 * hc + 1), 2048)]
    mix += [("glu", 1024), ("brs", 4096), ("bra", 4096)] + [("g%d" % j, 3072) for j in range(8)] + \
           [("wo0", 4096), ("wo1", 4096)]
    return ffn(1) + mix + ffn(2)


PIECES = layer_pieces()
PIDX = {n: i for i, (n, w) in enumerate(PIECES)}
NPL = len(PIECES)


def build_program(NSEQ, S, dbg=None):
    dbg = dbg or {}
    NT = S // T
    nlayers = dbg.get("layers", L)
    nc = bass.Bass("TRN2", target_bir_lowering=False)
    dram_in = lambda n, s, d=F32: nc.dram_tensor(n, s, d, kind="ExternalInput").ap()
    x_d = dram_in("x", [NSEQ * S, D])
    y_d = nc.dram_tensor("y", [NSEQ * S, D], F32, kind="ExternalOutput").ap()
    w_up = [dram_in("ffn1_w_up", [L, D, 2 * DFF]), dram_in("ffn2_w_up", [L, D, 2 * DFF])]
    w_dn = [dram_in("ffn1_w_down", [L, DFF, D]), dram_in("ffn2_w_down", [L, DFF, D])]
    w_in = dram_in("w_in", [L, D, IN_COLS])
    w_glu = dram_in("s5_w_glu", [L, 256, 512])
    w_brs = dram_in("w_br_s5", [L, 256, D])
    w_bra = dram_in("w_br_attn", [L, 512, D])
    w_brc = dram_in("w_br_conv", [L, 256, D])
    w_out = dram_in("w_out", [L, D, D])
    pv_d = dram_in("pv", [L, 128, NPV])
    lamR_d = dram_in("lamR", [L, 128, 3, 1024])
    braw_d = dram_in("braw", [L, 128, 2048])
    craw_d = dram_in("craw", [L, 128, 2048])
    btoe_d = dram_in("btoe", [L, 128, 5120])
    mask_d = dram_in("mask01", [128, 1280])
    ident_d = dram_in("ident", [128, 128])
    wscr = nc.dram_tensor("wscr", [L * NPL, 128, SLOTW], BF16, kind="Internal").ap()
    tscr = nc.dram_tensor("tscr", [L * 8, 128, 1024], F32, kind="Internal").ap()

    es = ExitStack()
    sb = lambda n, s, d: es.enter_context(nc.sbuf_tensor("sb_" + n, s, d))
    ident = sb("ident", [128, 128], F32)
    ones_bf = sb("ones_bf", [128, 128], BF16)
    bones_bf = sb("bones_bf", [128, 128], BF16)
    identb = sb("identb", [128, 128], BF16)
    PV = [sb("pv%d" % l, [128, NPV], F32) for l in range(L)]
    s5c = [sb("s5c%d" % l, [128, 5, 8], F32) for l in range(L)]
    R_ident, R_const = Reg("ident"), Reg("const")
    R_pv = [Reg("pv%d" % l) for l in range(L)]
    R_s5c = [Reg("s5c%d" % l) for l in range(L)]
    R_init = [[Reg("init%d_%d" % (l, j)) for j in range(8)] for l in range(L)]
    R_wscr = [[Reg("wscr%d_%d" % (l, i)) for i in range(NPL)] for l in range(L)]

    def scr(l, name, width=None):
        i = PIDX[name]
        w = width or PIECES[i][1]
        return wscr[l * NPL + i, :, 0:w]

    def issue_casts(P):
        cast_groups = {}

        def cast(l, name, dst_sl, src, grp):
            i = PIDX[name]
            P.op("pool", lambda e: e.dma_start(out=dst_sl, in_=src), [], [], dma_sem=grp)
            cast_groups.setdefault(grp, set()).add((l, i))

        def pc(l, name, nk, width):
            return scr(l, name, nk * width).rearrange("p (k c) -> p k c", k=nk)

        for l in range(nlayers):
            for f in range(2):
                g = "w%d_%d" % (l, f * 2)
                for j in range(11):
                    d = pc(l, "f%du%d" % (f + 1, j), 8, 512)
                    for half in range(2):
                        src = w_up[f][l, :, half * DFF + 256 * j: half * DFF + 256 * j + 256].rearrange("(k p) c -> p k c", p=128)
                        cast(l, "f%du%d" % (f + 1, j), d[:, :, half * 256:(half + 1) * 256], src, g)
                for j in range(6):
                    nk = 4 if j < 5 else 2
                    d = pc(l, "f%dd%d" % (f + 1, j), nk, 1024)
                    src = w_dn[f][l, 512 * j: 512 * j + 128 * nk, :].rearrange("(k p) c -> p k c", p=128)
                    cast(l, "f%dd%d" % (f + 1, j), d, src, g)
                if f == 0:
                    g = "w%d_1" % l
                    wi = lambda c0, n_: w_in[l, :, c0:c0 + n_].rearrange("(k p) c -> p k c", p=128)
                    cast(l, "inA", pc(l, "inA", 8, 512), wi(1792, 512), g)
                    cast(l, "inQ", pc(l, "inQ", 8, 512), wi(256, 512), g)
                    cast(l, "inK", pc(l, "inK", 8, 512), wi(768, 512), g)
                    cast(l, "inV", pc(l, "inV", 8, 512), wi(1280, 512), g)
                    cast(l, "inU", pc(l, "inU", 8, 256), wi(0, 256), g)
                    cast(l, "glu", pc(l, "glu", 2, 512), w_glu[l, :, :].rearrange("(k p) c -> p k c", p=128), g)
                    d = pc(l, "brs", 4, 1024)
                    cast(l, "brs", d[:, 0:2, :], w_brs[l, :, :].rearrange("(k p) c -> p k c", p=128), g)
                    cast(l, "brs", d[:, 2:4, :], w_brc[l, :, :].rearrange("(k p) c -> p k c", p=128), g)
                    cast(l, "bra", pc(l, "bra", 4, 1024), w_bra[l, :, :].rearrange("(k p) c -> p k c", p=128), g)
                    for m in range(8):
                        d = pc(l, "g%d" % m, 8, 384)
                        for b in range(3):
                            cast(l, "g%d" % m, d[:, :, b * 128:(b + 1) * 128], wi(2304 + 1024 * b + 128 * m, 128), g)
                    for j in range(2):
                        cast(l, "wo%d" % j, pc(l, "wo%d" % j, 8, 512),
                             w_out[l, :, 512 * j:512 * j + 512].rearrange("(k p) c -> p k c", p=128), g)


        return cast_groups

    PI = math.pi
    P = Plan()
    for l in range(-1, nlayers):
        with ExitStack() as ss:
            st = lambda n, s, d=F32, l=l: ss.enter_context(nc.sbuf_tensor("st%d_%s" % (l + 1, n), s, d))
            regs_all = []

            def RG(n):
                r = Reg(n)
                regs_all.append(r)
                return r
            if l < 0:
                P.op("sp", lambda e: e.dma_start(out=ident[:], in_=ident_d[:, :]), [], [R_ident], dma_sem="c0")
                P.op("pool", lambda e: e.memset(ones_bf[:], 1.0), [], [R_const])
                P.op("pool", lambda e: e.memset(bones_bf[:], 0.0), [], [R_const])
                P.op("pool", lambda e: e.memset(bones_bf[0:64, 0:64], 1.0), [], [R_const])
                P.op("pool", lambda e: e.memset(bones_bf[64:128, 64:128], 1.0), [], [R_const])
                P.op("dve", lambda e: e.tensor_copy(out=identb[:], in_=ident[:]), [R_ident], [R_const])
                cast_groups = issue_casts(P)
                cast_fin = {g_: P.dma_cnt[g_] for g_ in cast_groups}
                continue
            maskt = st("maskt", [128, 1280])
            R_mask = RG("mask")
            P.op("sp", lambda e: e.dma_start(out=maskt[:], in_=mask_d[:, :]), [], [R_mask], dma_sem="c1")

            def s5_params(lr, li, ldt, shape, pfx, R):
                tl = {}

                def t(n):
                    tl[n] = st("%s_%s" % (pfx, n), shape)
                    return tl[n]
                V = lambda fn: P.op("dve", fn, [R], [R])
                A = lambda fn: P.op("act", fn, [R], [R])
                lrc, dt, a, mag, th = t("lrc"), t("dt"), t("a"), t("mag"), t("th")
                V(lambda e: e.tensor_scalar(out=lrc[:], in0=lr, scalar1=-1e-4, scalar2=None, op0=ALU.min))
                A(lambda e: e.activation(out=dt[:], in_=ldt, func=AF.Exp))
                V(lambda e: e.tensor_tensor(out=a[:], in0=lrc[:], in1=dt[:], op=ALU.mult))
                A(lambda e: e.activation(out=mag[:], in_=a[:], func=AF.Exp))
                V(lambda e: e.tensor_tensor(out=th[:], in0=li, in1=dt[:], op=ALU.mult))
                ths, thc0, thc, m = t("ths"), t("thc0"), t("thc"), a
                V(lambda e: e.tensor_copy(out=ths[:], in_=th[:]))
                V(lambda e: e.tensor_scalar(out=thc0[:], in0=th[:], scalar1=PI / 2, scalar2=None, op0=ALU.add))
                V(lambda e: e.tensor_copy(out=thc[:], in_=thc0[:]))
                for kk in range(5):
                    thr = (2 * kk + 1) * PI
                    V(lambda e: e.tensor_scalar(out=m[:], in0=th[:], scalar1=thr, scalar2=-2 * PI, op0=ALU.is_gt, op1=ALU.mult))
                    V(lambda e: e.tensor_tensor(out=ths[:], in0=ths[:], in1=m[:], op=ALU.add))
                    V(lambda e: e.tensor_scalar(out=m[:], in0=thc0[:], scalar1=thr, scalar2=-2 * PI, op0=ALU.is_gt, op1=ALU.mult))
                    V(lambda e: e.tensor_tensor(out=thc[:], in0=thc[:], in1=m[:], op=ALU.add))
                sn, cs = ths, thc
                A(lambda e: e.activation(out=sn[:], in_=ths[:], func=AF.Sin))
                A(lambda e: e.activation(out=cs[:], in_=thc[:], func=AF.Sin))
                n2, t2 = thc0, th
                V(lambda e: e.tensor_tensor(out=n2[:], in0=sn[:], in1=sn[:], op=ALU.mult))
                V(lambda e: e.tensor_tensor(out=t2[:], in0=cs[:], in1=cs[:], op=ALU.mult))
                V(lambda e: e.tensor_tensor(out=n2[:], in0=n2[:], in1=t2[:], op=ALU.add))
                A(lambda e: e.activation(out=n2[:], in_=n2[:], func=AF.Sqrt))
                V(lambda e: e.reciprocal(out=n2[:], in_=n2[:]))
                V(lambda e: e.tensor_tensor(out=sn[:], in0=sn[:], in1=n2[:], op=ALU.mult))
                V(lambda e: e.tensor_tensor(out=cs[:], in0=cs[:], in1=n2[:], op=ALU.mult))
                return {"lrc": lrc, "mag": mag, "sn": sn, "cs": cs, "f1": n2, "f2": t2, "f3": a, "f4": dt}

            P.op("sp", lambda e: e.dma_start(out=PV[l][:], in_=pv_d[l, :, :]), [], [R_pv[l]], dma_sem="c2")
            RS = RG("s5S")
            P.op("dve", lambda e: e.tensor_copy(out=s5c[l][:, 0, :], in_=PV[l][:, PV_LR:PV_LR + 8]), [R_pv[l]], [RS])
            tS = s5_params(PV[l][:, PV_LR:PV_LR + 8], PV[l][:, PV_LI:PV_LI + 8], PV[l][:, PV_DT:PV_DT + 8], [128, 8], "S", RS)
            V = lambda fn: P.op("dve", fn, [RS], [RS])
            V(lambda e: e.tensor_copy(out=s5c[l][:, 0, :], in_=tS["mag"][:]))
            Ct = st("Ctab", [128, 8, T])
            St = st("Stab", [128, 8, T])
            cn = st("cn", [128, 8, 1])
            sn_ = st("snn", [128, 8, 1])
            tA = st("tA", [128, 8, 256])
            tB = st("tB", [128, 8, 256])
            V(lambda e: e.memset(Ct[:, :, 0:1], 1.0))
            V(lambda e: e.memset(St[:, :, 0:1], 0.0))
            V(lambda e: e.tensor_copy(out=cn[:, :, 0], in_=tS["cs"][:]))
            V(lambda e: e.tensor_copy(out=sn_[:, :, 0], in_=tS["sn"][:]))
            n = 1
            while n < T:
                cb = cn[:, :, 0:1].to_broadcast([128, 8, n])
                sbb = sn_[:, :, 0:1].to_broadcast([128, 8, n])
                V(lambda e: e.tensor_tensor(out=tA[:, :, 0:n], in0=Ct[:, :, 0:n], in1=cb, op=ALU.mult))
                V(lambda e: e.tensor_tensor(out=tB[:, :, 0:n], in0=St[:, :, 0:n], in1=sbb, op=ALU.mult))
                V(lambda e: e.tensor_tensor(out=Ct[:, :, n:2 * n], in0=tA[:, :, 0:n], in1=tB[:, :, 0:n], op=ALU.subtract))
                V(lambda e: e.tensor_tensor(out=tA[:, :, 0:n], in0=St[:, :, 0:n], in1=cb, op=ALU.mult))
                V(lambda e: e.tensor_tensor(out=tB[:, :, 0:n], in0=Ct[:, :, 0:n], in1=sbb, op=ALU.mult))
                V(lambda e: e.tensor_tensor(out=St[:, :, n:2 * n], in0=tA[:, :, 0:n], in1=tB[:, :, 0:n], op=ALU.add))
                V(lambda e: e.tensor_tensor(out=tA[:, :, 0:1], in0=cn[:], in1=cn[:], op=ALU.mult))
                V(lambda e: e.tensor_tensor(out=tB[:, :, 0:1], in0=sn_[:], in1=sn_[:], op=ALU.mult))
                V(lambda e: e.tensor_tensor(out=tB[:, :, 1:2], in0=cn[:], in1=sn_[:], op=ALU.mult))
                V(lambda e: e.tensor_tensor(out=cn[:], in0=tA[:, :, 0:1], in1=tB[:, :, 0:1], op=ALU.subtract))
                V(lambda e: e.tensor_scalar(out=sn_[:], in0=tB[:, :, 1:2], scalar1=2.0, scalar2=None, op0=ALU.mult))
                n *= 2
            V(lambda e: e.tensor_copy(out=s5c[l][:, 2, :], in_=sn_[:, :, 0]))
            P.op("dve", lambda e: e.tensor_copy(out=s5c[l][:, 1, :], in_=cn[:, :, 0]), [RS], [RS, R_s5c[l]])
            for j in range(8):
                P.op("sp", lambda e: e.dma_start(out=tscr[l * 8 + j, :, 0:T], in_=Ct[:, j, :]),
                     [RS], [R_wscr[l][PIDX["tab%d" % j]]], dma_sem="tst")
                P.op("sp", lambda e: e.dma_start(out=tscr[l * 8 + j, :, T:2 * T], in_=St[:, j, :]),
                     [RS], [R_wscr[l][PIDX["tab%d" % j]]], dma_sem="tst")
            RR = RG("s5R")
            lam = st("lam", [128, 3, 1024])
            P.op("sp", lambda e: e.dma_start(out=lam[:], in_=lamR_d[l, :, :, :]), [], [RR], dma_sem="c3")
            braw = st("braw", [128, 2, 1024])
            P.op("sp", lambda e: e.dma_start(out=braw[:].rearrange("p a b -> p (a b)"), in_=braw_d[l, :, :]), [], [RR], dma_sem="c4")
            craw = st("craw", [128, 2, 1024])
            P.op("sp", lambda e: e.dma_start(out=craw[:].rearrange("p a b -> p (a b)"), in_=craw_d[l, :, :]), [], [RR], dma_sem="c5")
            bb = st("bb", [128, 2, 1024], BF16)
            V = lambda fn: P.op("dve", fn, [RR], [RR])
            for cc in range(2):
                c0 = cc * 512
                lrA, liA, dtA = lam[:, 0, c0:c0 + 512], lam[:, 1, c0:c0 + 512], lam[:, 2, c0:c0 + 512]
                tR = s5_params(lrA, liA, dtA, [128, 512], "R%d" % cc, RR)
                ar, ai, den, cr, ci, u1 = [st("%s%d" % (n_, cc), [128, 512]) for n_ in ("ar", "ai", "den", "cr", "ci", "u1")]
                lrc = tR["lrc"]
                V(lambda e: e.tensor_tensor(out=ar[:], in0=tR["mag"][:], in1=tR["cs"][:], op=ALU.mult))
                V(lambda e: e.tensor_tensor(out=ai[:], in0=tR["mag"][:], in1=tR["sn"][:], op=ALU.mult))
                V(lambda e: e.tensor_tensor(out=den[:], in0=lrc[:], in1=lrc[:], op=ALU.mult))
                V(lambda e: e.tensor_tensor(out=u1[:], in0=liA, in1=liA, op=ALU.mult))
                V(lambda e: e.tensor_tensor(out=den[:], in0=den[:], in1=u1[:], op=ALU.add))
                V(lambda e: e.reciprocal(out=den[:], in_=den[:]))
                V(lambda e: e.tensor_scalar(out=ar[:], in0=ar[:], scalar1=-1.0, scalar2=None, op0=ALU.add))
                V(lambda e: e.tensor_tensor(out=cr[:], in0=ar[:], in1=lrc[:], op=ALU.mult))
                V(lambda e: e.tensor_tensor(out=u1[:], in0=ai[:], in1=liA, op=ALU.mult))
                V(lambda e: e.tensor_tensor(out=cr[:], in0=cr[:], in1=u1[:], op=ALU.add))
                V(lambda e: e.tensor_tensor(out=cr[:], in0=cr[:], in1=den[:], op=ALU.mult))
                V(lambda e: e.tensor_tensor(out=ci[:], in0=ai[:], in1=lrc[:], op=ALU.mult))
                V(lambda e: e.tensor_tensor(out=u1[:], in0=ar[:], in1=liA, op=ALU.mult))
                V(lambda e: e.tensor_tensor(out=ci[:], in0=ci[:], in1=u1[:], op=ALU.subtract))
                V(lambda e: e.tensor_tensor(out=ci[:], in0=ci[:], in1=den[:], op=ALU.mult))
                bre, bim = braw[:, 0, c0:c0 + 512], braw[:, 1, c0:c0 + 512]
                V(lambda e: e.tensor_tensor(out=u1[:], in0=cr[:], in1=bre, op=ALU.mult))
                V(lambda e: e.tensor_tensor(out=den[:], in0=ci[:], in1=bim, op=ALU.mult))
                V(lambda e: e.tensor_tensor(out=bb[:, 0, c0:c0 + 512], in0=u1[:], in1=den[:], op=ALU.subtract))
                V(lambda e: e.tensor_tensor(out=u1[:], in0=cr[:], in1=bim, op=ALU.mult))
                V(lambda e: e.tensor_tensor(out=den[:], in0=ci[:], in1=bre, op=ALU.mult))
                V(lambda e: e.tensor_tensor(out=bb[:, 1, c0:c0 + 512], in0=u1[:], in1=den[:], op=ALU.add))
            P.op("sp", lambda e: e.dma_start(out=scr(l, "bbar"), in_=bb[:].rearrange("p a b -> p (a b)")),
                 [RR], [R_wscr[l][PIDX["bbar"]]], dma_sem="tst")
            cb_ = st("cb", [128, 2, 1024], BF16)
            V(lambda e: e.tensor_copy(out=cb_[:, 0, :], in_=craw[:, 0, :]))
            V(lambda e: e.tensor_scalar(out=cb_[:, 1, :], in0=craw[:, 1, :], scalar1=-1.0, scalar2=None, op0=ALU.mult))
            P.op("sp", lambda e: e.dma_start(out=scr(l, "cmat"), in_=cb_[:].rearrange("p a b -> p (a b)")),
                 [RR], [R_wscr[l][PIDX["cmat"]]], dma_sem="tst")
            RM = RG("me")
            bt = st("bt", [128, 8, 640])
            mb = st("mb", [128, 8, 640], BF16)
            P.op("sp", lambda e: e.dma_start(out=bt[:].rearrange("p a b -> p (a b)"), in_=btoe_d[l, :, :]), [], [RM], dma_sem="c6")
            if BIAS_PE:
                negm = st("negm", [128, 640])
                P.op("dve", lambda e: e.tensor_scalar(out=negm[:], in0=maskt[:, 0:640], scalar1=-1.0, scalar2=30000.0, op0=ALU.add, op1=ALU.mult), [R_mask], [RM])
                P.op("dve", lambda e: e.scalar_tensor_tensor(out=bt[:], in0=bt[:], scalar=8.0, in1=maskt[:, 0:640].unsqueeze(1).to_broadcast([128, 8, 640]),
                                                             op0=ALU.mult, op1=ALU.mult), [RM, R_mask], [RM])
                P.op("dve", lambda e: e.tensor_tensor(out=mb[:], in0=bt[:], in1=negm[:].unsqueeze(1).to_broadcast([128, 8, 640]), op=ALU.add), [RM], [RM])
            else:
                P.op("act", lambda e: e.activation(out=bt[:], in_=bt[:], func=AF.Exp), [RM], [RM])
                P.op("dve", lambda e: e.tensor_tensor(out=mb[:], in0=bt[:], in1=maskt[:, 0:640].unsqueeze(1).to_broadcast([128, 8, 640]), op=ALU.mult), [RM, R_mask], [RM])
            for hc in range(4):
                P.op("sp", lambda e: e.dma_start(out=scr(l, "me%d" % hc), in_=mb[:, 2 * hc:2 * hc + 2, :].rearrange("p a b -> p (a b)")),
                     [RM], [R_wscr[l][PIDX["me%d" % hc]]], dma_sem="tst")
            RD = RG("cwd")
            for c in range(2):
                dg = st("dg%d" % c, [128, 31, 128], BF16)
                for k in range(31):
                    P.op("dve", lambda e: e.tensor_scalar(out=dg[:, k, :], in0=ident[:], scalar1=PV[l][:, PV_CW + c * 31 + k:PV_CW + c * 31 + k + 1],
                                                                                scalar2=None, op0=ALU.mult), [R_pv[l], R_ident], [RD])
                P.op("sp", lambda e: e.dma_start(out=scr(l, "cw%d" % c), in_=dg[:].rearrange("p a b -> p (a b)")),
                     [RD], [R_wscr[l][PIDX["cw%d" % c]]], dma_sem="tst")
            P.barrier()

    xres = sb("xres", [128, KC, T], F32)
    hT = sb("hT", [128, KC, T], BF16)
    hid = sb("hid", [128, HC, T], BF16)
    sq = sb("sq", [128, KC, T], BF16)
    NTMP = 10
    tmpall = sb("tmpall", [128, NTMP, T], F32)
    du = sb("du", [128, 2, T], F32)
    cv = sb("cv", [128, 2, T], F32)
    NSLOT = 6
    slots = [sb("slot%d" % i, [128, SLOTW], BF16) for i in range(NSLOT)]
    kbuf = [sb("kbuf%d" % l, [128, 4, 2, T], BF16) for l in range(L)]
    vbuf = [sb("vbuf%d" % l, [128, 8, 512], BF16) for l in range(L)]
    hbuf = [sb("hbuf%d" % l, [128, 2, 32 + T], BF16) for l in range(L)]
    ps = [es.enter_context(nc.psum_tensor("ps%d" % i, [128, T], F32)) for i in range(8)]
    R_x = [Reg("x%d" % k) for k in range(KC)]
    R_h = [Reg("h%d" % k) for k in range(KC)]
    R_hid = [Reg("hid%d" % k) for k in range(HC)]
    R_sq = [Reg("sq%d" % k) for k in range(KC)]
    R_tmp = [Reg("tmp%d" % k) for k in range(NTMP)]
    R_du, R_cv = Reg("du"), [Reg("cv0"), Reg("cv1")]
    R_slot = [Reg("slot%d" % i) for i in range(NSLOT)]
    R_k = [[Reg("k%d_%d" % (l, h)) for h in range(2)] for l in range(L)]
    R_v = [[Reg("v%d_%d" % (l, h)) for h in range(2)] for l in range(L)]
    R_hb = [[Reg("hb%d_%d" % (l, c)) for c in range(2)] for l in range(L)]
    R_ps = [Reg("ps%d" % i, excl=True) for i in range(8)]
    R_xd, R_yd = Reg("xd"), Reg("yd")

    bank_free = list(range(8))

    def bank():
        return bank_free.pop(0)

    def unbank(b):
        bank_free.append(b)

    tmp_rr = [0]

    def tmp():
        i = tmp_rr[0] % NTMP
        tmp_rr[0] += 1
        return tmpall[:, i, :], R_tmp[i]

    order = []
    DRY = [True]
    ring = {"next_load": 0, "next_use": 0, "free": list(range(NSLOT)), "where": {}}

    def ring_fill():
        if DRY[0]:
            return
        while ring["free"] and ring["next_load"] < len(order):
            idx = ring["next_load"]
            l, n_ = order[idx]
            sl = ring["free"].pop(0)
            i = PIDX[n_]
            w_ = PIECES[i][1]
            if n_.startswith("tab"):
                j = int(n_[3:])
                src = tscr[l * 8 + j, :, :]
                dst = slots[sl][:, 0:2048].bitcast(F32)
            else:
                src = scr(l, n_)
                dst = slots[sl][:, 0:w_]
            P.op("sp", lambda e, dst=dst, src=src: e.dma_start(out=dst, in_=src), [R_wscr[l][i]], [R_slot[sl]], dma_sem="ring%d" % sl)
            ring["where"][idx] = sl
            ring["next_load"] += 1

    def ring_get(l, name):
        if DRY[0]:
            order.append((l, name))
            return slots[0], R_slot[0], 0
        idx = ring["next_use"]
        assert order[idx] == (l, name), (order[idx], l, name)
        assert idx in ring["where"], "ring deadlock at %s" % name
        ring["next_use"] += 1
        sl = ring["where"][idx]
        return slots[sl], R_slot[sl], sl

    def ring_rel(sl):
        if DRY[0]:
            return
        ring["free"].append(sl)
        ring_fill()

    def mm(out, lhsT, rhs, start, stop, reads, writes, **kw):
        P.op("pe", lambda e: e.matmul(out, lhsT=lhsT, rhs=rhs, start=start, stop=stop, **kw), reads, writes)

    def rmsnorm(l, col):
        for k in range(KC):
            if k % 2 == 1:
                P.op("act", lambda e: e.activation(out=sq[:, k, :], in_=xres[:, k, :], func=AF.Square), [R_x[k]], [R_sq[k]])
            else:
                P.op("pool", lambda e: e.tensor_tensor(out=sq[:, k, :], in0=xres[:, k, :], in1=xres[:, k, :], op=ALU.mult), [R_x[k]], [R_sq[k]])
        b = bank()
        for k in range(KC):
            mm(ps[b][:], ones_bf[:], sq[:, k, :], k == 0, k == KC - 1, [R_sq[k]], [R_ps[b]])
        rs, rr = tmp()
        P.op("act", lambda e: e.activation(out=rs, in_=ps[b][:], func=AF.Ln, bias=EPS, scale=1.0 / D), [R_ps[b]], [rr])
        unbank(b)
        P.op("act", lambda e: e.activation(out=rs, in_=rs, func=AF.Exp, scale=-0.5), [rr], [rr])
        for k in range(KC):
            P.op("dve", lambda e, k=k: e.scalar_tensor_tensor(out=hT[:, k, :], in0=xres[:, k, :], scalar=PV[l][:, col + k:col + k + 1],
                                                              in1=rs, op0=ALU.mult, op1=ALU.mult), [R_x[k], rr], [R_h[k]])

    def ffn(l, f):
        rmsnorm(l, PV_N1 if f == 0 else PV_N2)
        for j in range(11):
            sl, rs_, si = ring_get(l, "f%du%d" % (f + 1, j))
            if j == 0:
                bk = {(half, ab): bank() for half in range(2) for ab in range(2)}
                for k in range(KC):
                    for half in range(2):
                        for ab in range(2):
                            mm(ps[bk[(half, ab)]][:], sl[:, k * 512 + ab * 256 + half * 128: k * 512 + ab * 256 + half * 128 + 128], hT[:, k, :],
                               k == 0, k == KC - 1, [rs_, R_h[k]], [R_ps[bk[(half, ab)]]])
                for half in range(2):
                    m = half
                    bA, bB = bk[(half, 0)], bk[(half, 1)]
                    ta, ra = tmp()
                    P.op("act", lambda e: e.activation(out=ta, in_=ps[bA][:], func=AF.Silu), [R_ps[bA]], [ra])
                    unbank(bA)
                    P.op("dve", lambda e: e.tensor_tensor(out=hid[:, m, :], in0=ps[bB][:], in1=ta, op=ALU.mult), [R_ps[bB], ra], [R_hid[m]])
                    unbank(bB)
                ring_rel(si)
                continue
            for half in range(2):
                m = 2 * j + half
                bA, bB = bank(), bank()
                for k in range(KC):
                    mm(ps[bA][:], sl[:, k * 512 + half * 128: k * 512 + half * 128 + 128], hT[:, k, :], k == 0, k == KC - 1,
                       [rs_, R_h[k]], [R_ps[bA]])
                for k in range(KC):
                    mm(ps[bB][:], sl[:, k * 512 + 256 + half * 128: k * 512 + 256 + half * 128 + 128], hT[:, k, :], k == 0, k == KC - 1,
                       [rs_, R_h[k]], [R_ps[bB]])
                ta, ra = tmp()
                P.op("act", lambda e, bA=bA, ta=ta: e.activation(out=ta, in_=ps[bA][:], func=AF.Silu), [R_ps[bA]], [ra])
                unbank(bA)
                P.op("dve", lambda e, bB=bB, ta=ta, m=m: e.tensor_tensor(out=hid[:, m, :], in0=ps[bB][:], in1=ta, op=ALU.mult),
                     [R_ps[bB], ra], [R_hid[m]])
                unbank(bB)
            ring_rel(si)
        bd = [bank() for _ in range(KC)]
        for j in range(6):
            nk = 4 if j < 5 else 2
            sl, rs_, si = ring_get(l, "f%dd%d" % (f + 1, j))
            for m in range(KC):
                for kk in range(nk):
                    k = 4 * j + kk
                    mm(ps[bd[m]][:], sl[:, kk * 1024 + m * 128: kk * 1024 + m * 128 + 128], hid[:, k, :], k == 0, k == HC - 1,
                       [rs_, R_hid[k]], [R_ps[bd[m]]])
            ring_rel(si)
        for m in range(KC):
            P.op("dve", lambda e, m=m: e.scalar_tensor_tensor(out=xres[:, m, :], in0=ps[bd[m]][:], scalar=0.5, in1=xres[:, m, :],
                                                              op0=ALU.mult, op1=ALU.add), [R_ps[bd[m]], R_x[m]], [R_x[m]])
            unbank(bd[m])

    MERG, QN, UT, ATT, S5O, CVO = 0, 8, 12, 14, 18, 20
    branches = dbg.get("branches", (0, 1, 2))
    NS5T = 12
    s5tmp = sb("s5tmp", [128, NS5T, T], F32)
    R_s5t = [Reg("s5t%d" % k) for k in range(NS5T)]
    ebuf = sb("ebuf", [128, 4, T], BF16)
    R_e = [Reg("e%d" % k) for k in range(4)]

    def mixer(l, i):
        half, hhalf = i % 2, 1 - i % 2
        pvl = PV[l]
        rmsnorm(l, PV_NM)
        hb = hbuf[l]
        sl, rs_, si = ring_get(l, "inU")
        for c in range(2):
            b = bank()
            for k in range(KC):
                mm(ps[b][:], sl[:, k * 256 + c * 128: k * 256 + c * 128 + 128], hT[:, k, :], k == 0, k == KC - 1, [rs_, R_h[k]], [R_ps[b]])
            P.op("act", lambda e: e.activation(out=hid[:, UT + c, :], in_=ps[b][:], func=AF.Copy), [R_ps[b]], [R_hid[UT + c]])
            P.op("act", lambda e: e.activation(out=du[:, c, :], in_=ps[b][:], func=AF.Copy, scale=pvl[:, PV_D + c:PV_D + c + 1]), [R_ps[b]], [R_du])
            unbank(b)
        ring_rel(si)

        def attention_hc(hc, slM, rM):
            bnum, bden = bank(), bank()
            steps = []
            for hl in range(2):
                for kbi in range(8):
                    if 4 * i - 4 + kbi >= 0:
                        steps.append((hl, kbi))
            firsts = {0: True, 1: True}
            info = {}
            for n in range(len(steps) + 3):
                if n < len(steps):
                    hl, kbi = steps[n]
                    p0 = 64 * hl
                    gkb = 4 * i - 4 + kbi
                    khalf = half if kbi >= 4 else hhalf
                    kcol = (kbi % 4) * 128
                    q_lo, q_hi = max(0, kbi - 4), min(3, kbi)
                    nq = q_hi - q_lo + 1
                    rel_lo = 4 * i + q_lo - gkb
                    bs = bank()
                    me = slM[:, hl * 640 + rel_lo * 128: hl * 640 + (rel_lo + nq) * 128]
                    ei = n % 4
                    mm(ps[bs][:, 0:nq * 128], kbuf[l][p0:p0 + 64, hc, khalf, kcol:kcol + 128], hid[p0:p0 + 64, QN + hc, q_lo * 128:(q_hi + 1) * 128],
                       True, True, [R_k[l][khalf], R_hid[QN + hc]], [R_ps[bs]])
                    P.op("act", lambda e: e.activation(out=ebuf[:, ei, 0:nq * 128], in_=ps[bs][:, 0:nq * 128], func=AF.Exp, scale=0.125), [R_ps[bs]], [R_e[ei]])
                    P.op("pool", lambda e: e.tensor_tensor(out=ebuf[:, ei, 0:nq * 128], in0=ebuf[:, ei, 0:nq * 128], in1=me, op=ALU.mult), [R_e[ei], rM], [R_e[ei]])
                    unbank(bs)
                    info[n] = (hl, kbi, khalf, q_lo, q_hi, ei)
                if n >= 3:
                    hl, kbi, khalf, q_lo, q_hi, ei = info[n - 3]
                    p0 = 64 * hl
                    h = 2 * hc + hl
                    nq = q_hi - q_lo + 1
                    mm(ps[bnum][p0:p0 + 64, q_lo * 128:(q_hi + 1) * 128], vbuf[l][:, khalf * 4 + kbi % 4, h * 64:(h + 1) * 64], ebuf[:, ei, 0:nq * 128],
                       firsts[hl], False, [R_v[l][khalf], R_e[ei]], [R_ps[bnum]], skip_group_check=True)
                    mm(ps[bden][p0:p0 + 64, q_lo * 128:(q_hi + 1) * 128], ones_bf[:, 0:64], ebuf[:, ei, 0:nq * 128],
                       firsts[hl], False, [R_e[ei]], [R_ps[bden]], skip_group_check=True)
                    firsts[hl] = False
                yield
            td, rd = tmp()
            P.op("act", lambda e: e.activation(out=td, in_=ps[bden][:], func=AF.Ln), [R_ps[bden]], [rd])
            unbank(bden)
            P.op("act", lambda e: e.activation(out=td, in_=td, func=AF.Exp, scale=-1.0), [rd], [rd])
            P.op("dve", lambda e: e.tensor_tensor(out=hid[:, ATT + hc, :], in0=ps[bnum][:], in1=td, op=ALU.mult), [R_ps[bnum], rd], [R_hid[ATT + hc]])
            unbank(bnum)

        def side():
            sl, rs_, si = ring_get(l, "inA")
            if i == 0:
                for c in range(2):
                    P.op("pool", lambda e: e.memset(hb[:, c, 0:32], 0.0), [], [R_hb[l][c]])
            for c in range(2):
                bA, bG = bank(), bank()
                for k in range(KC):
                    mm(ps[bA][:], sl[:, k * 512 + c * 128: k * 512 + c * 128 + 128], hT[:, k, :], k == 0, k == KC - 1, [rs_, R_h[k]], [R_ps[bA]])
                for k in range(KC):
                    mm(ps[bG][:], sl[:, k * 512 + 256 + c * 128: k * 512 + 256 + c * 128 + 128], hT[:, k, :], k == 0, k == KC - 1, [rs_, R_h[k]], [R_ps[bG]])
                tg, rg = tmp()
                P.op("act", lambda e: e.activation(out=tg, in_=ps[bG][:], func=AF.Sigmoid), [R_ps[bG]], [rg])
                unbank(bG)
                P.op("dve", lambda e: e.tensor_tensor(out=hb[:, c, 32:32 + T], in0=ps[bA][:], in1=tg, op=ALU.mult), [R_ps[bA], rg], [R_hb[l][c]])
                unbank(bA)
                yield
            ring_rel(si)
            for c in range(2):
                sl, rs_, si = ring_get(l, "cw%d" % c)
                b = bank()
                for k in range(31):
                    mm(ps[b][:], sl[:, k * 128:(k + 1) * 128], hb[:, c, 2 + k:2 + k + T], k == 0, k == 30, [rs_, R_hb[l][c]], [R_ps[b]])
                ring_rel(si)
                P.op("act", lambda e: e.activation(out=cv[:, c, :], in_=ps[b][:], func=AF.Identity, bias=pvl[:, PV_CB + c:PV_CB + c + 1], scale=1.0), [R_ps[b]], [R_cv[c]])
                unbank(b)
                P.op("pool", lambda e: e.tensor_copy(out=hb[:, c, 0:32], in_=hb[:, c, T:T + 32]), [R_hb[l][c]], [R_hb[l][c]])
                P.op("act", lambda e: e.activation(out=sq[:, c, :], in_=cv[:, c, :], func=AF.Copy), [R_cv[c]], [R_sq[c]])
                P.op("act", lambda e: e.activation(out=sq[:, 2 + c, :], in_=cv[:, c, :], func=AF.Square), [R_cv[c]], [R_sq[2 + c]])
                yield
            b1, b2 = bank(), bank()
            for c in range(2):
                mm(ps[b1][:], ones_bf[:], sq[:, c, :], c == 0, c == 1, [R_sq[c]], [R_ps[b1]])
            for c in range(2):
                mm(ps[b2][:], ones_bf[:], sq[:, 2 + c, :], c == 0, c == 1, [R_sq[2 + c]], [R_ps[b2]])
            tm, rm = tmp()
            tv, rv = tmp()
            P.op("act", lambda e: e.activation(out=tm, in_=ps[b1][:], func=AF.Copy, scale=1.0 / 256), [R_ps[b1]], [rm])
            unbank(b1)
            P.op("pool", lambda e: e.tensor_tensor(out=tv, in0=tm, in1=tm, op=ALU.mult), [rm], [rv])
            P.op("dve", lambda e: e.scalar_tensor_tensor(out=tv, in0=ps[b2][:], scalar=1.0 / 256, in1=tv, op0=ALU.mult, op1=ALU.subtract), [R_ps[b2], rv], [rv])
            unbank(b2)
            P.op("act", lambda e: e.activation(out=tv, in_=tv, func=AF.Ln, bias=EPS, scale=1.0), [rv], [rv])
            P.op("act", lambda e: e.activation(out=tv, in_=tv, func=AF.Exp, scale=-0.5), [rv], [rv])
            for c in range(2):
                P.op("pool", lambda e: e.tensor_tensor(out=cv[:, c, :], in0=cv[:, c, :], in1=tm, op=ALU.subtract), [R_cv[c], rm], [R_cv[c]])
                P.op("pool", lambda e: e.tensor_tensor(out=cv[:, c, :], in0=cv[:, c, :], in1=tv, op=ALU.mult), [R_cv[c], rv], [R_cv[c]])
            yield

            def ln_silu():
                for c in range(2):
                    P.op("act", lambda e: e.activation(out=hid[:, CVO + c, :], in_=cv[:, c, :], func=AF.Silu, bias=pvl[:, PV_LB + c:PV_LB + c + 1],
                                                       scale=pvl[:, PV_LG + c:PV_LG + c + 1]), [R_cv[c]], [R_hid[CVO + c]])
            qk_st = {}

            def qk_PE(cq):
                isq, c = cq < 4, cq % 4
                if cq == 0:
                    qk_st["sl"] = ring_get(l, "inQ")
                if cq == 4:
                    ring_rel(qk_st["sl"][2])
                    qk_st["sl"] = ring_get(l, "inK")
                sl, rs_, si = qk_st["sl"]
                b = bank()
                for k in range(KC):
                    mm(ps[b][:], sl[:, k * 512 + c * 128: k * 512 + c * 128 + 128], hT[:, k, :], k == 0, k == KC - 1, [rs_, R_h[k]], [R_ps[b]])
                tq, rq = tmp()
                sqi = cq % 4
                P.op("act", lambda e: e.activation(out=sq[:, sqi, :], in_=ps[b][:], func=AF.Square), [R_ps[b]], [R_sq[sqi]])
                P.op("act", lambda e: e.activation(out=tq, in_=ps[b][:], func=AF.Copy), [R_ps[b]], [rq])
                unbank(b)
                qk_st[cq] = (tq, rq, sqi)
                if cq == 7:
                    ring_rel(si)

            def qk_BLF(cq):
                isq, c = cq < 4, cq % 4
                tq, rq, sqi = qk_st[cq]
                gcol = PV_QG if isq else PV_KG
                b2 = bank()
                mm(ps[b2][:], bones_bf[:], sq[:, sqi, :], True, True, [R_sq[sqi]], [R_ps[b2]])
                tr, rr = tmp()
                P.op("act", lambda e: e.activation(out=tr, in_=ps[b2][:], func=AF.Ln, bias=EPS, scale=1.0 / 64), [R_ps[b2]], [rr])
                unbank(b2)
                P.op("act", lambda e: e.activation(out=tr, in_=tr, func=AF.Exp, scale=-0.5), [rr], [rr])
                dst = hid[:, QN + c, :] if isq else kbuf[l][:, c, half, :]
                dreg = [R_hid[QN + c]] if isq else [R_k[l][half]]
                P.op("dve", lambda e: e.scalar_tensor_tensor(out=dst, in0=tq, scalar=pvl[:, gcol:gcol + 1], in1=tr,
                                                             op0=ALU.mult, op1=ALU.mult), [rq, rr], dreg)
            for cq in range(9):
                if cq < 8:
                    qk_PE(cq)
                if cq == 2:
                    ln_silu()
                if cq >= 1:
                    qk_BLF(cq - 1)
                yield
            sl, rs_, si = ring_get(l, "inV")
            for tb in range(4):
                b = bank()
                for k in range(KC):
                    mm(ps[b][:], hT[:, k, tb * 128:(tb + 1) * 128], sl[:, k * 512:(k + 1) * 512], k == 0, k == KC - 1, [rs_, R_h[k]], [R_ps[b]])
                P.op("act", lambda e: e.activation(out=vbuf[l][:, half * 4 + tb, :], in_=ps[b][:], func=AF.Copy), [R_ps[b]], [R_v[l][half]])
                unbank(b)
                yield
            ring_rel(si)
            for hc in range(4):
                slM, rM, siM = ring_get(l, "me%d" % hc)
                for _ in attention_hc(hc, slM, rM):
                    yield
                ring_rel(siM)

        slB, rB, siB = ring_get(l, "bbar")
        slC, rC, siC = ring_get(l, "cmat")
        if i == 0:
            P.op("pool", lambda e: e.memset(s5c[l][:, 3:5, :], 0.0), [], R_init[l])
        pend = None
        ysb = None
        sgen = side()
        side_total = 18 + 4 * (3 + (8 if i == 0 else 16))
        side_done = 0
        for it in range(9):
            if pend is not None:
                j, (t1, r1), (t2, r2), (t3, r3), (t4, r4), (t5, r5), (t6, r6), cosT, sinT, rT, siT = pend
                xr_i, xi_i = 4 + (j % 2) * 2, 5 + (j % 2) * 2
                P.op("dve", lambda e: e.tensor_tensor(out=t1, in0=t2, in1=cosT, op=ALU.mult), [r2, rT], [r1])
                P.op("dve", lambda e: e.tensor_tensor(out=t3, in0=t4, in1=sinT, op=ALU.mult), [r4, rT], [r3])
                P.op("dve", lambda e: e.tensor_tensor(out=sq[:, xr_i, :], in0=t1, in1=t3, op=ALU.subtract), [r1, r3], [R_sq[xr_i]])
                P.op("dve", lambda e: e.tensor_tensor(out=t5, in0=t4, in1=cosT, op=ALU.mult), [r4, rT], [r5])
                P.op("dve", lambda e: e.tensor_tensor(out=t6, in0=t2, in1=sinT, op=ALU.mult), [r2, rT], [r6])
                P.op("dve", lambda e: e.tensor_tensor(out=sq[:, xi_i, :], in0=t5, in1=t6, op=ALU.add), [r5, r6], [R_sq[xi_i]])
                ring_rel(siT)
            if it < 8:
                j = it
                cc = j // 4
                slT, rT, siT = ring_get(l, "tab%d" % j)
                tabf = slT[:, 0:2048].bitcast(F32)
                cosT, sinT = tabf[:, 0:T], tabf[:, T:2 * T]
                br_, bi_ = bank(), bank()
                mm(ps[br_][:], slB[:, cc * 512 + (j % 4) * 128: cc * 512 + (j % 4) * 128 + 128], hid[:, UT + cc, :], True, True, [rB, R_hid[UT + cc]], [R_ps[br_]])
                mm(ps[bi_][:], slB[:, 1024 + cc * 512 + (j % 4) * 128: 1024 + cc * 512 + (j % 4) * 128 + 128], hid[:, UT + cc, :], True, True, [rB, R_hid[UT + cc]], [R_ps[bi_]])
                tt = [(s5tmp[:, (6 * j + q) % NS5T, :], R_s5t[(6 * j + q) % NS5T]) for q in range(6)]
                (t1, r1), (t2, r2), (t3, r3), (t4, r4), (t5, r5), (t6, r6) = tt
                P.op("dve", lambda e: e.tensor_tensor(out=t1, in0=ps[br_][:], in1=cosT, op=ALU.mult), [R_ps[br_], rT], [r1])
                P.op("dve", lambda e: e.tensor_tensor(out=t4, in0=ps[br_][:], in1=sinT, op=ALU.mult), [R_ps[br_], rT], [r4])
                unbank(br_)
                P.op("dve", lambda e: e.tensor_tensor(out=t2, in0=ps[bi_][:], in1=sinT, op=ALU.mult), [R_ps[bi_], rT], [r2])
                P.op("dve", lambda e: e.tensor_tensor(out=t3, in0=ps[bi_][:], in1=cosT, op=ALU.mult), [R_ps[bi_], rT], [r3])
                unbank(bi_)
                P.op("dve", lambda e: e.tensor_tensor(out=t1, in0=t1, in1=t2, op=ALU.add), [r1, r2], [r1])
                P.op("dve", lambda e: e.tensor_tensor(out=t3, in0=t3, in1=t4, op=ALU.subtract), [r3, r4], [r3])
                rdec = s5c[l][:, 0, j:j + 1].to_broadcast([128, T])
                P.op("dve", lambda e: e.tensor_tensor_scan(out=t2, data0=rdec, data1=t1, initial=s5c[l][:, 3, j:j + 1],
                                                           op0=ALU.mult, op1=ALU.add), [r1, R_init[l][j], R_s5c[l]], [r2])
                P.op("dve", lambda e: e.tensor_tensor_scan(out=t4, data0=rdec, data1=t3, initial=s5c[l][:, 4, j:j + 1],
                                                           op0=ALU.mult, op1=ALU.add), [r3, R_init[l][j], R_s5c[l]], [r4])
                P.op("dve", lambda e: e.tensor_scalar(out=t1[:, 0:1], in0=t4[:, T - 1:T], scalar1=s5c[l][:, 2, j:j + 1], scalar2=None, op0=ALU.mult),
                     [r4, R_s5c[l]], [r1])
                P.op("dve", lambda e: e.tensor_scalar(out=t1[:, 1:2], in0=t2[:, T - 1:T], scalar1=s5c[l][:, 2, j:j + 1], scalar2=None, op0=ALU.mult),
                     [r2, R_s5c[l]], [r1])
                P.op("dve", lambda e: e.scalar_tensor_tensor(out=s5c[l][:, 3, j:j + 1], in0=t2[:, T - 1:T], scalar=s5c[l][:, 1, j:j + 1], in1=t1[:, 0:1],
                                                             op0=ALU.mult, op1=ALU.subtract), [r2, r1, R_s5c[l]], [R_init[l][j]])
                P.op("dve", lambda e: e.scalar_tensor_tensor(out=s5c[l][:, 4, j:j + 1], in0=t4[:, T - 1:T], scalar=s5c[l][:, 1, j:j + 1], in1=t1[:, 1:2],
                                                             op0=ALU.mult, op1=ALU.add), [r4, r1, R_s5c[l]], [R_init[l][j]])
                newpend = (j, (t1, r1), (t2, r2), (t3, r3), (t4, r4), (t5, r5), (t6, r6), cosT, sinT, rT, siT)
            else:
                newpend = None
            if sgen is not None:
                nsteps = -(-(side_total - side_done) // (9 - it))
                for _ in range(nsteps):
                    try:
                        next(sgen)
                        side_done += 1
                    except StopIteration:
                        sgen = None
                        break
            if pend is not None:
                j = pend[0]
                cc = j // 4
                xr_i, xi_i = 4 + (j % 2) * 2, 5 + (j % 2) * 2
                if j % 4 == 0:
                    ysb = bank()
                mm(ps[ysb][:], slC[:, j * 128:(j + 1) * 128], sq[:, xr_i, :], j % 4 == 0, False, [rC, R_sq[xr_i]], [R_ps[ysb]])
                mm(ps[ysb][:], slC[:, 1024 + j * 128:1024 + (j + 1) * 128], sq[:, xi_i, :], False, j % 4 == 3, [rC, R_sq[xi_i]], [R_ps[ysb]])
                if j % 4 == 3:
                    (ty, ry), (tz, rz) = tmp(), tmp()
                    P.op("dve", lambda e: e.tensor_tensor(out=ty, in0=ps[ysb][:], in1=du[:, cc, :], op=ALU.add), [R_ps[ysb], R_du], [ry])
                    unbank(ysb)
                    P.op("act", lambda e: e.activation(out=tz, in_=ty, func=AF.Square), [ry], [rz])
                    P.op("act", lambda e: e.activation(out=tz, in_=tz, func=AF.Identity, bias=1.0, scale=0.044715), [rz], [rz])
                    P.op("pool", lambda e: e.tensor_tensor(out=tz, in0=tz, in1=ty, op=ALU.mult), [rz, ry], [rz])
                    P.op("act", lambda e: e.activation(out=tz, in_=tz, func=AF.Sigmoid, scale=1.5957691216057308), [rz], [rz])
                    P.op("pool", lambda e: e.tensor_tensor(out=hid[:, MERG + cc, :], in0=ty, in1=tz, op=ALU.mult), [ry, rz], [R_hid[MERG + cc]])
            pend = newpend
        ring_rel(siB)
        ring_rel(siC)
        if sgen is not None:
            for _ in sgen:
                pass
        sl, rs_, si = ring_get(l, "glu")
        for c in range(2):
            bA, bG = bank(), bank()
            for k in range(2):
                mm(ps[bA][:], sl[:, k * 512 + c * 128: k * 512 + c * 128 + 128], hid[:, MERG + k, :], k == 0, k == 1, [rs_, R_hid[MERG + k]], [R_ps[bA]])
            for k in range(2):
                mm(ps[bG][:], sl[:, k * 512 + 256 + c * 128: k * 512 + 256 + c * 128 + 128], hid[:, MERG + k, :], k == 0, k == 1, [rs_, R_hid[MERG + k]], [R_ps[bG]])
            tg, rg = tmp()
            P.op("act", lambda e: e.activation(out=tg, in_=ps[bG][:], func=AF.Sigmoid), [R_ps[bG]], [rg])
            unbank(bG)
            P.op("dve", lambda e: e.tensor_tensor(out=hid[:, S5O + c, :], in0=ps[bA][:], in1=tg, op=ALU.mult), [R_ps[bA], rg], [R_hid[S5O + c]])
            unbank(bA)
        ring_rel(si)

        slS, rS, siS = ring_get(l, "brs")
        slA, rA, siA = ring_get(l, "bra")
        for m in range(8):
            slG, rG, siG = ring_get(l, "g%d" % m)
            acc = None
            for b in range(3):
                bg, by = bank(), bank()
                for k in range(KC):
                    mm(ps[bg][:], slG[:, k * 384 + b * 128: k * 384 + b * 128 + 128], hT[:, k, :], k == 0, k == KC - 1, [rG, R_h[k]], [R_ps[bg]])
                if b == 0:
                    for k in range(2):
                        mm(ps[by][:], slS[:, k * 1024 + m * 128: k * 1024 + m * 128 + 128], hid[:, S5O + k, :], k == 0, k == 1, [rS, R_hid[S5O + k]], [R_ps[by]])
                elif b == 1:
                    for k in range(4):
                        mm(ps[by][:], slA[:, k * 1024 + m * 128: k * 1024 + m * 128 + 128], hid[:, ATT + k, :], k == 0, k == 3, [rA, R_hid[ATT + k]], [R_ps[by]])
                else:
                    for k in range(2):
                        mm(ps[by][:], slS[:, (2 + k) * 1024 + m * 128: (2 + k) * 1024 + m * 128 + 128], hid[:, CVO + k, :], k == 0, k == 1, [rS, R_hid[CVO + k]], [R_ps[by]])
                tg, rg = tmp()
                P.op("act", lambda e: e.activation(out=tg, in_=ps[bg][:], func=AF.Sigmoid,
                                                   bias=pvl[:, PV_BG + b * 8 + m:PV_BG + b * 8 + m + 1], scale=1.0), [R_ps[bg]], [rg])
                unbank(bg)
                last = (b == max(branches)) and acc is not None
                if b not in branches:
                    unbank(by)
                    continue
                if acc is None:
                    P.op("dve", lambda e: e.tensor_tensor(out=tg, in0=ps[by][:], in1=tg, op=ALU.mult), [R_ps[by], rg], [rg])
                    acc = (tg, rg)
                else:
                    P.op("dve", lambda e: e.tensor_tensor(out=tg, in0=ps[by][:], in1=tg, op=ALU.mult), [R_ps[by], rg], [rg])
                    if last:
                        P.op("pool", lambda e: e.tensor_tensor(out=hid[:, MERG + m, :], in0=acc[0], in1=tg, op=ALU.add), [acc[1], rg], [R_hid[MERG + m]])
                    else:
                        P.op("pool", lambda e: e.tensor_tensor(out=acc[0], in0=acc[0], in1=tg, op=ALU.add), [acc[1], rg], [acc[1]])
                unbank(by)
            if len(branches) == 1:
                P.op("pool", lambda e: e.tensor_copy(out=hid[:, MERG + m, :], in_=acc[0]), [acc[1]], [R_hid[MERG + m]])
            ring_rel(siG)
        ring_rel(siS)
        ring_rel(siA)
        for jj in range(2):
            sl, rs_, si = ring_get(l, "wo%d" % jj)
            for mm_ in range(4):
                m = 4 * jj + mm_
                b = bank()
                for k in range(KC):
                    mm(ps[b][:], sl[:, k * 512 + mm_ * 128: k * 512 + mm_ * 128 + 128], hid[:, MERG + k, :], k == 0, k == KC - 1, [rs_, R_hid[MERG + k]], [R_ps[b]])
                P.op("dve", lambda e: e.tensor_tensor(out=xres[:, m, :], in0=ps[b][:], in1=xres[:, m, :], op=ALU.add), [R_ps[b], R_x[m]], [R_x[m]])
                unbank(b)
            ring_rel(si)

    def run_tiles():
        xtok = tmpall[:, 0:8, :].rearrange("p (a b) c -> p a (b c)", b=2)
        stages = dbg.get("stages", ("f1", "mix", "f2"))
        for s_ in range(NSEQ):
            for i in range(NT):
                r0 = s_ * S + i * T
                P.op("sp", lambda e, r0=r0: e.dma_start(out=xtok, in_=x_d[r0:r0 + T, :].rearrange("(a p) c -> p a c", p=128)), [R_xd], R_tmp[0:8], dma_sem="xin")
                for k in range(KC):
                    b = bank()
                    for tb in range(4):
                        P.op("pe", lambda e, b=b, tb=tb, k=k: e.transpose(out=ps[b][:, tb * 128:(tb + 1) * 128], in_=xtok[:, tb, k * 128:(k + 1) * 128], identity=ident[:]),
                             R_tmp[0:8] + [R_ident], [R_ps[b]])
                    P.op("act", lambda e, b=b, k=k: e.activation(out=xres[:, k, :], in_=ps[b][:], func=AF.Copy), [R_ps[b]], [R_x[k]])
                    unbank(b)
                for l in range(nlayers):
                    if "f1" in stages:
                        ffn(l, 0)
                    else:
                        for n_, w_ in PIECES[0:17]:
                            ring_rel(ring_get(l, n_)[2])
                    if "mix" in stages:
                        mixer(l, i)
                    else:
                        for n_, w_ in PIECES[17:NPL - 17]:
                            ring_rel(ring_get(l, n_)[2])
                    if "f2" in stages:
                        ffn(l, 1)
                    else:
                        for n_, w_ in PIECES[NPL - 17:]:
                            ring_rel(ring_get(l, n_)[2])
                for tb in range(4):
                    for g in range(2):
                        b = bank()
                        for kk in range(4):
                            k = 4 * g + kk
                            P.op("pe", lambda e, b=b, tb=tb, k=k, kk=kk: e.transpose(out=ps[b][:, kk * 128:(kk + 1) * 128], in_=xres[:, k, tb * 128:(tb + 1) * 128], identity=ident[:]),
                                 [R_x[k], R_ident], [R_ps[b]])
                        P.op("act", lambda e, b=b, tb=tb, g=g: e.activation(out=xtok[:, tb, g * 512:(g + 1) * 512], in_=ps[b][:], func=AF.Copy), [R_ps[b]], R_tmp[0:8])
                        unbank(b)
                P.op("sp", lambda e, r0=r0: e.dma_start(out=y_d[r0:r0 + T, :].rearrange("(a p) c -> p a c", p=128), in_=xtok), R_tmp[0:8], [R_yd], dma_sem="yout")

    P_real = P
    P = Plan()
    run_tiles()
    for r_ in P.allregs.values():
        r_.w = None
        r_.readers = {}
    P = P_real
    DRY[0] = False
    bank_free[:] = list(range(8))
    tmp_rr[0] = 0
    for grp, members in cast_groups.items():
        fin = cast_fin[grp]
        for (l_, i_) in members:
            R_wscr[l_][i_].w = (grp, fin, "dma:" + grp)
    ring_fill()
    run_tiles()
    P.final_wait("sp", [R_yd])
    for e_ in COMPUTE:
        P.final_wait(e_, [R_yd])
    P.emit(nc)
    es.close()
    return nc


def host_layouts(inp):
    f = lambda a: np.asarray(a, dtype=np.float32)
    pv = np.zeros((L, 128, NPV), np.float32)
    lamR = np.zeros((L, 128, 3, 1024), np.float32)
    braw = np.zeros((L, 128, 2, 2, 512), np.float32)
    craw = np.zeros((L, 128, 2, 8, 128), np.float32)
    btoe = np.zeros((L, 128, 8, 5, 128), np.float32)
    kk = np.arange(128)[:, None, None]
    rel = np.arange(5)[None, :, None]
    qq = np.arange(128)[None, None, :]
    dist = 128 * rel + qq - kk
    idx = np.clip(dist, -128, 128) + 128
    dch = 2 * rel + (qq >= 64) - (kk >= 64)
    mask = ((dch >= 0) & (dch <= 8)).astype(np.float32).reshape(128, 640)
    mask01 = np.concatenate([mask, mask], axis=1)
    for l in range(L):
        fm = lambda v, n: f(v).reshape(n, 128).T
        pv[l, :, PV_N1:PV_N1 + 8] = fm(inp["ffn1_norm"][l], 8)
        pv[l, :, PV_NM:PV_NM + 8] = fm(inp["mix_norm"][l], 8)
        pv[l, :, PV_N2:PV_N2 + 8] = fm(inp["ffn2_norm"][l], 8)
        pv[l, :, PV_BG:PV_BG + 24] = fm(inp["b_gate"][l], 24)
        pv[l, :, PV_D:PV_D + 2] = fm(inp["s5_d"][l], 2)
        pv[l, :, PV_CB:PV_CB + 2] = fm(inp["conv_b_dw"][l], 2)
        pv[l, :, PV_LG:PV_LG + 2] = fm(inp["conv_ln_g"][l], 2)
        pv[l, :, PV_LB:PV_LB + 2] = fm(inp["conv_ln_b"][l], 2)
        cw = f(inp["conv_w_dw"][l])
        for c in range(2):
            pv[l, :, PV_CW + c * 31:PV_CW + c * 31 + 31] = cw[:, c * 128:(c + 1) * 128].T
        pv[l, :, PV_QG] = np.tile(f(inp["attn_q_gain"][l]), 2)
        pv[l, :, PV_KG] = np.tile(f(inp["attn_k_gain"][l]), 2)
        lre, lim, ldt = f(inp["s5_lambda_re"][l]), f(inp["s5_lambda_im"][l]), f(inp["s5_log_dt"][l])
        pv[l, :, PV_LR:PV_LR + 8] = lre.reshape(8, 128).T
        pv[l, :, PV_LI:PV_LI + 8] = lim.reshape(8, 128).T
        pv[l, :, PV_DT:PV_DT + 8] = np.repeat(ldt, 64).reshape(8, 128).T
        lamR[l, :, 0, :] = lre.reshape(1, 1024)
        lamR[l, :, 1, :] = lim.reshape(1, 1024)
        lamR[l, :, 2, :] = np.repeat(ldt, 64).reshape(1, 1024)
        bre, bim = f(inp["s5_b_re"][l]), f(inp["s5_b_im"][l])
        cre, cim = f(inp["s5_c_re"][l]), f(inp["s5_c_im"][l])
        for g in range(16):
            cc, gl = g // 8, g % 8
            braw[l, 16 * gl:16 * gl + 16, 0, cc, 64 * gl:64 * gl + 64] = bre[g].T
            braw[l, 16 * gl:16 * gl + 16, 1, cc, 64 * gl:64 * gl + 64] = bim[g].T
            j, g2 = g // 2, g % 2
            craw[l, 64 * g2:64 * g2 + 64, 0, j, 16 * gl:16 * gl + 16] = cre[g].T
            craw[l, 64 * g2:64 * g2 + 64, 1, j, 16 * gl:16 * gl + 16] = cim[g].T
        rb = f(inp["attn_rel_bias"][l])
        btoe[l] = np.transpose(rb[:, idx], (1, 0, 2, 3))
    return {"pv": pv, "lamR": lamR, "braw": braw.reshape(L, 128, 2048), "craw": craw.reshape(L, 128, 2048),
            "btoe": btoe.reshape(L, 128, 5120), "mask01": mask01, "ident": np.eye(128, dtype=np.float32)}


WKEYS = ["ffn1_w_up", "ffn2_w_up", "ffn1_w_down", "ffn2_w_down", "w_in", "s5_w_glu", "w_br_s5", "w_br_attn", "w_br_conv", "w_out"]


def run(inputs, n_cores, dbg=None):
    x = np.asarray(inputs["x"], dtype=np.float32)
    B, S, _ = x.shape
    nseq = B // n_cores
    nc = build_program(nseq, S, dbg)
    common = host_layouts(inputs)
    for k in WKEYS:
        common[k] = np.ascontiguousarray(np.asarray(inputs[k], dtype=np.float32))
    in_maps = []
    for c in range(n_cores):
        m = dict(common)
        m["x"] = np.ascontiguousarray(x[c * nseq:(c + 1) * nseq].reshape(nseq * S, D))
        in_maps.append(m)
    res = run_bass_kernel_spmd(nc, in_maps, core_ids=list(range(n_cores)))
    out = np.stack([np.asarray(r["y"]).reshape(nseq, S, D) for r in res.results], axis=0)
    return out.reshape(B, S, D).astype(np.float32)


def kernel(**inputs):
    return run(inputs, 8)
```

```python
import math
import numpy as np
from contextlib import ExitStack
import concourse.bass as bass
import concourse.mybir as mybir
from concourse.bass_utils import run_bass_kernel_spmd

F32 = mybir.dt.float32
BF16 = mybir.dt.bfloat16
AF = mybir.ActivationFunctionType
ALU = mybir.AluOpType

COMPUTE = ("pe", "act", "dve", "pool")
ALL_ENG = COMPUTE + ("sp",)


class Reg:
    __slots__ = ("name", "w", "readers", "excl")

    def __init__(self, name, excl=False):
        self.name = name
        self.w = None
        self.readers = {}
        self.excl = excl


class Op:
    __slots__ = ("eng", "fn", "waits", "tok", "needed", "dma")

    def __init__(self, eng, fn, waits, tok, dma):
        self.eng = eng
        self.fn = fn
        self.waits = waits
        self.tok = tok
        self.needed = False
        self.dma = dma


class _Rec:
    def __init__(self):
        self.calls = []

    def __getattr__(self, name):
        def f(*a, **k):
            self.calls.append((name, a, k))
        return f


class Plan:
    def __init__(self):
        self.ops = {e: [] for e in ALL_ENG}
        self.cnt = {e: 0 for e in COMPUTE}
        self.seen = {e: {} for e in ALL_ENG}
        self.dma_cnt = {}
        self.tokmap = {}
        self.allregs = {}
        self.ext = {}

    def _need(self, eng, waits, tok, same_ok):
        if tok is None:
            return
        key, val, teng = tok
        if teng == eng and same_ok and eng == "pe":
            return
        if self.seen[eng].get(key, 0) >= val:
            return
        if waits.get(key, 0) < val:
            waits[key] = val

    def op(self, eng, fn, reads=(), writes=(), dma_sem=None):
        waits = {}
        for r in reads:
            self.allregs[id(r)] = r
        for r in writes:
            self.allregs[id(r)] = r
        for r in reads:
            if r.excl:
                self._need(eng, waits, r.w, True)
                for k, (v, e) in r.readers.items():
                    self._need(eng, waits, (k, v, e), True)
            else:
                self._need(eng, waits, r.w, False)
        for w in writes:
            self._need(eng, waits, w.w, True)
            for k, (v, e) in w.readers.items():
                self._need(eng, waits, (k, v, e), True)
        if dma_sem is not None:
            val = self.dma_cnt.get(dma_sem, 0) + 16
            self.dma_cnt[dma_sem] = val
            tok = (dma_sem, val, "dma:" + dma_sem)
        else:
            self.cnt[eng] += 1
            tok = (eng, self.cnt[eng], eng)
        rec = _Rec()
        fn(rec)
        assert len(rec.calls) == 1
        o = Op(eng, rec.calls[0], waits, tok, dma_sem is not None)
        self.tokmap[(tok[0], tok[1])] = o
        for k, v in waits.items():
            self.seen[eng][k] = v
            t = self.tokmap.get((k, v))
            if t is not None:
                t.needed = True
        self.ops[eng].append(o)
        for r in reads:
            old = r.readers.get(tok[0])
            if old is None or old[0] < tok[1]:
                r.readers[tok[0]] = (tok[1], tok[2])
        for w in writes:
            w.w = tok
            w.readers = {}
        return o

    def final_wait(self, eng, regs):
        waits = {}
        for r in regs:
            self._need(eng, waits, r.w, True)
            for k, (v, e) in r.readers.items():
                self._need(eng, waits, (k, v, e), True)
        o = Op(eng, None, waits, None, False)
        for k, v in waits.items():
            self.seen[eng][k] = max(self.seen[eng].get(k, 0), v)
            t = self.tokmap.get((k, v))
            if t is not None:
                t.needed = True
        self.ops[eng].append(o)

    def barrier(self):
        regs = list(self.allregs.values())
        for e_ in ALL_ENG:
            self.final_wait(e_, regs)

    def emit(self, nc, M=3000):
        remap = {}
        nep = {}
        for e in COMPUTE:
            c = 0
            m = {}
            for o in self.ops[e]:
                if o.tok is not None and not o.dma and o.needed:
                    m[o.tok[1]] = (c // M, c % M + 1)
                    c += 1
            remap[e] = m
            nep[e] = (c + M - 1) // M
        with ExitStack() as es:
            sems = {}
            for e in COMPUTE:
                for k in range(max(1, nep[e])):
                    sems[(e, k)] = es.enter_context(nc.semaphore("s_%s%d" % (e, k)))
            for k, tot in self.dma_cnt.items():
                if k in self.ext:
                    sems[(k, 0)] = self.ext[k]
                    continue
                n = (tot // 16 + M - 1) // M
                for j in range(max(1, n)):
                    sems[(k, j)] = es.enter_context(nc.semaphore("d_%s_%d" % (k, j)))
            self.nsems = len(sems)
            block = es.enter_context(nc.Block())
            plan = self

            def sv(k, v):
                if k in remap:
                    ep, val = remap[k][v]
                    return sems[(k, ep)], val
                n = v // 16 - 1
                return sems[(k, n // M)], (n % M + 1) * 16

            def run(engname):
                def body(eng):
                    for o in plan.ops[engname]:
                        for k, v in o.waits.items():
                            s_, v_ = sv(k, v)
                            eng.wait_ge(s_, v_)
                        if o.fn is None:
                            continue
                        name_, a_, k_ = o.fn
                        ins = getattr(eng, name_)(*a_, **k_)
                        if o.dma:
                            s_, _ = sv(o.tok[0], o.tok[1])
                            ins.then_inc(s_, 16)
                        elif o.needed:
                            s_, _ = sv(o.tok[0], o.tok[1])
                            ins.then_inc(s_, 1)
                return body

            block.tensor(run("pe"))
            block.scalar(run("act"))
            block.vector(run("dve"))
            block.gpsimd(run("pool"))
            block.sync(run("sp"))


D = 1024
KC = 8
DFF = 2816
HC = 22
T = 512
L = 2
EPS = 1e-6
IN_COLS = 5376
NPV = 144
BIAS_PE = False
SLOTW = 4096
PV_N1, PV_NM, PV_N2, PV_BG, PV_D, PV_CB, PV_LG, PV_LB, PV_CW, PV_QG, PV_KG, PV_LR, PV_LI, PV_DT = (
    0, 8, 16, 24, 48, 50, 52, 54, 56, 118, 119, 120, 128, 136)


def layer_pieces():
    p = []
    for f in (1, 2):
        pass
    ffn = lambda f: [("f%du%d" % (f, j), 4096) for j in range(11)] + \
        [("f%dd%d" % (f, j), 4096 if j < 5 else 2048) for j in range(6)]
    mix = [("inA", 4096), ("cw0", 3968), ("cw1", 3968), ("inQ", 4096), ("inK", 4096), ("inV", 4096), ("inU", 2048),
           ("bbar", 2048), ("cmat", 2048)]
    for hc in range(4):
        mix += [("me%d" % hc, 1280), ("tab%d" % (2 * hc), 2048), ("tab%d" % (2 * hc + 1), 2048)]
    mix += [("glu", 1024), ("brs", 4096), ("bra", 4096)] + [("g%d" % j, 3072) for j in range(8)] + \
           [("wo0", 4096), ("wo1", 4096)]
    return ffn(1) + mix + ffn(2)


PIECES = layer_pieces()
PIDX = {n: i for i, (n, w) in enumerate(PIECES)}
NPL = len(PIECES)


def build_program(NSEQ, S, dbg=None):
    dbg = dbg or {}
    NT = S // T
    nlayers = dbg.get("layers", L)
    nc = bass.Bass("TRN2", target_bir_lowering=False)
    dram_in = lambda n, s, d=F32: nc.dram_tensor(n, s, d, kind="ExternalInput").ap()
    x_d = dram_in("x", [NSEQ * S, D])
    y_d = nc.dram_tensor("y", [NSEQ * S, D], F32, kind="ExternalOutput").ap()
    w_up = [dram_in("ffn1_w_up", [L, D, 2 * DFF]), dram_in("ffn2_w_up", [L, D, 2 * DFF])]
    w_dn = [dram_in("ffn1_w_down", [L, DFF, D]), dram_in("ffn2_w_down", [L, DFF, D])]
    w_in = dram_in("w_in", [L, D, IN_COLS])
    w_glu = dram_in("s5_w_glu", [L, 256, 512])
    w_brs = dram_in("w_br_s5", [L, 256, D])
    w_bra = dram_in("w_br_attn", [L, 512, D])
    w_brc = dram_in("w_br_conv", [L, 256, D])
    w_out = dram_in("w_out", [L, D, D])
    pv_d = dram_in("pv", [L, 128, NPV])
    lamR_d = dram_in("lamR", [L, 128, 3, 1024])
    braw_d = dram_in("braw", [L, 128, 2048])
    craw_d = dram_in("craw", [L, 128, 2048])
    btoe_d = dram_in("btoe", [L, 128, 5120])
    mask_d = dram_in("mask01", [128, 1280])
    ident_d = dram_in("ident", [128, 128])
    wscr = nc.dram_tensor("wscr", [L * NPL, 128, SLOTW], BF16, kind="Internal").ap()
    tscr = nc.dram_tensor("tscr", [L * 8, 128, 1024], F32, kind="Internal").ap()

    es = ExitStack()
    sb = lambda n, s, d: es.enter_context(nc.sbuf_tensor("sb_" + n, s, d))
    ident = sb("ident", [128, 128], F32)
    ones_bf = sb("ones_bf", [128, 128], BF16)
    bones_bf = sb("bones_bf", [128, 128], BF16)
    identb = sb("identb", [128, 128], BF16)
    PV = [sb("pv%d" % l, [128, NPV], F32) for l in range(L)]
    s5c = [sb("s5c%d" % l, [128, 5, 8], F32) for l in range(L)]
    R_ident, R_const = Reg("ident"), Reg("const")
    R_pv = [Reg("pv%d" % l) for l in range(L)]
    R_s5c = [Reg("s5c%d" % l) for l in range(L)]
    R_init = [[Reg("init%d_%d" % (l, j)) for j in range(8)] for l in range(L)]
    R_wscr = [[Reg("wscr%d_%d" % (l, i)) for i in range(NPL)] for l in range(L)]

    def scr(l, name, width=None):
        i = PIDX[name]
        w = width or PIECES[i][1]
        return wscr[l * NPL + i, :, 0:w]

    def issue_casts(P):
        cast_groups = {}

        def cast(l, name, dst_sl, src, grp):
            i = PIDX[name]
            P.op("pool", lambda e: e.dma_start(out=dst_sl, in_=src), [], [], dma_sem=grp)
            cast_groups.setdefault(grp, set()).add((l, i))

        def pc(l, name, nk, width):
            return scr(l, name, nk * width).rearrange("p (k c) -> p k c", k=nk)

        for l in range(nlayers):
            for f in range(2):
                g = "w%d_%d" % (l, f * 2)
                for j in range(11):
                    d = pc(l, "f%du%d" % (f + 1, j), 8, 512)
                    for half in range(2):
                        src = w_up[f][l, :, half * DFF + 256 * j: half * DFF + 256 * j + 256].rearrange("(k p) c -> p k c", p=128)
                        cast(l, "f%du%d" % (f + 1, j), d[:, :, half * 256:(half + 1) * 256], src, g)
                for j in range(6):
                    nk = 4 if j < 5 else 2
                    d = pc(l, "f%dd%d" % (f + 1, j), nk, 1024)
                    src = w_dn[f][l, 512 * j: 512 * j + 128 * nk, :].rearrange("(k p) c -> p k c", p=128)
                    cast(l, "f%dd%d" % (f + 1, j), d, src, g)
                if f == 0:
                    g = "w%d_1" % l
                    wi = lambda c0, n_: w_in[l, :, c0:c0 + n_].rearrange("(k p) c -> p k c", p=128)
                    cast(l, "inA", pc(l, "inA", 8, 512), wi(1792, 512), g)
                    cast(l, "inQ", pc(l, "inQ", 8, 512), wi(256, 512), g)
                    cast(l, "inK", pc(l, "inK", 8, 512), wi(768, 512), g)
                    cast(l, "inV", pc(l, "inV", 8, 512), wi(1280, 512), g)
                    cast(l, "inU", pc(l, "inU", 8, 256), wi(0, 256), g)
                    cast(l, "glu", pc(l, "glu", 2, 512), w_glu[l, :, :].rearrange("(k p) c -> p k c", p=128), g)
                    d = pc(l, "brs", 4, 1024)
                    cast(l, "brs", d[:, 0:2, :], w_brs[l, :, :].rearrange("(k p) c -> p k c", p=128), g)
                    cast(l, "brs", d[:, 2:4, :], w_brc[l, :, :].rearrange("(k p) c -> p k c", p=128), g)
                    cast(l, "bra", pc(l, "bra", 4, 1024), w_bra[l, :, :].rearrange("(k p) c -> p k c", p=128), g)
                    for m in range(8):
                        d = pc(l, "g%d" % m, 8, 384)
                        for b in range(3):
                            cast(l, "g%d" % m, d[:, :, b * 128:(b + 1) * 128], wi(2304 + 1024 * b + 128 * m, 128), g)
                    for j in range(2):
                        cast(l, "wo%d" % j, pc(l, "wo%d" % j, 8, 512),
                             w_out[l, :, 512 * j:512 * j + 512].rearrange("(k p) c -> p k c", p=128), g)


        return cast_groups

    PI = math.pi
    P = Plan()
    for l in range(-1, nlayers):
        with ExitStack() as ss:
            st = lambda n, s, d=F32, l=l: ss.enter_context(nc.sbuf_tensor("st%d_%s" % (l + 1, n), s, d))
            regs_all = []

            def RG(n):
                r = Reg(n)
                regs_all.append(r)
                return r
            if l < 0:
                P.op("sp", lambda e: e.dma_start(out=ident[:], in_=ident_d[:, :]), [], [R_ident], dma_sem="c0")
                P.op("pool", lambda e: e.memset(ones_bf[:], 1.0), [], [R_const])
                P.op("pool", lambda e: e.memset(bones_bf[:], 0.0), [], [R_const])
                P.op("pool", lambda e: e.memset(bones_bf[0:64, 0:64], 1.0), [], [R_const])
                P.op("pool", lambda e: e.memset(bones_bf[64:128, 64:128], 1.0), [], [R_const])
                P.op("dve", lambda e: e.tensor_copy(out=identb[:], in_=ident[:]), [R_ident], [R_const])
                cast_groups = issue_casts(P)
                cast_fin = {g_: P.dma_cnt[g_] for g_ in cast_groups}
                continue
            maskt = st("maskt", [128, 1280])
            R_mask = RG("mask")
            P.op("sp", lambda e: e.dma_start(out=maskt[:], in_=mask_d[:, :]), [], [R_mask], dma_sem="c1")

            def s5_params(lr, li, ldt, shape, pfx, R):
                tl = {}

                def t(n):
                    tl[n] = st("%s_%s" % (pfx, n), shape)
                    return tl[n]
                V = lambda fn: P.op("dve", fn, [R], [R])
                A = lambda fn: P.op("act", fn, [R], [R])
                lrc, dt, a, mag, th = t("lrc"), t("dt"), t("a"), t("mag"), t("th")
                V(lambda e: e.tensor_scalar(out=lrc[:], in0=lr, scalar1=-1e-4, scalar2=None, op0=ALU.min))
                A(lambda e: e.activation(out=dt[:], in_=ldt, func=AF.Exp))
                V(lambda e: e.tensor_tensor(out=a[:], in0=lrc[:], in1=dt[:], op=ALU.mult))
                A(lambda e: e.activation(out=mag[:], in_=a[:], func=AF.Exp))
                V(lambda e: e.tensor_tensor(out=th[:], in0=li, in1=dt[:], op=ALU.mult))
                ths, thc0, thc, m = t("ths"), t("thc0"), t("thc"), a
                V(lambda e: e.tensor_copy(out=ths[:], in_=th[:]))
                V(lambda e: e.tensor_scalar(out=thc0[:], in0=th[:], scalar1=PI / 2, scalar2=None, op0=ALU.add))
                V(lambda e: e.tensor_copy(out=thc[:], in_=thc0[:]))
                for kk in range(5):
                    thr = (2 * kk + 1) * PI
                    V(lambda e: e.tensor_scalar(out=m[:], in0=th[:], scalar1=thr, scalar2=-2 * PI, op0=ALU.is_gt, op1=ALU.mult))
                    V(lambda e: e.tensor_tensor(out=ths[:], in0=ths[:], in1=m[:], op=ALU.add))
                    V(lambda e: e.tensor_scalar(out=m[:], in0=thc0[:], scalar1=thr, scalar2=-2 * PI, op0=ALU.is_gt, op1=ALU.mult))
                    V(lambda e: e.tensor_tensor(out=thc[:], in0=thc[:], in1=m[:], op=ALU.add))
                sn, cs = ths, thc
                A(lambda e: e.activation(out=sn[:], in_=ths[:], func=AF.Sin))
                A(lambda e: e.activation(out=cs[:], in_=thc[:], func=AF.Sin))
                n2, t2 = thc0, th
                V(lambda e: e.tensor_tensor(out=n2[:], in0=sn[:], in1=sn[:], op=ALU.mult))
                V(lambda e: e.tensor_tensor(out=t2[:], in0=cs[:], in1=cs[:], op=ALU.mult))
                V(lambda e: e.tensor_tensor(out=n2[:], in0=n2[:], in1=t2[:], op=ALU.add))
                A(lambda e: e.activation(out=n2[:], in_=n2[:], func=AF.Sqrt))
                V(lambda e: e.reciprocal(out=n2[:], in_=n2[:]))
                V(lambda e: e.tensor_tensor(out=sn[:], in0=sn[:], in1=n2[:], op=ALU.mult))
                V(lambda e: e.tensor_tensor(out=cs[:], in0=cs[:], in1=n2[:], op=ALU.mult))
                return {"lrc": lrc, "mag": mag, "sn": sn, "cs": cs, "f1": n2, "f2": t2, "f3": a, "f4": dt}

            P.op("sp", lambda e: e.dma_start(out=PV[l][:], in_=pv_d[l, :, :]), [], [R_pv[l]], dma_sem="c2")
            RS = RG("s5S")
            P.op("dve", lambda e: e.tensor_copy(out=s5c[l][:, 0, :], in_=PV[l][:, PV_LR:PV_LR + 8]), [R_pv[l]], [RS])
            tS = s5_params(PV[l][:, PV_LR:PV_LR + 8], PV[l][:, PV_LI:PV_LI + 8], PV[l][:, PV_DT:PV_DT + 8], [128, 8], "S", RS)
            V = lambda fn: P.op("dve", fn, [RS], [RS])
            V(lambda e: e.tensor_copy(out=s5c[l][:, 0, :], in_=tS["mag"][:]))
            Ct = st("Ctab", [128, 8, T])
            St = st("Stab", [128, 8, T])
            cn = st("cn", [128, 8, 1])
            sn_ = st("snn", [128, 8, 1])
            tA = st("tA", [128, 8, 256])
            tB = st("tB", [128, 8, 256])
            V(lambda e: e.memset(Ct[:, :, 0:1], 1.0))
            V(lambda e: e.memset(St[:, :, 0:1], 0.0))
            V(lambda e: e.tensor_copy(out=cn[:, :, 0], in_=tS["cs"][:]))
            V(lambda e: e.tensor_copy(out=sn_[:, :, 0], in_=tS["sn"][:]))
            n = 1
            while n < T:
                cb = cn[:, :, 0:1].to_broadcast([128, 8, n])
                sbb = sn_[:, :, 0:1].to_broadcast([128, 8, n])
                V(lambda e: e.tensor_tensor(out=tA[:, :, 0:n], in0=Ct[:, :, 0:n], in1=cb, op=ALU.mult))
                V(lambda e: e.tensor_tensor(out=tB[:, :, 0:n], in0=St[:, :, 0:n], in1=sbb, op=ALU.mult))
                V(lambda e: e.tensor_tensor(out=Ct[:, :, n:2 * n], in0=tA[:, :, 0:n], in1=tB[:, :, 0:n], op=ALU.subtract))
                V(lambda e: e.tensor_tensor(out=tA[:, :, 0:n], in0=St[:, :, 0:n], in1=cb, op=ALU.mult))
                V(lambda e: e.tensor_tensor(out=tB[:, :, 0:n], in0=Ct[:, :, 0:n], in1=sbb, op=ALU.mult))
                V(lambda e: e.tensor_tensor(out=St[:, :, n:2 * n], in0=tA[:, :, 0:n], in1=tB[:, :, 0:n], op=ALU.add))
                V(lambda e: e.tensor_tensor(out=tA[:, :, 0:1], in0=cn[:], in1=cn[:], op=ALU.mult))
                V(lambda e: e.tensor_tensor(out=tB[:, :, 0:1], in0=sn_[:], in1=sn_[:], op=ALU.mult))
                V(lambda e: e.tensor_tensor(out=tB[:, :, 1:2], in0=cn[:], in1=sn_[:], op=ALU.mult))
                V(lambda e: e.tensor_tensor(out=cn[:], in0=tA[:, :, 0:1], in1=tB[:, :, 0:1], op=ALU.subtract))
                V(lambda e: e.tensor_scalar(out=sn_[:], in0=tB[:, :, 1:2], scalar1=2.0, scalar2=None, op0=ALU.mult))
                n *= 2
            V(lambda e: e.tensor_copy(out=s5c[l][:, 2, :], in_=sn_[:, :, 0]))
            P.op("dve", lambda e: e.tensor_copy(out=s5c[l][:, 1, :], in_=cn[:, :, 0]), [RS], [RS, R_s5c[l]])
            for j in range(8):
                P.op("sp", lambda e: e.dma_start(out=tscr[l * 8 + j, :, 0:T], in_=Ct[:, j, :]),
                     [RS], [R_wscr[l][PIDX["tab%d" % j]]], dma_sem="tst")
                P.op("sp", lambda e: e.dma_start(out=tscr[l * 8 + j, :, T:2 * T], in_=St[:, j, :]),
                     [RS], [R_wscr[l][PIDX["tab%d" % j]]], dma_sem="tst")
            RR = RG("s5R")
            lam = st("lam", [128, 3, 1024])
            P.op("sp", lambda e: e.dma_start(out=lam[:], in_=lamR_d[l, :, :, :]), [], [RR], dma_sem="c3")
            braw = st("braw", [128, 2, 1024])
            P.op("sp", lambda e: e.dma_start(out=braw[:].rearrange("p a b -> p (a b)"), in_=braw_d[l, :, :]), [], [RR], dma_sem="c4")
            craw = st("craw", [128, 2, 1024])
            P.op("sp", lambda e: e.dma_start(out=craw[:].rearrange("p a b -> p (a b)"), in_=craw_d[l, :, :]), [], [RR], dma_sem="c5")
            bb = st("bb", [128, 2, 1024], BF16)
            V = lambda fn: P.op("dve", fn, [RR], [RR])
            for cc in range(2):
                c0 = cc * 512
                lrA, liA, dtA = lam[:, 0, c0:c0 + 512], lam[:, 1, c0:c0 + 512], lam[:, 2, c0:c0 + 512]
                tR = s5_params(lrA, liA, dtA, [128, 512], "R%d" % cc, RR)
                ar, ai, den, cr, ci, u1 = [st("%s%d" % (n_, cc), [128, 512]) for n_ in ("ar", "ai", "den", "cr", "ci", "u1")]
                lrc = tR["lrc"]
                V(lambda e: e.tensor_tensor(out=ar[:], in0=tR["mag"][:], in1=tR["cs"][:], op=ALU.mult))
                V(lambda e: e.tensor_tensor(out=ai[:], in0=tR["mag"][:], in1=tR["sn"][:], op=ALU.mult))
                V(lambda e: e.tensor_tensor(out=den[:], in0=lrc[:], in1=lrc[:], op=ALU.mult))
                V(lambda e: e.tensor_tensor(out=u1[:], in0=liA, in1=liA, op=ALU.mult))
                V(lambda e: e.tensor_tensor(out=den[:], in0=den[:], in1=u1[:], op=ALU.add))
                V(lambda e: e.reciprocal(out=den[:], in_=den[:]))
                V(lambda e: e.tensor_scalar(out=ar[:], in0=ar[:], scalar1=-1.0, scalar2=None, op0=ALU.add))
                V(lambda e: e.tensor_tensor(out=cr[:], in0=ar[:], in1=lrc[:], op=ALU.mult))
                V(lambda e: e.tensor_tensor(out=u1[:], in0=ai[:], in1=liA, op=ALU.mult))
                V(lambda e: e.tensor_tensor(out=cr[:], in0=cr[:], in1=u1[:], op=ALU.add))
                V(lambda e: e.tensor_tensor(out=cr[:], in0=cr[:], in1=den[:], op=ALU.mult))
                V(lambda e: e.tensor_tensor(out=ci[:], in0=ai[:], in1=lrc[:], op=ALU.mult))
                V(lambda e: e.tensor_tensor(out=u1[:], in0=ar[:], in1=liA, op=ALU.mult))
                V(lambda e: e.tensor_tensor(out=ci[:], in0=ci[:], in1=u1[:], op=ALU.subtract))
                V(lambda e: e.tensor_tensor(out=ci[:], in0=ci[:], in1=den[:], op=ALU.mult))
                bre, bim = braw[:, 0, c0:c0 + 512], braw[:, 1, c0:c0 + 512]
                V(lambda e: e.tensor_tensor(out=u1[:], in0=cr[:], in1=bre, op=ALU.mult))
                V(lambda e: e.tensor_tensor(out=den[:], in0=ci[:], in1=bim, op=ALU.mult))
                V(lambda e: e.tensor_tensor(out=bb[:, 0, c0:c0 + 512], in0=u1[:], in1=den[:], op=ALU.subtract))
                V(lambda e: e.tensor_tensor(out=u1[:], in0=cr[:], in1=bim, op=ALU.mult))
                V(lambda e: e.tensor_tensor(out=den[:], in0=ci[:], in1=bre, op=ALU.mult))
                V(lambda e: e.tensor_tensor(out=bb[:, 1, c0:c0 + 512], in0=u1[:], in1=den[:], op=ALU.add))
            P.op("sp", lambda e: e.dma_start(out=scr(l, "bbar"), in_=bb[:].rearrange("p a b -> p (a b)")),
                 [RR], [R_wscr[l][PIDX["bbar"]]], dma_sem="tst")
            cb_ = st("cb", [128, 2, 1024], BF16)
            V(lambda e: e.tensor_copy(out=cb_[:, 0, :], in_=craw[:, 0, :]))
            V(lambda e: e.tensor_scalar(out=cb_[:, 1, :], in0=craw[:, 1, :], scalar1=-1.0, scalar2=None, op0=ALU.mult))
            P.op("sp", lambda e: e.dma_start(out=scr(l, "cmat"), in_=cb_[:].rearrange("p a b -> p (a b)")),
                 [RR], [R_wscr[l][PIDX["cmat"]]], dma_sem="tst")
            RM = RG("me")
            bt = st("bt", [128, 8, 640])
            mb = st("mb", [128, 8, 640], BF16)
            P.op("sp", lambda e: e.dma_start(out=bt[:].rearrange("p a b -> p (a b)"), in_=btoe_d[l, :, :]), [], [RM], dma_sem="c6")
            if BIAS_PE:
                negm = st("negm", [128, 640])
                P.op("dve", lambda e: e.tensor_scalar(out=negm[:], in0=maskt[:, 0:640], scalar1=-1.0, scalar2=30000.0, op0=ALU.add, op1=ALU.mult), [R_mask], [RM])
                P.op("dve", lambda e: e.scalar_tensor_tensor(out=bt[:], in0=bt[:], scalar=8.0, in1=maskt[:, 0:640].unsqueeze(1).to_broadcast([128, 8, 640]),
                                                             op0=ALU.mult, op1=ALU.mult), [RM, R_mask], [RM])
                P.op("dve", lambda e: e.tensor_tensor(out=mb[:], in0=bt[:], in1=negm[:].unsqueeze(1).to_broadcast([128, 8, 640]), op=ALU.add), [RM], [RM])
            else:
                P.op("act", lambda e: e.activation(out=bt[:], in_=bt[:], func=AF.Exp), [RM], [RM])
                P.op("dve", lambda e: e.tensor_tensor(out=mb[:], in0=bt[:], in1=maskt[:, 0:640].unsqueeze(1).to_broadcast([128, 8, 640]), op=ALU.mult), [RM, R_mask], [RM])
            for hc in range(4):
                P.op("sp", lambda e: e.dma_start(out=scr(l, "me%d" % hc), in_=mb[:, 2 * hc:2 * hc + 2, :].rearrange("p a b -> p (a b)")),
                     [RM], [R_wscr[l][PIDX["me%d" % hc]]], dma_sem="tst")
            RD = RG("cwd")
            for c in range(2):
                dg = st("dg%d" % c, [128, 31, 128], BF16)
                for k in range(31):
                    P.op("dve", lambda e: e.tensor_scalar(out=dg[:, k, :], in0=ident[:], scalar1=PV[l][:, PV_CW + c * 31 + k:PV_CW + c * 31 + k + 1],
                                                                                scalar2=None, op0=ALU.mult), [R_pv[l], R_ident], [RD])
                P.op("sp", lambda e: e.dma_start(out=scr(l, "cw%d" % c), in_=dg[:].rearrange("p a b -> p (a b)")),
                     [RD], [R_wscr[l][PIDX["cw%d" % c]]], dma_sem="tst")
            P.barrier()

    xres = sb("xres", [128, KC, T], F32)
    hT = sb("hT", [128, KC, T], BF16)
    hid = sb("hid", [128, HC, T], BF16)
    sq = sb("sq", [128, KC, T], BF16)
    NTMP = 10
    tmpall = sb("tmpall", [128, NTMP, T], F32)
    du = sb("du", [128, 2, T], F32)
    cv = sb("cv", [128, 2, T], F32)
    NSLOT = 6
    slots = [sb("slot%d" % i, [128, SLOTW], BF16) for i in range(NSLOT)]
    kbuf = [sb("kbuf%d" % l, [128, 4, 2, T], BF16) for l in range(L)]
    vbuf = [sb("vbuf%d" % l, [128, 8, 512], BF16) for l in range(L)]
    hbuf = [sb("hbuf%d" % l, [128, 2, 32 + T], BF16) for l in range(L)]
    ps = [es.enter_context(nc.psum_tensor("ps%d" % i, [128, T], F32)) for i in range(8)]
    R_x = [Reg("x%d" % k) for k in range(KC)]
    R_h = [Reg("h%d" % k) for k in range(KC)]
    R_hid = [Reg("hid%d" % k) for k in range(HC)]
    R_sq = [Reg("sq%d" % k) for k in range(KC)]
    R_tmp = [Reg("tmp%d" % k) for k in range(NTMP)]
    R_du, R_cv = Reg("du"), [Reg("cv0"), Reg("cv1")]
    R_slot = [Reg("slot%d" % i) for i in range(NSLOT)]
    R_k = [[Reg("k%d_%d" % (l, h)) for h in range(2)] for l in range(L)]
    R_v = [[Reg("v%d_%d" % (l, h)) for h in range(2)] for l in range(L)]
    R_hb = [[Reg("hb%d_%d" % (l, c)) for c in range(2)] for l in range(L)]
    R_ps = [Reg("ps%d" % i, excl=True) for i in range(8)]
    R_xd, R_yd = Reg("xd"), Reg("yd")

    bank_free = list(range(8))

    cur_pool = [bank_free]

    def bank():
        return cur_pool[0].pop(0)

    def unbank(b):
        cur_pool[0].append(b)

    tmp_rr = [0]

    def tmp():
        i = tmp_rr[0] % NTMP
        tmp_rr[0] += 1
        return tmpall[:, i, :], R_tmp[i]

    order = []
    DRY = [True]
    ring = {"next_load": 0, "next_use": 0, "free": list(range(NSLOT)), "where": {}}

    def ring_fill():
        if DRY[0]:
            return
        while ring["free"] and ring["next_load"] < len(order):
            idx = ring["next_load"]
            l, n_ = order[idx]
            sl = ring["free"].pop(0)
            i = PIDX[n_]
            w_ = PIECES[i][1]
            if n_.startswith("tab"):
                j = int(n_[3:])
                src = tscr[l * 8 + j, :, :]
                dst = slots[sl][:, 0:2048].bitcast(F32)
            else:
                src = scr(l, n_)
                dst = slots[sl][:, 0:w_]
            P.op("sp", lambda e, dst=dst, src=src: e.dma_start(out=dst, in_=src), [R_wscr[l][i]], [R_slot[sl]], dma_sem="ring%d" % sl)
            ring["where"][idx] = sl
            ring["next_load"] += 1

    def ring_get(l, name):
        if DRY[0]:
            order.append((l, name))
            return slots[0], R_slot[0], 0
        idx = ring["next_use"]
        assert order[idx] == (l, name), (order[idx], l, name)
        assert idx in ring["where"], "ring deadlock at %s" % name
        ring["next_use"] += 1
        sl = ring["where"][idx]
        return slots[sl], R_slot[sl], sl

    def ring_rel(sl):
        if DRY[0]:
            return
        ring["free"].append(sl)
        ring_fill()

    def mm(out, lhsT, rhs, start, stop, reads, writes, **kw):
        P.op("pe", lambda e: e.matmul(out, lhsT=lhsT, rhs=rhs, start=start, stop=stop, **kw), reads, writes)

    def rmsnorm(l, col):
        for k in range(KC):
            if k % 2 == 1:
                P.op("act", lambda e: e.activation(out=sq[:, k, :], in_=xres[:, k, :], func=AF.Square), [R_x[k]], [R_sq[k]])
            else:
                P.op("pool", lambda e: e.tensor_tensor(out=sq[:, k, :], in0=xres[:, k, :], in1=xres[:, k, :], op=ALU.mult), [R_x[k]], [R_sq[k]])
        b = bank()
        for k in range(KC):
            mm(ps[b][:], ones_bf[:], sq[:, k, :], k == 0, k == KC - 1, [R_sq[k]], [R_ps[b]])
        rs, rr = tmp()
        P.op("act", lambda e: e.activation(out=rs, in_=ps[b][:], func=AF.Ln, bias=EPS, scale=1.0 / D), [R_ps[b]], [rr])
        unbank(b)
        P.op("act", lambda e: e.activation(out=rs, in_=rs, func=AF.Exp, scale=-0.5), [rr], [rr])
        for k in range(KC):
            P.op("dve", lambda e, k=k: e.scalar_tensor_tensor(out=hT[:, k, :], in0=xres[:, k, :], scalar=PV[l][:, col + k:col + k + 1],
                                                              in1=rs, op0=ALU.mult, op1=ALU.mult), [R_x[k], rr], [R_h[k]])

    def ffn(l, f):
        rmsnorm(l, PV_N1 if f == 0 else PV_N2)
        for j in range(11):
            sl, rs_, si = ring_get(l, "f%du%d" % (f + 1, j))
            if j == 0:
                bk = {(half, ab): bank() for half in range(2) for ab in range(2)}
                for k in range(KC):
                    for half in range(2):
                        for ab in range(2):
                            mm(ps[bk[(half, ab)]][:], sl[:, k * 512 + ab * 256 + half * 128: k * 512 + ab * 256 + half * 128 + 128], hT[:, k, :],
                               k == 0, k == KC - 1, [rs_, R_h[k]], [R_ps[bk[(half, ab)]]])
                for half in range(2):
                    m = half
                    bA, bB = bk[(half, 0)], bk[(half, 1)]
                    ta, ra = tmp()
                    P.op("act", lambda e: e.activation(out=ta, in_=ps[bA][:], func=AF.Silu), [R_ps[bA]], [ra])
                    unbank(bA)
                    P.op("dve", lambda e: e.tensor_tensor(out=hid[:, m, :], in0=ps[bB][:], in1=ta, op=ALU.mult), [R_ps[bB], ra], [R_hid[m]])
                    unbank(bB)
                ring_rel(si)
                continue
            for half in range(2):
                m = 2 * j + half
                bA, bB = bank(), bank()
                for k in range(KC):
                    mm(ps[bA][:], sl[:, k * 512 + half * 128: k * 512 + half * 128 + 128], hT[:, k, :], k == 0, k == KC - 1,
                       [rs_, R_h[k]], [R_ps[bA]])
                for k in range(KC):
                    mm(ps[bB][:], sl[:, k * 512 + 256 + half * 128: k * 512 + 256 + half * 128 + 128], hT[:, k, :], k == 0, k == KC - 1,
                       [rs_, R_h[k]], [R_ps[bB]])
                ta, ra = tmp()
                P.op("act", lambda e, bA=bA, ta=ta: e.activation(out=ta, in_=ps[bA][:], func=AF.Silu), [R_ps[bA]], [ra])
                unbank(bA)
                P.op("dve", lambda e, bB=bB, ta=ta, m=m: e.tensor_tensor(out=hid[:, m, :], in0=ps[bB][:], in1=ta, op=ALU.mult),
                     [R_ps[bB], ra], [R_hid[m]])
                unbank(bB)
            ring_rel(si)
        bd = [bank() for _ in range(KC)]
        for j in range(6):
            nk = 4 if j < 5 else 2
            sl, rs_, si = ring_get(l, "f%dd%d" % (f + 1, j))
            for m in range(KC):
                for kk in range(nk):
                    k = 4 * j + kk
                    mm(ps[bd[m]][:], sl[:, kk * 1024 + m * 128: kk * 1024 + m * 128 + 128], hid[:, k, :], k == 0, k == HC - 1,
                       [rs_, R_hid[k]], [R_ps[bd[m]]])
            ring_rel(si)
        for m in range(KC):
            P.op("dve", lambda e, m=m: e.scalar_tensor_tensor(out=xres[:, m, :], in0=ps[bd[m]][:], scalar=0.5, in1=xres[:, m, :],
                                                              op0=ALU.mult, op1=ALU.add), [R_ps[bd[m]], R_x[m]], [R_x[m]])
            unbank(bd[m])

    MERG, QN, UT, ATT, S5O, CVO = 0, 8, 12, 14, 18, 20
    branches = dbg.get("branches", (0, 1, 2))
    NS5T = 12
    s5tmp = sb("s5tmp", [128, NS5T, T], F32)
    R_s5t = [Reg("s5t%d" % k) for k in range(NS5T)]
    ebuf = sb("ebuf", [128, 4, T], BF16)
    R_e = [Reg("e%d" % k) for k in range(4)]

    def mixer(l, i):
        half, hhalf = i % 2, 1 - i % 2
        pvl = PV[l]
        rmsnorm(l, PV_NM)
        hb = hbuf[l]
        sl, rs_, si = ring_get(l, "inU")
        for c in range(2):
            b = bank()
            for k in range(KC):
                mm(ps[b][:], sl[:, k * 256 + c * 128: k * 256 + c * 128 + 128], hT[:, k, :], k == 0, k == KC - 1, [rs_, R_h[k]], [R_ps[b]])
            P.op("act", lambda e: e.activation(out=hid[:, UT + c, :], in_=ps[b][:], func=AF.Copy), [R_ps[b]], [R_hid[UT + c]])
            P.op("act", lambda e: e.activation(out=du[:, c, :], in_=ps[b][:], func=AF.Copy, scale=pvl[:, PV_D + c:PV_D + c + 1]), [R_ps[b]], [R_du])
            unbank(b)
        ring_rel(si)

        def attention_hc(hc, slM, rM):
            bnum, bden = bank(), bank()
            steps = []
            for hl in range(2):
                for kbi in range(8):
                    if 4 * i - 4 + kbi >= 0:
                        steps.append((hl, kbi))
            firsts = {0: True, 1: True}
            info = {}
            for n in range(len(steps) + 3):
                if n < len(steps):
                    hl, kbi = steps[n]
                    p0 = 64 * hl
                    gkb = 4 * i - 4 + kbi
                    khalf = half if kbi >= 4 else hhalf
                    kcol = (kbi % 4) * 128
                    q_lo, q_hi = max(0, kbi - 4), min(3, kbi)
                    nq = q_hi - q_lo + 1
                    rel_lo = 4 * i + q_lo - gkb
                    bs = bank()
                    me = slM[:, hl * 640 + rel_lo * 128: hl * 640 + (rel_lo + nq) * 128]
                    ei = n % 4
                    mm(ps[bs][:, 0:nq * 128], kbuf[l][p0:p0 + 64, hc, khalf, kcol:kcol + 128], hid[p0:p0 + 64, QN + hc, q_lo * 128:(q_hi + 1) * 128],
                       True, True, [R_k[l][khalf], R_hid[QN + hc]], [R_ps[bs]])
                    P.op("act", lambda e: e.activation(out=ebuf[:, ei, 0:nq * 128], in_=ps[bs][:, 0:nq * 128], func=AF.Exp, scale=0.125), [R_ps[bs]], [R_e[ei]])
                    P.op("pool", lambda e: e.tensor_tensor(out=ebuf[:, ei, 0:nq * 128], in0=ebuf[:, ei, 0:nq * 128], in1=me, op=ALU.mult), [R_e[ei], rM], [R_e[ei]])
                    unbank(bs)
                    info[n] = (hl, kbi, khalf, q_lo, q_hi, ei)
                if n >= 3:
                    hl, kbi, khalf, q_lo, q_hi, ei = info[n - 3]
                    p0 = 64 * hl
                    h = 2 * hc + hl
                    nq = q_hi - q_lo + 1
                    mm(ps[bnum][p0:p0 + 64, q_lo * 128:(q_hi + 1) * 128], vbuf[l][:, khalf * 4 + kbi % 4, h * 64:(h + 1) * 64], ebuf[:, ei, 0:nq * 128],
                       firsts[hl], False, [R_v[l][khalf], R_e[ei]], [R_ps[bnum]], skip_group_check=True)
                    mm(ps[bden][p0:p0 + 64, q_lo * 128:(q_hi + 1) * 128], ones_bf[:, 0:64], ebuf[:, ei, 0:nq * 128],
                       firsts[hl], False, [R_e[ei]], [R_ps[bden]], skip_group_check=True)
                    firsts[hl] = False
                yield
            td, rd = tmp()
            P.op("act", lambda e: e.activation(out=td, in_=ps[bden][:], func=AF.Ln), [R_ps[bden]], [rd])
            unbank(bden)
            P.op("act", lambda e: e.activation(out=td, in_=td, func=AF.Exp, scale=-1.0), [rd], [rd])
            P.op("dve", lambda e: e.tensor_tensor(out=hid[:, ATT + hc, :], in0=ps[bnum][:], in1=td, op=ALU.mult), [R_ps[bnum], rd], [R_hid[ATT + hc]])
            unbank(bnum)

        def side():
            sl, rs_, si = ring_get(l, "inA")
            if i == 0:
                for c in range(2):
                    P.op("pool", lambda e: e.memset(hb[:, c, 0:32], 0.0), [], [R_hb[l][c]])
            for c in range(2):
                bA, bG = bank(), bank()
                for k in range(KC):
                    mm(ps[bA][:], sl[:, k * 512 + c * 128: k * 512 + c * 128 + 128], hT[:, k, :], k == 0, k == KC - 1, [rs_, R_h[k]], [R_ps[bA]])
                for k in range(KC):
                    mm(ps[bG][:], sl[:, k * 512 + 256 + c * 128: k * 512 + 256 + c * 128 + 128], hT[:, k, :], k == 0, k == KC - 1, [rs_, R_h[k]], [R_ps[bG]])
                tg, rg = tmp()
                P.op("act", lambda e: e.activation(out=tg, in_=ps[bG][:], func=AF.Sigmoid), [R_ps[bG]], [rg])
                unbank(bG)
                P.op("dve", lambda e: e.tensor_tensor(out=hb[:, c, 32:32 + T], in0=ps[bA][:], in1=tg, op=ALU.mult), [R_ps[bA], rg], [R_hb[l][c]])
                unbank(bA)
                yield
            ring_rel(si)
            for c in range(2):
                sl, rs_, si = ring_get(l, "cw%d" % c)
                b = bank()
                for k in range(31):
                    mm(ps[b][:], sl[:, k * 128:(k + 1) * 128], hb[:, c, 2 + k:2 + k + T], k == 0, k == 30, [rs_, R_hb[l][c]], [R_ps[b]])
                ring_rel(si)
                P.op("act", lambda e: e.activation(out=cv[:, c, :], in_=ps[b][:], func=AF.Identity, bias=pvl[:, PV_CB + c:PV_CB + c + 1], scale=1.0), [R_ps[b]], [R_cv[c]])
                unbank(b)
                P.op("pool", lambda e: e.tensor_copy(out=hb[:, c, 0:32], in_=hb[:, c, T:T + 32]), [R_hb[l][c]], [R_hb[l][c]])
                P.op("act", lambda e: e.activation(out=sq[:, c, :], in_=cv[:, c, :], func=AF.Copy), [R_cv[c]], [R_sq[c]])
                P.op("act", lambda e: e.activation(out=sq[:, 2 + c, :], in_=cv[:, c, :], func=AF.Square), [R_cv[c]], [R_sq[2 + c]])
                yield
            b1, b2 = bank(), bank()
            for c in range(2):
                mm(ps[b1][:], ones_bf[:], sq[:, c, :], c == 0, c == 1, [R_sq[c]], [R_ps[b1]])
            for c in range(2):
                mm(ps[b2][:], ones_bf[:], sq[:, 2 + c, :], c == 0, c == 1, [R_sq[2 + c]], [R_ps[b2]])
            tm, rm = tmp()
            tv, rv = tmp()
            P.op("act", lambda e: e.activation(out=tm, in_=ps[b1][:], func=AF.Copy, scale=1.0 / 256), [R_ps[b1]], [rm])
            unbank(b1)
            P.op("pool", lambda e: e.tensor_tensor(out=tv, in0=tm, in1=tm, op=ALU.mult), [rm], [rv])
            P.op("dve", lambda e: e.scalar_tensor_tensor(out=tv, in0=ps[b2][:], scalar=1.0 / 256, in1=tv, op0=ALU.mult, op1=ALU.subtract), [R_ps[b2], rv], [rv])
            unbank(b2)
            P.op("act", lambda e: e.activation(out=tv, in_=tv, func=AF.Ln, bias=EPS, scale=1.0), [rv], [rv])
            P.op("act", lambda e: e.activation(out=tv, in_=tv, func=AF.Exp, scale=-0.5), [rv], [rv])
            for c in range(2):
                P.op("pool", lambda e: e.tensor_tensor(out=cv[:, c, :], in0=cv[:, c, :], in1=tm, op=ALU.subtract), [R_cv[c], rm], [R_cv[c]])
                P.op("pool", lambda e: e.tensor_tensor(out=cv[:, c, :], in0=cv[:, c, :], in1=tv, op=ALU.mult), [R_cv[c], rv], [R_cv[c]])
            yield

            def ln_silu():
                for c in range(2):
                    P.op("act", lambda e: e.activation(out=hid[:, CVO + c, :], in_=cv[:, c, :], func=AF.Silu, bias=pvl[:, PV_LB + c:PV_LB + c + 1],
                                                       scale=pvl[:, PV_LG + c:PV_LG + c + 1]), [R_cv[c]], [R_hid[CVO + c]])
            qk_st = {}

            def qk_PE(cq):
                isq, c = cq < 4, cq % 4
                if cq == 0:
                    qk_st["sl"] = ring_get(l, "inQ")
                if cq == 4:
                    ring_rel(qk_st["sl"][2])
                    qk_st["sl"] = ring_get(l, "inK")
                sl, rs_, si = qk_st["sl"]
                b = bank()
                for k in range(KC):
                    mm(ps[b][:], sl[:, k * 512 + c * 128: k * 512 + c * 128 + 128], hT[:, k, :], k == 0, k == KC - 1, [rs_, R_h[k]], [R_ps[b]])
                tq, rq = tmp()
                sqi = cq % 4
                P.op("act", lambda e: e.activation(out=sq[:, sqi, :], in_=ps[b][:], func=AF.Square), [R_ps[b]], [R_sq[sqi]])
                P.op("act", lambda e: e.activation(out=tq, in_=ps[b][:], func=AF.Copy), [R_ps[b]], [rq])
                unbank(b)
                qk_st[cq] = (tq, rq, sqi)
                if cq == 7:
                    ring_rel(si)

            def qk_BLF(cq):
                isq, c = cq < 4, cq % 4
                tq, rq, sqi = qk_st[cq]
                gcol = PV_QG if isq else PV_KG
                b2 = bank()
                mm(ps[b2][:], bones_bf[:], sq[:, sqi, :], True, True, [R_sq[sqi]], [R_ps[b2]])
                tr, rr = tmp()
                P.op("act", lambda e: e.activation(out=tr, in_=ps[b2][:], func=AF.Ln, bias=EPS, scale=1.0 / 64), [R_ps[b2]], [rr])
                unbank(b2)
                P.op("act", lambda e: e.activation(out=tr, in_=tr, func=AF.Exp, scale=-0.5), [rr], [rr])
                dst = hid[:, QN + c, :] if isq else kbuf[l][:, c, half, :]
                dreg = [R_hid[QN + c]] if isq else [R_k[l][half]]
                P.op("dve", lambda e: e.scalar_tensor_tensor(out=dst, in0=tq, scalar=pvl[:, gcol:gcol + 1], in1=tr,
                                                             op0=ALU.mult, op1=ALU.mult), [rq, rr], dreg)
            for cq in range(9):
                if cq < 8:
                    qk_PE(cq)
                if cq == 2:
                    ln_silu()
                if cq >= 1:
                    qk_BLF(cq - 1)
                yield
            sl, rs_, si = ring_get(l, "inV")
            for tb in range(4):
                b = bank()
                for k in range(KC):
                    mm(ps[b][:], hT[:, k, tb * 128:(tb + 1) * 128], sl[:, k * 512:(k + 1) * 512], k == 0, k == KC - 1, [rs_, R_h[k]], [R_ps[b]])
                P.op("act", lambda e: e.activation(out=vbuf[l][:, half * 4 + tb, :], in_=ps[b][:], func=AF.Copy), [R_ps[b]], [R_v[l][half]])
                unbank(b)
                yield
            ring_rel(si)
            for hc in range(4):
                slM, rM, siM = ring_get(l, "me%d" % hc)
                for _ in attention_hc(hc, slM, rM):
                    yield
                ring_rel(siM)

        slB, rB, siB = ring_get(l, "bbar")
        slC, rC, siC = ring_get(l, "cmat")
        if i == 0:
            P.op("pool", lambda e: e.memset(s5c[l][:, 3:5, :], 0.0), [], R_init[l])
        pend = None
        ysb = None
        assert len(bank_free) == 8
        pool_s5 = [bank_free.pop(0) for _ in range(3)]
        pool_side = list(bank_free)
        del bank_free[:]
        cur_pool[0] = pool_s5
        sgen = side()
        side_total = 18 + 4 * (3 + (8 if i == 0 else 16))
        side_done = 0
        for it in range(9):
            if pend is not None:
                j, (t1, r1), (t2, r2), (t3, r3), (t4, r4), (t5, r5), (t6, r6), cosT, sinT, rT, siT = pend
                xr_i, xi_i = 4 + (j % 2) * 2, 5 + (j % 2) * 2
                P.op("dve", lambda e: e.tensor_tensor(out=t1, in0=t2, in1=cosT, op=ALU.mult), [r2, rT], [r1])
                P.op("dve", lambda e: e.tensor_tensor(out=t3, in0=t4, in1=sinT, op=ALU.mult), [r4, rT], [r3])
                P.op("dve", lambda e: e.tensor_tensor(out=sq[:, xr_i, :], in0=t1, in1=t3, op=ALU.subtract), [r1, r3], [R_sq[xr_i]])
                P.op("dve", lambda e: e.tensor_tensor(out=t5, in0=t4, in1=cosT, op=ALU.mult), [r4, rT], [r5])
                P.op("dve", lambda e: e.tensor_tensor(out=t6, in0=t2, in1=sinT, op=ALU.mult), [r2, rT], [r6])
                P.op("dve", lambda e: e.tensor_tensor(out=sq[:, xi_i, :], in0=t5, in1=t6, op=ALU.add), [r5, r6], [R_sq[xi_i]])
                ring_rel(siT)
            if it < 8:
                j = it
                cc = j // 4
                slT, rT, siT = ring_get(l, "tab%d" % j)
                tabf = slT[:, 0:2048].bitcast(F32)
                cosT, sinT = tabf[:, 0:T], tabf[:, T:2 * T]
                br_, bi_ = bank(), bank()
                mm(ps[br_][:], slB[:, cc * 512 + (j % 4) * 128: cc * 512 + (j % 4) * 128 + 128], hid[:, UT + cc, :], True, True, [rB, R_hid[UT + cc]], [R_ps[br_]])
                mm(ps[bi_][:], slB[:, 1024 + cc * 512 + (j % 4) * 128: 1024 + cc * 512 + (j % 4) * 128 + 128], hid[:, UT + cc, :], True, True, [rB, R_hid[UT + cc]], [R_ps[bi_]])
                tt = [(s5tmp[:, (6 * j + q) % NS5T, :], R_s5t[(6 * j + q) % NS5T]) for q in range(6)]
                (t1, r1), (t2, r2), (t3, r3), (t4, r4), (t5, r5), (t6, r6) = tt
                P.op("dve", lambda e: e.tensor_tensor(out=t1, in0=ps[br_][:], in1=cosT, op=ALU.mult), [R_ps[br_], rT], [r1])
                P.op("dve", lambda e: e.tensor_tensor(out=t4, in0=ps[br_][:], in1=sinT, op=ALU.mult), [R_ps[br_], rT], [r4])
                unbank(br_)
                P.op("dve", lambda e: e.tensor_tensor(out=t2, in0=ps[bi_][:], in1=sinT, op=ALU.mult), [R_ps[bi_], rT], [r2])
                P.op("dve", lambda e: e.tensor_tensor(out=t3, in0=ps[bi_][:], in1=cosT, op=ALU.mult), [R_ps[bi_], rT], [r3])
                unbank(bi_)
                P.op("dve", lambda e: e.tensor_tensor(out=t1, in0=t1, in1=t2, op=ALU.add), [r1, r2], [r1])
                P.op("dve", lambda e: e.tensor_tensor(out=t3, in0=t3, in1=t4, op=ALU.subtract), [r3, r4], [r3])
                rdec = s5c[l][:, 0, j:j + 1].to_broadcast([128, T])
                P.op("dve", lambda e: e.tensor_tensor_scan(out=t2, data0=rdec, data1=t1, initial=s5c[l][:, 3, j:j + 1],
                                                           op0=ALU.mult, op1=ALU.add), [r1, R_init[l][j], R_s5c[l]], [r2])
                P.op("dve", lambda e: e.tensor_tensor_scan(out=t4, data0=rdec, data1=t3, initial=s5c[l][:, 4, j:j + 1],
                                                           op0=ALU.mult, op1=ALU.add), [r3, R_init[l][j], R_s5c[l]], [r4])
                P.op("dve", lambda e: e.tensor_scalar(out=t1[:, 0:1], in0=t4[:, T - 1:T], scalar1=s5c[l][:, 2, j:j + 1], scalar2=None, op0=ALU.mult),
                     [r4, R_s5c[l]], [r1])
                P.op("dve", lambda e: e.tensor_scalar(out=t1[:, 1:2], in0=t2[:, T - 1:T], scalar1=s5c[l][:, 2, j:j + 1], scalar2=None, op0=ALU.mult),
                     [r2, R_s5c[l]], [r1])
                P.op("dve", lambda e: e.scalar_tensor_tensor(out=s5c[l][:, 3, j:j + 1], in0=t2[:, T - 1:T], scalar=s5c[l][:, 1, j:j + 1], in1=t1[:, 0:1],
                                                             op0=ALU.mult, op1=ALU.subtract), [r2, r1, R_s5c[l]], [R_init[l][j]])
                P.op("dve", lambda e: e.scalar_tensor_tensor(out=s5c[l][:, 4, j:j + 1], in0=t4[:, T - 1:T], scalar=s5c[l][:, 1, j:j + 1], in1=t1[:, 1:2],
                                                             op0=ALU.mult, op1=ALU.add), [r4, r1, R_s5c[l]], [R_init[l][j]])
                newpend = (j, (t1, r1), (t2, r2), (t3, r3), (t4, r4), (t5, r5), (t6, r6), cosT, sinT, rT, siT)
            else:
                newpend = None
            if sgen is not None:
                cur_pool[0] = pool_side
                nsteps = -(-(side_total - side_done) // (9 - it))
                for _ in range(nsteps):
                    try:
                        next(sgen)
                        side_done += 1
                    except StopIteration:
                        sgen = None
                        break
                cur_pool[0] = pool_s5
            if pend is not None:
                j = pend[0]
                cc = j // 4
                xr_i, xi_i = 4 + (j % 2) * 2, 5 + (j % 2) * 2
                if j % 4 == 0:
                    ysb = bank()
                mm(ps[ysb][:], slC[:, j * 128:(j + 1) * 128], sq[:, xr_i, :], j % 4 == 0, False, [rC, R_sq[xr_i]], [R_ps[ysb]])
                mm(ps[ysb][:], slC[:, 1024 + j * 128:1024 + (j + 1) * 128], sq[:, xi_i, :], False, j % 4 == 3, [rC, R_sq[xi_i]], [R_ps[ysb]])
                if j % 4 == 3:
                    (ty, ry), (tz, rz) = tmp(), tmp()
                    P.op("dve", lambda e: e.tensor_tensor(out=ty, in0=ps[ysb][:], in1=du[:, cc, :], op=ALU.add), [R_ps[ysb], R_du], [ry])
                    unbank(ysb)
                    P.op("act", lambda e: e.activation(out=tz, in_=ty, func=AF.Square), [ry], [rz])
                    P.op("act", lambda e: e.activation(out=tz, in_=tz, func=AF.Identity, bias=1.0, scale=0.044715), [rz], [rz])
                    P.op("pool", lambda e: e.tensor_tensor(out=tz, in0=tz, in1=ty, op=ALU.mult), [rz, ry], [rz])
                    P.op("act", lambda e: e.activation(out=tz, in_=tz, func=AF.Sigmoid, scale=1.5957691216057308), [rz], [rz])
                    P.op("pool", lambda e: e.tensor_tensor(out=hid[:, MERG + cc, :], in0=ty, in1=tz, op=ALU.mult), [ry, rz], [R_hid[MERG + cc]])
            pend = newpend
        ring_rel(siB)
        ring_rel(siC)
        if sgen is not None:
            cur_pool[0] = pool_side
            for _ in sgen:
                pass
        assert len(pool_s5) + len(pool_side) == 8, (pool_s5, pool_side)
        bank_free[:] = pool_s5 + pool_side
        cur_pool[0] = bank_free
        sl, rs_, si = ring_get(l, "glu")
        for c in range(2):
            bA, bG = bank(), bank()
            for k in range(2):
                mm(ps[bA][:], sl[:, k * 512 + c * 128: k * 512 + c * 128 + 128], hid[:, MERG + k, :], k == 0, k == 1, [rs_, R_hid[MERG + k]], [R_ps[bA]])
            for k in range(2):
                mm(ps[bG][:], sl[:, k * 512 + 256 + c * 128: k * 512 + 256 + c * 128 + 128], hid[:, MERG + k, :], k == 0, k == 1, [rs_, R_hid[MERG + k]], [R_ps[bG]])
            tg, rg = tmp()
            P.op("act", lambda e: e.activation(out=tg, in_=ps[bG][:], func=AF.Sigmoid), [R_ps[bG]], [rg])
            unbank(bG)
            P.op("dve", lambda e: e.tensor_tensor(out=hid[:, S5O + c, :], in0=ps[bA][:], in1=tg, op=ALU.mult), [R_ps[bA], rg], [R_hid[S5O + c]])
            unbank(bA)
        ring_rel(si)

        slS, rS, siS = ring_get(l, "brs")
        slA, rA, siA = ring_get(l, "bra")
        for m in range(8):
            slG, rG, siG = ring_get(l, "g%d" % m)
            acc = None
            for b in range(3):
                bg, by = bank(), bank()
                for k in range(KC):
                    mm(ps[bg][:], slG[:, k * 384 + b * 128: k * 384 + b * 128 + 128], hT[:, k, :], k == 0, k == KC - 1, [rG, R_h[k]], [R_ps[bg]])
                if b == 0:
                    for k in range(2):
                        mm(ps[by][:], slS[:, k * 1024 + m * 128: k * 1024 + m * 128 + 128], hid[:, S5O + k, :], k == 0, k == 1, [rS, R_hid[S5O + k]], [R_ps[by]])
                elif b == 1:
                    for k in range(4):
                        mm(ps[by][:], slA[:, k * 1024 + m * 128: k * 1024 + m * 128 + 128], hid[:, ATT + k, :], k == 0, k == 3, [rA, R_hid[ATT + k]], [R_ps[by]])
                else:
                    for k in range(2):
                        mm(ps[by][:], slS[:, (2 + k) * 1024 + m * 128: (2 + k) * 1024 + m * 128 + 128], hid[:, CVO + k, :], k == 0, k == 1, [rS, R_hid[CVO + k]], [R_ps[by]])
                tg, rg = tmp()
                P.op("act", lambda e: e.activation(out=tg, in_=ps[bg][:], func=AF.Sigmoid,
                                                   bias=pvl[:, PV_BG + b * 8 + m:PV_BG + b * 8 + m + 1], scale=1.0), [R_ps[bg]], [rg])
                unbank(bg)
                last = (b == max(branches)) and acc is not None
                if b not in branches:
                    unbank(by)
                    continue
                if acc is None:
                    P.op("dve", lambda e: e.tensor_tensor(out=tg, in0=ps[by][:], in1=tg, op=ALU.mult), [R_ps[by], rg], [rg])
                    acc = (tg, rg)
                else:
                    P.op("dve", lambda e: e.tensor_tensor(out=tg, in0=ps[by][:], in1=tg, op=ALU.mult), [R_ps[by], rg], [rg])
                    if last:
                        P.op("pool", lambda e: e.tensor_tensor(out=hid[:, MERG + m, :], in0=acc[0], in1=tg, op=ALU.add), [acc[1], rg], [R_hid[MERG + m]])
                    else:
                        P.op("pool", lambda e: e.tensor_tensor(out=acc[0], in0=acc[0], in1=tg, op=ALU.add), [acc[1], rg], [acc[1]])
                unbank(by)
            if len(branches) == 1:
                P.op("pool", lambda e: e.tensor_copy(out=hid[:, MERG + m, :], in_=acc[0]), [acc[1]], [R_hid[MERG + m]])
            ring_rel(siG)
        ring_rel(siS)
        ring_rel(siA)
        for jj in range(2):
            sl, rs_, si = ring_get(l, "wo%d" % jj)
            for mm_ in range(4):
                m = 4 * jj + mm_
                b = bank()
                for k in range(KC):
                    mm(ps[b][:], sl[:, k * 512 + mm_ * 128: k * 512 + mm_ * 128 + 128], hid[:, MERG + k, :], k == 0, k == KC - 1, [rs_, R_hid[MERG + k]], [R_ps[b]])
                P.op("dve", lambda e: e.tensor_tensor(out=xres[:, m, :], in0=ps[b][:], in1=xres[:, m, :], op=ALU.add), [R_ps[b], R_x[m]], [R_x[m]])
                unbank(b)
            ring_rel(si)

    def run_tiles():
        xtok = tmpall[:, 0:8, :].rearrange("p (a b) c -> p a (b c)", b=2)
        stages = dbg.get("stages", ("f1", "mix", "f2"))
        for s_ in range(NSEQ):
            for i in range(NT):
                r0 = s_ * S + i * T
                P.op("sp", lambda e, r0=r0: e.dma_start(out=xtok, in_=x_d[r0:r0 + T, :].rearrange("(a p) c -> p a c", p=128)), [R_xd], R_tmp[0:8], dma_sem="xin")
                for k in range(KC):
                    b = bank()
                    for tb in range(4):
                        P.op("pe", lambda e, b=b, tb=tb, k=k: e.transpose(out=ps[b][:, tb * 128:(tb + 1) * 128], in_=xtok[:, tb, k * 128:(k + 1) * 128], identity=ident[:]),
                             R_tmp[0:8] + [R_ident], [R_ps[b]])
                    P.op("act", lambda e, b=b, k=k: e.activation(out=xres[:, k, :], in_=ps[b][:], func=AF.Copy), [R_ps[b]], [R_x[k]])
                    unbank(b)
                for l in range(nlayers):
                    if "f1" in stages:
                        ffn(l, 0)
                    else:
                        for n_, w_ in PIECES[0:17]:
                            ring_rel(ring_get(l, n_)[2])
                    if "mix" in stages:
                        mixer(l, i)
                    else:
                        for n_, w_ in PIECES[17:NPL - 17]:
                            ring_rel(ring_get(l, n_)[2])
                    if "f2" in stages:
                        ffn(l, 1)
                    else:
                        for n_, w_ in PIECES[NPL - 17:]:
                            ring_rel(ring_get(l, n_)[2])
                for tb in range(4):
                    for g in range(2):
                        b = bank()
                        for kk in range(4):
                            k = 4 * g + kk
                            P.op("pe", lambda e, b=b, tb=tb, k=k, kk=kk: e.transpose(out=ps[b][:, kk * 128:(kk + 1) * 128], in_=xres[:, k, tb * 128:(tb + 1) * 128], identity=ident[:]),
                                 [R_x[k], R_ident], [R_ps[b]])
                        P.op("act", lambda e, b=b, tb=tb, g=g: e.activation(out=xtok[:, tb, g * 512:(g + 1) * 512], in_=ps[b][:], func=AF.Copy), [R_ps[b]], R_tmp[0:8])
                        unbank(b)
                P.op("sp", lambda e, r0=r0: e.dma_start(out=y_d[r0:r0 + T, :].rearrange("(a p) c -> p a c", p=128), in_=xtok), R_tmp[0:8], [R_yd], dma_sem="yout")

    P_real = P
    P = Plan()
    run_tiles()
    for r_ in P.allregs.values():
        r_.w = None
        r_.readers = {}
    P = P_real
    DRY[0] = False
    bank_free[:] = list(range(8))
    tmp_rr[0] = 0
    for grp, members in cast_groups.items():
        fin = cast_fin[grp]
        for (l_, i_) in members:
            R_wscr[l_][i_].w = (grp, fin, "dma:" + grp)
    ring_fill()
    run_tiles()
    P.final_wait("sp", [R_yd])
    for e_ in COMPUTE:
        P.final_wait(e_, [R_yd])
    P.emit(nc)
    es.close()
    return nc


def host_layouts(inp):
    f = lambda a: np.asarray(a, dtype=np.float32)
    pv = np.zeros((L, 128, NPV), np.float32)
    lamR = np.zeros((L, 128, 3, 1024), np.float32)
    braw = np.zeros((L, 128, 2, 2, 512), np.float32)
    craw = np.zeros((L, 128, 2, 8, 128), np.float32)
    btoe = np.zeros((L, 128, 8, 5, 128), np.float32)
    kk = np.arange(128)[:, None, None]
    rel = np.arange(5)[None, :, None]
    qq = np.arange(128)[None, None, :]
    dist = 128 * rel + qq - kk
    idx = np.clip(dist, -128, 128) + 128
    dch = 2 * rel + (qq >= 64) - (kk >= 64)
    mask = ((dch >= 0) & (dch <= 8)).astype(np.float32).reshape(128, 640)
    mask01 = np.concatenate([mask, mask], axis=1)
    for l in range(L):
        fm = lambda v, n: f(v).reshape(n, 128).T
        pv[l, :, PV_N1:PV_N1 + 8] = fm(inp["ffn1_norm"][l], 8)
        pv[l, :, PV_NM:PV_NM + 8] = fm(inp["mix_norm"][l], 8)
        pv[l, :, PV_N2:PV_N2 + 8] = fm(inp["ffn2_norm"][l], 8)
        pv[l, :, PV_BG:PV_BG + 24] = fm(inp["b_gate"][l], 24)
        pv[l, :, PV_D:PV_D + 2] = fm(inp["s5_d"][l], 2)
        pv[l, :, PV_CB:PV_CB + 2] = fm(inp["conv_b_dw"][l], 2)
        pv[l, :, PV_LG:PV_LG + 2] = fm(inp["conv_ln_g"][l], 2)
        pv[l, :, PV_LB:PV_LB + 2] = fm(inp["conv_ln_b"][l], 2)
        cw = f(inp["conv_w_dw"][l])
        for c in range(2):
            pv[l, :, PV_CW + c * 31:PV_CW + c * 31 + 31] = cw[:, c * 128:(c + 1) * 128].T
        pv[l, :, PV_QG] = np.tile(f(inp["attn_q_gain"][l]), 2)
        pv[l, :, PV_KG] = np.tile(f(inp["attn_k_gain"][l]), 2)
        lre, lim, ldt = f(inp["s5_lambda_re"][l]), f(inp["s5_lambda_im"][l]), f(inp["s5_log_dt"][l])
        pv[l, :, PV_LR:PV_LR + 8] = lre.reshape(8, 128).T
        pv[l, :, PV_LI:PV_LI + 8] = lim.reshape(8, 128).T
        pv[l, :, PV_DT:PV_DT + 8] = np.repeat(ldt, 64).reshape(8, 128).T
        lamR[l, :, 0, :] = lre.reshape(1, 1024)
        lamR[l, :, 1, :] = lim.reshape(1, 1024)
        lamR[l, :, 2, :] = np.repeat(ldt, 64).reshape(1, 1024)
        bre, bim = f(inp["s5_b_re"][l]), f(inp["s5_b_im"][l])
        cre, cim = f(inp["s5_c_re"][l]), f(inp["s5_c_im"][l])
        for g in range(16):
            cc, gl = g // 8, g % 8
            braw[l, 16 * gl:16 * gl + 16, 0, cc, 64 * gl:64 * gl + 64] = bre[g].T
            braw[l, 16 * gl:16 * gl + 16, 1, cc, 64 * gl:64 * gl + 64] = bim[g].T
            j, g2 = g // 2, g % 2
            craw[l, 64 * g2:64 * g2 + 64, 0, j, 16 * gl:16 * gl + 16] = cre[g].T
            craw[l, 64 * g2:64 * g2 + 64, 1, j, 16 * gl:16 * gl + 16] = cim[g].T
        rb = f(inp["attn_rel_bias"][l])
        btoe[l] = np.transpose(rb[:, idx], (1, 0, 2, 3))
    return {"pv": pv, "lamR": lamR, "braw": braw.reshape(L, 128, 2048), "craw": craw.reshape(L, 128, 2048),
            "btoe": btoe.reshape(L, 128, 5120), "mask01": mask01, "ident": np.eye(128, dtype=np.float32)}


WKEYS = ["ffn1_w_up", "ffn2_w_up", "ffn1_w_down", "ffn2_w_down", "w_in", "s5_w_glu", "w_br_s5", "w_br_attn", "w_br_conv", "w_out"]


def run(inputs, n_cores, dbg=None):
    x = np.asarray(inputs["x"], dtype=np.float32)
    B, S, _ = x.shape
    nseq = B // n_cores
    nc = build_program(nseq, S, dbg)
    common = host_layouts(inputs)
    for k in WKEYS:
        common[k] = np.ascontiguousarray(np.asarray(inputs[k], dtype=np.float32))
    in_maps = []
    for c in range(n_cores):
        m = dict(common)
        m["x"] = np.ascontiguousarray(x[c * nseq:(c + 1) * nseq].reshape(nseq * S, D))
        in_maps.append(m)
    res = run_bass_kernel_spmd(nc, in_maps, core_ids=list(range(n_cores)))
    out = np.stack([np.asarray(r["y"]).reshape(nseq, S, D) for r in res.results], axis=0)
    return out.reshape(B, S, D).astype(np.float32)


def kernel(**inputs):
    return run(inputs, 8)
```

```python
import math
import numpy as np
from contextlib import ExitStack
import concourse.bass as bass
import concourse.mybir as mybir
from concourse.bass_utils import run_bass_kernel_spmd

F32 = mybir.dt.float32
BF16 = mybir.dt.bfloat16
AF = mybir.ActivationFunctionType
ALU = mybir.AluOpType

COMPUTE = ("pe", "act", "dve", "pool")
ALL_ENG = COMPUTE + ("sp",)


class Reg:
    __slots__ = ("name", "w", "readers", "excl")

    def __init__(self, name, excl=False):
        self.name = name
        self.w = None
        self.readers = {}
        self.excl = excl


class Op:
    __slots__ = ("eng", "fn", "waits", "tok", "needed", "dma")

    def __init__(self, eng, fn, waits, tok, dma):
        self.eng = eng
        self.fn = fn
        self.waits = waits
        self.tok = tok
        self.needed = False
        self.dma = dma


class _Rec:
    def __init__(self):
        self.calls = []

    def __getattr__(self, name):
        def f(*a, **k):
            self.calls.append((name, a, k))
        return f


class Plan:
    def __init__(self):
        self.ops = {e: [] for e in ALL_ENG}
        self.cnt = {e: 0 for e in COMPUTE}
        self.seen = {e: {} for e in ALL_ENG}
        self.dma_cnt = {}
        self.tokmap = {}
        self.allregs = {}
        self.ext = {}

    def _need(self, eng, waits, tok, same_ok):
        if tok is None:
            return
        key, val, teng = tok
        if teng == eng and same_ok and eng == "pe":
            return
        if self.seen[eng].get(key, 0) >= val:
            return
        if waits.get(key, 0) < val:
            waits[key] = val

    def op(self, eng, fn, reads=(), writes=(), dma_sem=None):
        waits = {}
        for r in reads:
            self.allregs[id(r)] = r
        for r in writes:
            self.allregs[id(r)] = r
        for r in reads:
            if r.excl:
                self._need(eng, waits, r.w, True)
                for k, (v, e) in r.readers.items():
                    self._need(eng, waits, (k, v, e), True)
            else:
                self._need(eng, waits, r.w, False)
        for w in writes:
            self._need(eng, waits, w.w, True)
            for k, (v, e) in w.readers.items():
                self._need(eng, waits, (k, v, e), True)
        if dma_sem is not None:
            val = self.dma_cnt.get(dma_sem, 0) + 16
            self.dma_cnt[dma_sem] = val
            tok = (dma_sem, val, "dma:" + dma_sem)
        else:
            self.cnt[eng] += 1
            tok = (eng, self.cnt[eng], eng)
        rec = _Rec()
        fn(rec)
        assert len(rec.calls) == 1
        o = Op(eng, rec.calls[0], waits, tok, dma_sem is not None)
        self.tokmap[(tok[0], tok[1])] = o
        for k, v in waits.items():
            self.seen[eng][k] = v
            t = self.tokmap.get((k, v))
            if t is not None:
                t.needed = True
        self.ops[eng].append(o)
        for r in reads:
            old = r.readers.get(tok[0])
            if old is None or old[0] < tok[1]:
                r.readers[tok[0]] = (tok[1], tok[2])
        for w in writes:
            w.w = tok
            w.readers = {}
        return o

    def final_wait(self, eng, regs):
        waits = {}
        for r in regs:
            self._need(eng, waits, r.w, True)
            for k, (v, e) in r.readers.items():
                self._need(eng, waits, (k, v, e), True)
        o = Op(eng, None, waits, None, False)
        for k, v in waits.items():
            self.seen[eng][k] = max(self.seen[eng].get(k, 0), v)
            t = self.tokmap.get((k, v))
            if t is not None:
                t.needed = True
        self.ops[eng].append(o)

    def barrier(self):
        regs = list(self.allregs.values())
        for e_ in ALL_ENG:
            self.final_wait(e_, regs)

    def emit(self, nc, M=3000):
        remap = {}
        nep = {}
        for e in COMPUTE:
            c = 0
            m = {}
            for o in self.ops[e]:
                if o.tok is not None and not o.dma and o.needed:
                    m[o.tok[1]] = (c // M, c % M + 1)
                    c += 1
            remap[e] = m
            nep[e] = (c + M - 1) // M
        with ExitStack() as es:
            sems = {}
            for e in COMPUTE:
                for k in range(max(1, nep[e])):
                    sems[(e, k)] = es.enter_context(nc.semaphore("s_%s%d" % (e, k)))
            for k, tot in self.dma_cnt.items():
                if k in self.ext:
                    sems[(k, 0)] = self.ext[k]
                    continue
                n = (tot // 16 + M - 1) // M
                for j in range(max(1, n)):
                    sems[(k, j)] = es.enter_context(nc.semaphore("d_%s_%d" % (k, j)))
            self.nsems = len(sems)
            block = es.enter_context(nc.Block())
            plan = self

            def sv(k, v):
                if k in remap:
                    ep, val = remap[k][v]
                    return sems[(k, ep)], val
                n = v // 16 - 1
                return sems[(k, n // M)], (n % M + 1) * 16

            def run(engname):
                def body(eng):
                    for o in plan.ops[engname]:
                        for k, v in o.waits.items():
                            s_, v_ = sv(k, v)
                            eng.wait_ge(s_, v_)
                        if o.fn is None:
                            continue
                        name_, a_, k_ = o.fn
                        ins = getattr(eng, name_)(*a_, **k_)
                        if o.dma:
                            s_, _ = sv(o.tok[0], o.tok[1])
                            ins.then_inc(s_, 16)
                        elif o.needed:
                            s_, _ = sv(o.tok[0], o.tok[1])
                            ins.then_inc(s_, 1)
                return body

            block.tensor(run("pe"))
            block.scalar(run("act"))
            block.vector(run("dve"))
            block.gpsimd(run("pool"))
            block.sync(run("sp"))


D = 1024
KC = 8
DFF = 2816
HC = 22
T = 512
L = 2
EPS = 1e-6
IN_COLS = 5376
NPV = 144
BIAS_PE = False
SLOTW = 4096
PV_N1, PV_NM, PV_N2, PV_BG, PV_D, PV_CB, PV_LG, PV_LB, PV_CW, PV_QG, PV_KG, PV_LR, PV_LI, PV_DT = (
    0, 8, 16, 24, 48, 50, 52, 54, 56, 118, 119, 120, 128, 136)


def layer_pieces():
    p = []
    for f in (1, 2):
        pass
    ffn = lambda f: [("f%du%d" % (f, j), 4096) for j in range(11)] + \
        [("f%dd%d" % (f, j), 4096 if j < 5 else 2048) for j in range(6)]
    mix = [("inA", 4096), ("cw0", 3968), ("cw1", 3968), ("inQ", 4096), ("inK", 4096), ("inV", 4096), ("inU", 2048),
           ("bbar", 2048), ("cmat", 2048)]
    for hc in range(4):
        mix += [("me%d" % hc, 1280), ("tab%d" % (2 * hc), 2048), ("tab%d" % (2 * hc + 1), 2048)]
    mix += [("glu", 1024), ("brs", 4096), ("bra", 4096)] + [("g%d" % j, 3072) for j in range(8)] + \
           [("wo0", 4096), ("wo1", 4096)]
    return ffn(1) + mix + ffn(2)


PIECES = layer_pieces()
PIDX = {n: i for i, (n, w) in enumerate(PIECES)}
NPL = len(PIECES)


def build_program(NSEQ, S, dbg=None):
    dbg = dbg or {}
    NT = S // T
    nlayers = dbg.get("layers", L)
    nc = bass.Bass("TRN2", target_bir_lowering=False)
    dram_in = lambda n, s, d=F32: nc.dram_tensor(n, s, d, kind="ExternalInput").ap()
    x_d = dram_in("x", [NSEQ * S, D])
    y_d = nc.dram_tensor("y", [NSEQ * S, D], F32, kind="ExternalOutput").ap()
    w_up = [dram_in("ffn1_w_up", [L, D, 2 * DFF]), dram_in("ffn2_w_up", [L, D, 2 * DFF])]
    w_dn = [dram_in("ffn1_w_down", [L, DFF, D]), dram_in("ffn2_w_down", [L, DFF, D])]
    w_in = dram_in("w_in", [L, D, IN_COLS])
    w_glu = dram_in("s5_w_glu", [L, 256, 512])
    w_brs = dram_in("w_br_s5", [L, 256, D])
    w_bra = dram_in("w_br_attn", [L, 512, D])
    w_brc = dram_in("w_br_conv", [L, 256, D])
    w_out = dram_in("w_out", [L, D, D])
    pv_d = dram_in("pv", [L, 128, NPV])
    lamR_d = dram_in("lamR", [L, 128, 3, 1024])
    braw_d = dram_in("braw", [L, 128, 2048])
    craw_d = dram_in("craw", [L, 128, 2048])
    btoe_d = dram_in("btoe", [L, 128, 5120])
    mask_d = dram_in("mask01", [128, 1280])
    ident_d = dram_in("ident", [128, 128])
    wscr = nc.dram_tensor("wscr", [L * NPL, 128, SLOTW], BF16, kind="Internal").ap()
    tscr = nc.dram_tensor("tscr", [L * 8, 128, 1024], F32, kind="Internal").ap()

    es = ExitStack()
    sb = lambda n, s, d: es.enter_context(nc.sbuf_tensor("sb_" + n, s, d))
    ident = sb("ident", [128, 128], F32)
    ones_bf = sb("ones_bf", [128, 128], BF16)
    bones_bf = sb("bones_bf", [128, 128], BF16)
    identb = sb("identb", [128, 128], BF16)
    PV = [sb("pv%d" % l, [128, NPV], F32) for l in range(L)]
    s5c = [sb("s5c%d" % l, [128, 5, 8], F32) for l in range(L)]
    R_ident, R_const = Reg("ident"), Reg("const")
    R_pv = [Reg("pv%d" % l) for l in range(L)]
    R_s5c = [Reg("s5c%d" % l) for l in range(L)]
    R_init = [[Reg("init%d_%d" % (l, j)) for j in range(8)] for l in range(L)]
    R_wscr = [[Reg("wscr%d_%d" % (l, i)) for i in range(NPL)] for l in range(L)]

    def scr(l, name, width=None):
        i = PIDX[name]
        w = width or PIECES[i][1]
        return wscr[l * NPL + i, :, 0:w]

    def issue_casts(P):
        cast_groups = {}

        def cast(l, name, dst_sl, src, grp):
            i = PIDX[name]
            P.op("pool", lambda e: e.dma_start(out=dst_sl, in_=src), [], [], dma_sem=grp)
            cast_groups.setdefault(grp, set()).add((l, i))

        def pc(l, name, nk, width):
            return scr(l, name, nk * width).rearrange("p (k c) -> p k c", k=nk)

        for l in range(nlayers):
            for f in range(2):
                g = "w%d_%d" % (l, f * 2)
                for j in range(11):
                    d = pc(l, "f%du%d" % (f + 1, j), 8, 512)
                    for half in range(2):
                        src = w_up[f][l, :, half * DFF + 256 * j: half * DFF + 256 * j + 256].rearrange("(k p) c -> p k c", p=128)
                        cast(l, "f%du%d" % (f + 1, j), d[:, :, half * 256:(half + 1) * 256], src, g)
                for j in range(6):
                    nk = 4 if j < 5 else 2
                    d = pc(l, "f%dd%d" % (f + 1, j), nk, 1024)
                    src = w_dn[f][l, 512 * j: 512 * j + 128 * nk, :].rearrange("(k p) c -> p k c", p=128)
                    cast(l, "f%dd%d" % (f + 1, j), d, src, g)
                if f == 0:
                    g = "w%d_1" % l
                    wi = lambda c0, n_: w_in[l, :, c0:c0 + n_].rearrange("(k p) c -> p k c", p=128)
                    cast(l, "inA", pc(l, "inA", 8, 512), wi(1792, 512), g)
                    cast(l, "inQ", pc(l, "inQ", 8, 512), wi(256, 512), g)
                    cast(l, "inK", pc(l, "inK", 8, 512), wi(768, 512), g)
                    cast(l, "inV", pc(l, "inV", 8, 512), wi(1280, 512), g)
                    cast(l, "inU", pc(l, "inU", 8, 256), wi(0, 256), g)
                    cast(l, "glu", pc(l, "glu", 2, 512), w_glu[l, :, :].rearrange("(k p) c -> p k c", p=128), g)
                    d = pc(l, "brs", 4, 1024)
                    cast(l, "brs", d[:, 0:2, :], w_brs[l, :, :].rearrange("(k p) c -> p k c", p=128), g)
                    cast(l, "brs", d[:, 2:4, :], w_brc[l, :, :].rearrange("(k p) c -> p k c", p=128), g)
                    cast(l, "bra", pc(l, "bra", 4, 1024), w_bra[l, :, :].rearrange("(k p) c -> p k c", p=128), g)
                    for m in range(8):
                        d = pc(l, "g%d" % m, 8, 384)
                        for b in range(3):
                            cast(l, "g%d" % m, d[:, :, b * 128:(b + 1) * 128], wi(2304 + 1024 * b + 128 * m, 128), g)
                    for j in range(2):
                        cast(l, "wo%d" % j, pc(l, "wo%d" % j, 8, 512),
                             w_out[l, :, 512 * j:512 * j + 512].rearrange("(k p) c -> p k c", p=128), g)


        return cast_groups

    PI = math.pi
    P = Plan()
    for l in range(-1, nlayers):
        with ExitStack() as ss:
            st = lambda n, s, d=F32, l=l: ss.enter_context(nc.sbuf_tensor("st%d_%s" % (l + 1, n), s, d))
            regs_all = []

            def RG(n):
                r = Reg(n)
                regs_all.append(r)
                return r
            if l < 0:
                P.op("sp", lambda e: e.dma_start(out=ident[:], in_=ident_d[:, :]), [], [R_ident], dma_sem="c0")
                P.op("pool", lambda e: e.memset(ones_bf[:], 1.0), [], [R_const])
                P.op("pool", lambda e: e.memset(bones_bf[:], 0.0), [], [R_const])
                P.op("pool", lambda e: e.memset(bones_bf[0:64, 0:64], 1.0), [], [R_const])
                P.op("pool", lambda e: e.memset(bones_bf[64:128, 64:128], 1.0), [], [R_const])
                P.op("dve", lambda e: e.tensor_copy(out=identb[:], in_=ident[:]), [R_ident], [R_const])
                cast_groups = issue_casts(P)
                cast_fin = {g_: P.dma_cnt[g_] for g_ in cast_groups}
                continue
            maskt = st("maskt", [128, 1280])
            R_mask = RG("mask")
            P.op("sp", lambda e: e.dma_start(out=maskt[:], in_=mask_d[:, :]), [], [R_mask], dma_sem="c1")

            def s5_params(lr, li, ldt, shape, pfx, R):
                tl = {}

                def t(n):
                    tl[n] = st("%s_%s" % (pfx, n), shape)
                    return tl[n]
                V = lambda fn: P.op("dve", fn, [R], [R])
                A = lambda fn: P.op("act", fn, [R], [R])
                lrc, dt, a, mag, th = t("lrc"), t("dt"), t("a"), t("mag"), t("th")
                V(lambda e: e.tensor_scalar(out=lrc[:], in0=lr, scalar1=-1e-4, scalar2=None, op0=ALU.min))
                A(lambda e: e.activation(out=dt[:], in_=ldt, func=AF.Exp))
                V(lambda e: e.tensor_tensor(out=a[:], in0=lrc[:], in1=dt[:], op=ALU.mult))
                A(lambda e: e.activation(out=mag[:], in_=a[:], func=AF.Exp))
                V(lambda e: e.tensor_tensor(out=th[:], in0=li, in1=dt[:], op=ALU.mult))
                ths, thc0, thc, m = t("ths"), t("thc0"), t("thc"), a
                V(lambda e: e.tensor_copy(out=ths[:], in_=th[:]))
                V(lambda e: e.tensor_scalar(out=thc0[:], in0=th[:], scalar1=PI / 2, scalar2=None, op0=ALU.add))
                V(lambda e: e.tensor_copy(out=thc[:], in_=thc0[:]))
                for kk in range(5):
                    thr = (2 * kk + 1) * PI
                    V(lambda e: e.tensor_scalar(out=m[:], in0=th[:], scalar1=thr, scalar2=-2 * PI, op0=ALU.is_gt, op1=ALU.mult))
                    V(lambda e: e.tensor_tensor(out=ths[:], in0=ths[:], in1=m[:], op=ALU.add))
                    V(lambda e: e.tensor_scalar(out=m[:], in0=thc0[:], scalar1=thr, scalar2=-2 * PI, op0=ALU.is_gt, op1=ALU.mult))
                    V(lambda e: e.tensor_tensor(out=thc[:], in0=thc[:], in1=m[:], op=ALU.add))
                sn, cs = ths, thc
                A(lambda e: e.activation(out=sn[:], in_=ths[:], func=AF.Sin))
                A(lambda e: e.activation(out=cs[:], in_=thc[:], func=AF.Sin))
                n2, t2 = thc0, th
                V(lambda e: e.tensor_tensor(out=n2[:], in0=sn[:], in1=sn[:], op=ALU.mult))
                V(lambda e: e.tensor_tensor(out=t2[:], in0=cs[:], in1=cs[:], op=ALU.mult))
                V(lambda e: e.tensor_tensor(out=n2[:], in0=n2[:], in1=t2[:], op=ALU.add))
                A(lambda e: e.activation(out=n2[:], in_=n2[:], func=AF.Sqrt))
                V(lambda e: e.reciprocal(out=n2[:], in_=n2[:]))
                V(lambda e: e.tensor_tensor(out=sn[:], in0=sn[:], in1=n2[:], op=ALU.mult))
                V(lambda e: e.tensor_tensor(out=cs[:], in0=cs[:], in1=n2[:], op=ALU.mult))
                return {"lrc": lrc, "mag": mag, "sn": sn, "cs": cs, "f1": n2, "f2": t2, "f3": a, "f4": dt}

            P.op("sp", lambda e: e.dma_start(out=PV[l][:], in_=pv_d[l, :, :]), [], [R_pv[l]], dma_sem="c2")
            RS = RG("s5S")
            P.op("dve", lambda e: e.tensor_copy(out=s5c[l][:, 0, :], in_=PV[l][:, PV_LR:PV_LR + 8]), [R_pv[l]], [RS])
            tS = s5_params(PV[l][:, PV_LR:PV_LR + 8], PV[l][:, PV_LI:PV_LI + 8], PV[l][:, PV_DT:PV_DT + 8], [128, 8], "S", RS)
            V = lambda fn: P.op("dve", fn, [RS], [RS])
            V(lambda e: e.tensor_copy(out=s5c[l][:, 0, :], in_=tS["mag"][:]))
            Ct = st("Ctab", [128, 8, T])
            St = st("Stab", [128, 8, T])
            cn = st("cn", [128, 8, 1])
            sn_ = st("snn", [128, 8, 1])
            tA = st("tA", [128, 8, 256])
            tB = st("tB", [128, 8, 256])
            V(lambda e: e.memset(Ct[:, :, 0:1], 1.0))
            V(lambda e: e.memset(St[:, :, 0:1], 0.0))
            V(lambda e: e.tensor_copy(out=cn[:, :, 0], in_=tS["cs"][:]))
            V(lambda e: e.tensor_copy(out=sn_[:, :, 0], in_=tS["sn"][:]))
            n = 1
            while n < T:
                cb = cn[:, :, 0:1].to_broadcast([128, 8, n])
                sbb = sn_[:, :, 0:1].to_broadcast([128, 8, n])
                V(lambda e: e.tensor_tensor(out=tA[:, :, 0:n], in0=Ct[:, :, 0:n], in1=cb, op=ALU.mult))
                V(lambda e: e.tensor_tensor(out=tB[:, :, 0:n], in0=St[:, :, 0:n], in1=sbb, op=ALU.mult))
                V(lambda e: e.tensor_tensor(out=Ct[:, :, n:2 * n], in0=tA[:, :, 0:n], in1=tB[:, :, 0:n], op=ALU.subtract))
                V(lambda e: e.tensor_tensor(out=tA[:, :, 0:n], in0=St[:, :, 0:n], in1=cb, op=ALU.mult))
                V(lambda e: e.tensor_tensor(out=tB[:, :, 0:n], in0=Ct[:, :, 0:n], in1=sbb, op=ALU.mult))
                V(lambda e: e.tensor_tensor(out=St[:, :, n:2 * n], in0=tA[:, :, 0:n], in1=tB[:, :, 0:n], op=ALU.add))
                V(lambda e: e.tensor_tensor(out=tA[:, :, 0:1], in0=cn[:], in1=cn[:], op=ALU.mult))
                V(lambda e: e.tensor_tensor(out=tB[:, :, 0:1], in0=sn_[:], in1=sn_[:], op=ALU.mult))
                V(lambda e: e.tensor_tensor(out=tB[:, :, 1:2], in0=cn[:], in1=sn_[:], op=ALU.mult))
                V(lambda e: e.tensor_tensor(out=cn[:], in0=tA[:, :, 0:1], in1=tB[:, :, 0:1], op=ALU.subtract))
                V(lambda e: e.tensor_scalar(out=sn_[:], in0=tB[:, :, 1:2], scalar1=2.0, scalar2=None, op0=ALU.mult))
                n *= 2
            V(lambda e: e.tensor_copy(out=s5c[l][:, 2, :], in_=sn_[:, :, 0]))
            P.op("dve", lambda e: e.tensor_copy(out=s5c[l][:, 1, :], in_=cn[:, :, 0]), [RS], [RS, R_s5c[l]])
            for j in range(8):
                P.op("sp", lambda e: e.dma_start(out=tscr[l * 8 + j, :, 0:T], in_=Ct[:, j, :]),
                     [RS], [R_wscr[l][PIDX["tab%d" % j]]], dma_sem="tst")
                P.op("sp", lambda e: e.dma_start(out=tscr[l * 8 + j, :, T:2 * T], in_=St[:, j, :]),
                     [RS], [R_wscr[l][PIDX["tab%d" % j]]], dma_sem="tst")
            RR = RG("s5R")
            lam = st("lam", [128, 3, 1024])
            P.op("sp", lambda e: e.dma_start(out=lam[:], in_=lamR_d[l, :, :, :]), [], [RR], dma_sem="c3")
            braw = st("braw", [128, 2, 1024])
            P.op("sp", lambda e: e.dma_start(out=braw[:].rearrange("p a b -> p (a b)"), in_=braw_d[l, :, :]), [], [RR], dma_sem="c4")
            craw = st("craw", [128, 2, 1024])
            P.op("sp", lambda e: e.dma_start(out=craw[:].rearrange("p a b -> p (a b)"), in_=craw_d[l, :, :]), [], [RR], dma_sem="c5")
            bb = st("bb", [128, 2, 1024], BF16)
            V = lambda fn: P.op("dve", fn, [RR], [RR])
            for cc in range(2):
                c0 = cc * 512
                lrA, liA, dtA = lam[:, 0, c0:c0 + 512], lam[:, 1, c0:c0 + 512], lam[:, 2, c0:c0 + 512]
                tR = s5_params(lrA, liA, dtA, [128, 512], "R%d" % cc, RR)
                ar, ai, den, cr, ci, u1 = [st("%s%d" % (n_, cc), [128, 512]) for n_ in ("ar", "ai", "den", "cr", "ci", "u1")]
                lrc = tR["lrc"]
                V(lambda e: e.tensor_tensor(out=ar[:], in0=tR["mag"][:], in1=tR["cs"][:], op=ALU.mult))
                V(lambda e: e.tensor_tensor(out=ai[:], in0=tR["mag"][:], in1=tR["sn"][:], op=ALU.mult))
                V(lambda e: e.tensor_tensor(out=den[:], in0=lrc[:], in1=lrc[:], op=ALU.mult))
                V(lambda e: e.tensor_tensor(out=u1[:], in0=liA, in1=liA, op=ALU.mult))
                V(lambda e: e.tensor_tensor(out=den[:], in0=den[:], in1=u1[:], op=ALU.add))
                V(lambda e: e.reciprocal(out=den[:], in_=den[:]))
                V(lambda e: e.tensor_scalar(out=ar[:], in0=ar[:], scalar1=-1.0, scalar2=None, op0=ALU.add))
                V(lambda e: e.tensor_tensor(out=cr[:], in0=ar[:], in1=lrc[:], op=ALU.mult))
                V(lambda e: e.tensor_tensor(out=u1[:], in0=ai[:], in1=liA, op=ALU.mult))
                V(lambda e: e.tensor_tensor(out=cr[:], in0=cr[:], in1=u1[:], op=ALU.add))
                V(lambda e: e.tensor_tensor(out=cr[:], in0=cr[:], in1=den[:], op=ALU.mult))
                V(lambda e: e.tensor_tensor(out=ci[:], in0=ai[:], in1=lrc[:], op=ALU.mult))
                V(lambda e: e.tensor_tensor(out=u1[:], in0=ar[:], in1=liA, op=ALU.mult))
                V(lambda e: e.tensor_tensor(out=ci[:], in0=ci[:], in1=u1[:], op=ALU.subtract))
                V(lambda e: e.tensor_tensor(out=ci[:], in0=ci[:], in1=den[:], op=ALU.mult))
                bre, bim = braw[:, 0, c0:c0 + 512], braw[:, 1, c0:c0 + 512]
                V(lambda e: e.tensor_tensor(out=u1[:], in0=cr[:], in1=bre, op=ALU.mult))
                V(lambda e: e.tensor_tensor(out=den[:], in0=ci[:], in1=bim, op=ALU.mult))
                V(lambda e: e.tensor_tensor(out=bb[:, 0, c0:c0 + 512], in0=u1[:], in1=den[:], op=ALU.subtract))
                V(lambda e: e.tensor_tensor(out=u1[:], in0=cr[:], in1=bim, op=ALU.mult))
                V(lambda e: e.tensor_tensor(out=den[:], in0=ci[:], in1=bre, op=ALU.mult))
                V(lambda e: e.tensor_tensor(out=bb[:, 1, c0:c0 + 512], in0=u1[:], in1=den[:], op=ALU.add))
            P.op("sp", lambda e: e.dma_start(out=scr(l, "bbar"), in_=bb[:].rearrange("p a b -> p (a b)")),
                 [RR], [R_wscr[l][PIDX["bbar"]]], dma_sem="tst")
            cb_ = st("cb", [128, 2, 1024], BF16)
            V(lambda e: e.tensor_copy(out=cb_[:, 0, :], in_=craw[:, 0, :]))
            V(lambda e: e.tensor_scalar(out=cb_[:, 1, :], in0=craw[:, 1, :], scalar1=-1.0, scalar2=None, op0=ALU.mult))
            P.op("sp", lambda e: e.dma_start(out=scr(l, "cmat"), in_=cb_[:].rearrange("p a b -> p (a b)")),
                 [RR], [R_wscr[l][PIDX["cmat"]]], dma_sem="tst")
            RM = RG("me")
            bt = st("bt", [128, 8, 640])
            mb = st("mb", [128, 8, 640], BF16)
            P.op("sp", lambda e: e.dma_start(out=bt[:].rearrange("p a b -> p (a b)"), in_=btoe_d[l, :, :]), [], [RM], dma_sem="c6")
            if BIAS_PE:
                negm = st("negm", [128, 640])
                P.op("dve", lambda e: e.tensor_scalar(out=negm[:], in0=maskt[:, 0:640], scalar1=-1.0, scalar2=30000.0, op0=ALU.add, op1=ALU.mult), [R_mask], [RM])
                P.op("dve", lambda e: e.scalar_tensor_tensor(out=bt[:], in0=bt[:], scalar=8.0, in1=maskt[:, 0:640].unsqueeze(1).to_broadcast([128, 8, 640]),
                                                             op0=ALU.mult, op1=ALU.mult), [RM, R_mask], [RM])
                P.op("dve", lambda e: e.tensor_tensor(out=mb[:], in0=bt[:], in1=negm[:].unsqueeze(1).to_broadcast([128, 8, 640]), op=ALU.add), [RM], [RM])
            else:
                P.op("act", lambda e: e.activation(out=bt[:], in_=bt[:], func=AF.Exp), [RM], [RM])
                P.op("dve", lambda e: e.tensor_tensor(out=mb[:], in0=bt[:], in1=maskt[:, 0:640].unsqueeze(1).to_broadcast([128, 8, 640]), op=ALU.mult), [RM, R_mask], [RM])
            for hc in range(4):
                P.op("sp", lambda e: e.dma_start(out=scr(l, "me%d" % hc), in_=mb[:, 2 * hc:2 * hc + 2, :].rearrange("p a b -> p (a b)")),
                     [RM], [R_wscr[l][PIDX["me%d" % hc]]], dma_sem="tst")
            RD = RG("cwd")
            for c in range(2):
                dg = st("dg%d" % c, [128, 31, 128], BF16)
                for k in range(31):
                    P.op("dve", lambda e: e.tensor_scalar(out=dg[:, k, :], in0=ident[:], scalar1=PV[l][:, PV_CW + c * 31 + k:PV_CW + c * 31 + k + 1],
                                                                                scalar2=None, op0=ALU.mult), [R_pv[l], R_ident], [RD])
                P.op("sp", lambda e: e.dma_start(out=scr(l, "cw%d" % c), in_=dg[:].rearrange("p a b -> p (a b)")),
                     [RD], [R_wscr[l][PIDX["cw%d" % c]]], dma_sem="tst")
            P.barrier()

    xres = sb("xres", [128, KC, T], F32)
    hT = sb("hT", [128, KC, T], BF16)
    hid = sb("hid", [128, HC, T], BF16)
    sq = sb("sq", [128, KC, T], BF16)
    NTMP = 10
    tmpall = sb("tmpall", [128, NTMP, T], F32)
    du = sb("du", [128, 2, T], F32)
    cv = sb("cv", [128, 2, T], F32)
    NSLOT = 6
    slots = [sb("slot%d" % i, [128, SLOTW], BF16) for i in range(NSLOT)]
    kbuf = [sb("kbuf%d" % l, [128, 4, 2, T], BF16) for l in range(L)]
    vbuf = [sb("vbuf%d" % l, [128, 8, 512], BF16) for l in range(L)]
    hbuf = [sb("hbuf%d" % l, [128, 2, 32 + T], BF16) for l in range(L)]
    ps = [es.enter_context(nc.psum_tensor("ps%d" % i, [128, T], F32)) for i in range(8)]
    R_x = [Reg("x%d" % k) for k in range(KC)]
    R_h = [Reg("h%d" % k) for k in range(KC)]
    R_hid = [Reg("hid%d" % k) for k in range(HC)]
    R_sq = [Reg("sq%d" % k) for k in range(KC)]
    R_tmp = [Reg("tmp%d" % k) for k in range(NTMP)]
    R_du, R_cv = Reg("du"), [Reg("cv0"), Reg("cv1")]
    R_slot = [Reg("slot%d" % i) for i in range(NSLOT)]
    R_k = [[Reg("k%d_%d" % (l, h)) for h in range(2)] for l in range(L)]
    R_v = [[Reg("v%d_%d" % (l, h)) for h in range(2)] for l in range(L)]
    R_hb = [[Reg("hb%d_%d" % (l, c)) for c in range(2)] for l in range(L)]
    R_ps = [Reg("ps%d" % i, excl=True) for i in range(8)]
    R_xd, R_yd = Reg("xd"), Reg("yd")

    bank_free = list(range(8))

    cur_pool = [bank_free]

    def bank():
        return cur_pool[0].pop(0)

    def unbank(b):
        cur_pool[0].append(b)

    tmp_rr = [0]

    def tmp():
        i = tmp_rr[0] % NTMP
        tmp_rr[0] += 1
        return tmpall[:, i, :], R_tmp[i]

    order = []
    DRY = [True]
    ring = {"next_load": 0, "next_use": 0, "free": list(range(NSLOT)), "where": {}}

    def ring_fill():
        if DRY[0]:
            return
        while ring["free"] and ring["next_load"] < len(order):
            idx = ring["next_load"]
            l, n_ = order[idx]
            sl = ring["free"].pop(0)
            i = PIDX[n_]
            w_ = PIECES[i][1]
            if n_.startswith("tab"):
                j = int(n_[3:])
                src = tscr[l * 8 + j, :, :]
                dst = slots[sl][:, 0:2048].bitcast(F32)
            else:
                src = scr(l, n_)
                dst = slots[sl][:, 0:w_]
            P.op("sp", lambda e, dst=dst, src=src: e.dma_start(out=dst, in_=src), [R_wscr[l][i]], [R_slot[sl]], dma_sem="ring%d" % sl)
            ring["where"][idx] = sl
            ring["next_load"] += 1

    def ring_get(l, name):
        if DRY[0]:
            order.append((l, name))
            return slots[0], R_slot[0], 0
        idx = ring["next_use"]
        assert order[idx] == (l, name), (order[idx], l, name)
        assert idx in ring["where"], "ring deadlock at %s" % name
        ring["next_use"] += 1
        sl = ring["where"][idx]
        return slots[sl], R_slot[sl], sl

    def ring_rel(sl):
        if DRY[0]:
            return
        ring["free"].append(sl)
        ring_fill()

    def mm(out, lhsT, rhs, start, stop, reads, writes, **kw):
        P.op("pe", lambda e: e.matmul(out, lhsT=lhsT, rhs=rhs, start=start, stop=stop, **kw), reads, writes)

    def rmsnorm(l, col):
        for k in range(KC):
            if k % 2 == 1:
                P.op("act", lambda e: e.activation(out=sq[:, k, :], in_=xres[:, k, :], func=AF.Square), [R_x[k]], [R_sq[k]])
            else:
                P.op("pool", lambda e: e.tensor_tensor(out=sq[:, k, :], in0=xres[:, k, :], in1=xres[:, k, :], op=ALU.mult), [R_x[k]], [R_sq[k]])
        b = bank()
        for k in range(KC):
            mm(ps[b][:], ones_bf[:], sq[:, k, :], k == 0, k == KC - 1, [R_sq[k]], [R_ps[b]])
        rs, rr = tmp()
        P.op("act", lambda e: e.activation(out=rs, in_=ps[b][:], func=AF.Ln, bias=EPS, scale=1.0 / D), [R_ps[b]], [rr])
        unbank(b)
        P.op("act", lambda e: e.activation(out=rs, in_=rs, func=AF.Exp, scale=-0.5), [rr], [rr])
        for k in range(KC):
            P.op("dve", lambda e, k=k: e.scalar_tensor_tensor(out=hT[:, k, :], in0=xres[:, k, :], scalar=PV[l][:, col + k:col + k + 1],
                                                              in1=rs, op0=ALU.mult, op1=ALU.mult), [R_x[k], rr], [R_h[k]])

    def ffn(l, f):
        rmsnorm(l, PV_N1 if f == 0 else PV_N2)
        for j in range(11):
            sl, rs_, si = ring_get(l, "f%du%d" % (f + 1, j))
            if j == 0:
                bk = {(half, ab): bank() for half in range(2) for ab in range(2)}
                for k in range(KC):
                    for half in range(2):
                        for ab in range(2):
                            mm(ps[bk[(half, ab)]][:], sl[:, k * 512 + ab * 256 + half * 128: k * 512 + ab * 256 + half * 128 + 128], hT[:, k, :],
                               k == 0, k == KC - 1, [rs_, R_h[k]], [R_ps[bk[(half, ab)]]])
                for half in range(2):
                    m = half
                    bA, bB = bk[(half, 0)], bk[(half, 1)]
                    ta, ra = tmp()
                    P.op("act", lambda e: e.activation(out=ta, in_=ps[bA][:], func=AF.Silu), [R_ps[bA]], [ra])
                    unbank(bA)
                    P.op("dve", lambda e: e.tensor_tensor(out=hid[:, m, :], in0=ps[bB][:], in1=ta, op=ALU.mult), [R_ps[bB], ra], [R_hid[m]])
                    unbank(bB)
                ring_rel(si)
                continue
            for half in range(2):
                m = 2 * j + half
                bA, bB = bank(), bank()
                for k in range(KC):
                    mm(ps[bA][:], sl[:, k * 512 + half * 128: k * 512 + half * 128 + 128], hT[:, k, :], k == 0, k == KC - 1,
                       [rs_, R_h[k]], [R_ps[bA]])
                for k in range(KC):
                    mm(ps[bB][:], sl[:, k * 512 + 256 + half * 128: k * 512 + 256 + half * 128 + 128], hT[:, k, :], k == 0, k == KC - 1,
                       [rs_, R_h[k]], [R_ps[bB]])
                ta, ra = tmp()
                P.op("act", lambda e, bA=bA, ta=ta: e.activation(out=ta, in_=ps[bA][:], func=AF.Silu), [R_ps[bA]], [ra])
                unbank(bA)
                P.op("dve", lambda e, bB=bB, ta=ta, m=m: e.tensor_tensor(out=hid[:, m, :], in0=ps[bB][:], in1=ta, op=ALU.mult),
                     [R_ps[bB], ra], [R_hid[m]])
                unbank(bB)
            ring_rel(si)
        bd = [bank() for _ in range(KC)]
        for j in range(6):
            nk = 4 if j < 5 else 2
            sl, rs_, si = ring_get(l, "f%dd%d" % (f + 1, j))
            for m in range(KC):
                for kk in range(nk):
                    k = 4 * j + kk
                    mm(ps[bd[m]][:], sl[:, kk * 1024 + m * 128: kk * 1024 + m * 128 + 128], hid[:, k, :], k == 0, k == HC - 1,
                       [rs_, R_hid[k]], [R_ps[bd[m]]])
            ring_rel(si)
        for m in range(KC):
            P.op("dve", lambda e, m=m: e.scalar_tensor_tensor(out=xres[:, m, :], in0=ps[bd[m]][:], scalar=0.5, in1=xres[:, m, :],
                                                              op0=ALU.mult, op1=ALU.add), [R_ps[bd[m]], R_x[m]], [R_x[m]])
            unbank(bd[m])

    MERG, QN, UT, ATT, S5O, CVO = 0, 8, 12, 14, 18, 20
    branches = dbg.get("branches", (0, 1, 2))
    NS5T = 12
    s5tmp = sb("s5tmp", [128, NS5T, T], F32)
    R_s5t = [Reg("s5t%d" % k) for k in range(NS5T)]
    ebuf = sb("ebuf", [128, 4, T], BF16)
    R_e = [Reg("e%d" % k) for k in range(4)]

    def mixer(l, i):
        half, hhalf = i % 2, 1 - i % 2
        pvl = PV[l]
        rmsnorm(l, PV_NM)
        hb = hbuf[l]
        sl, rs_, si = ring_get(l, "inU")
        for c in range(2):
            b = bank()
            for k in range(KC):
                mm(ps[b][:], sl[:, k * 256 + c * 128: k * 256 + c * 128 + 128], hT[:, k, :], k == 0, k == KC - 1, [rs_, R_h[k]], [R_ps[b]])
            P.op("act", lambda e: e.activation(out=hid[:, UT + c, :], in_=ps[b][:], func=AF.Copy), [R_ps[b]], [R_hid[UT + c]])
            P.op("act", lambda e: e.activation(out=du[:, c, :], in_=ps[b][:], func=AF.Copy, scale=pvl[:, PV_D + c:PV_D + c + 1]), [R_ps[b]], [R_du])
            unbank(b)
        ring_rel(si)

        def attention_hc(hc, slM, rM):
            bnum, bden = bank(), bank()
            steps = []
            for hl in range(2):
                for kbi in range(8):
                    if 4 * i - 4 + kbi >= 0:
                        steps.append((hl, kbi))
            firsts = {0: True, 1: True}
            info = {}
            for n in range(len(steps) + 3):
                if n < len(steps):
                    hl, kbi = steps[n]
                    p0 = 64 * hl
                    gkb = 4 * i - 4 + kbi
                    khalf = half if kbi >= 4 else hhalf
                    kcol = (kbi % 4) * 128
                    q_lo, q_hi = max(0, kbi - 4), min(3, kbi)
                    nq = q_hi - q_lo + 1
                    rel_lo = 4 * i + q_lo - gkb
                    bs = bank()
                    me = slM[:, hl * 640 + rel_lo * 128: hl * 640 + (rel_lo + nq) * 128]
                    ei = n % 4
                    mm(ps[bs][:, 0:nq * 128], kbuf[l][p0:p0 + 64, hc, khalf, kcol:kcol + 128], hid[p0:p0 + 64, QN + hc, q_lo * 128:(q_hi + 1) * 128],
                       True, True, [R_k[l][khalf], R_hid[QN + hc]], [R_ps[bs]])
                    P.op("act", lambda e: e.activation(out=ebuf[:, ei, 0:nq * 128], in_=ps[bs][:, 0:nq * 128], func=AF.Exp, scale=0.125), [R_ps[bs]], [R_e[ei]])
                    P.op("pool", lambda e: e.tensor_tensor(out=ebuf[:, ei, 0:nq * 128], in0=ebuf[:, ei, 0:nq * 128], in1=me, op=ALU.mult), [R_e[ei], rM], [R_e[ei]])
                    unbank(bs)
                    info[n] = (hl, kbi, khalf, q_lo, q_hi, ei)
                if n >= 3:
                    hl, kbi, khalf, q_lo, q_hi, ei = info[n - 3]
                    p0 = 64 * hl
                    h = 2 * hc + hl
                    nq = q_hi - q_lo + 1
                    mm(ps[bnum][p0:p0 + 64, q_lo * 128:(q_hi + 1) * 128], vbuf[l][:, khalf * 4 + kbi % 4, h * 64:(h + 1) * 64], ebuf[:, ei, 0:nq * 128],
                       firsts[hl], False, [R_v[l][khalf], R_e[ei]], [R_ps[bnum]], skip_group_check=True)
                    mm(ps[bden][p0:p0 + 64, q_lo * 128:(q_hi + 1) * 128], ones_bf[:, 0:64], ebuf[:, ei, 0:nq * 128],
                       firsts[hl], False, [R_e[ei]], [R_ps[bden]], skip_group_check=True)
                    firsts[hl] = False
                yield
            td, rd = tmp()
            tn, rn = tmp()
            P.op("act", lambda e: e.activation(out=tn, in_=ps[bnum][:], func=AF.Copy), [R_ps[bnum]], [rn])
            unbank(bnum)
            P.op("act", lambda e: e.activation(out=td, in_=ps[bden][:], func=AF.Ln), [R_ps[bden]], [rd])
            unbank(bden)
            P.op("act", lambda e: e.activation(out=td, in_=td, func=AF.Exp, scale=-1.0), [rd], [rd])
            P.op("pool", lambda e: e.tensor_tensor(out=hid[:, ATT + hc, :], in0=tn, in1=td, op=ALU.mult), [rn, rd], [R_hid[ATT + hc]])

        def side():
            sl, rs_, si = ring_get(l, "inA")
            if i == 0:
                for c in range(2):
                    P.op("pool", lambda e: e.memset(hb[:, c, 0:32], 0.0), [], [R_hb[l][c]])
            for c in range(2):
                bA, bG = bank(), bank()
                for k in range(KC):
                    mm(ps[bA][:], sl[:, k * 512 + c * 128: k * 512 + c * 128 + 128], hT[:, k, :], k == 0, k == KC - 1, [rs_, R_h[k]], [R_ps[bA]])
                for k in range(KC):
                    mm(ps[bG][:], sl[:, k * 512 + 256 + c * 128: k * 512 + 256 + c * 128 + 128], hT[:, k, :], k == 0, k == KC - 1, [rs_, R_h[k]], [R_ps[bG]])
                tg, rg = tmp()
                P.op("act", lambda e: e.activation(out=tg, in_=ps[bG][:], func=AF.Sigmoid), [R_ps[bG]], [rg])
                unbank(bG)
                P.op("dve", lambda e: e.tensor_tensor(out=hb[:, c, 32:32 + T], in0=ps[bA][:], in1=tg, op=ALU.mult), [R_ps[bA], rg], [R_hb[l][c]])
                unbank(bA)
                yield
            ring_rel(si)
            for c in range(2):
                sl, rs_, si = ring_get(l, "cw%d" % c)
                b = bank()
                for k in range(31):
                    mm(ps[b][:], sl[:, k * 128:(k + 1) * 128], hb[:, c, 2 + k:2 + k + T], k == 0, k == 30, [rs_, R_hb[l][c]], [R_ps[b]])
                ring_rel(si)
                P.op("act", lambda e: e.activation(out=cv[:, c, :], in_=ps[b][:], func=AF.Identity, bias=pvl[:, PV_CB + c:PV_CB + c + 1], scale=1.0), [R_ps[b]], [R_cv[c]])
                unbank(b)
                P.op("pool", lambda e: e.tensor_copy(out=hb[:, c, 0:32], in_=hb[:, c, T:T + 32]), [R_hb[l][c]], [R_hb[l][c]])
                P.op("act", lambda e: e.activation(out=sq[:, c, :], in_=cv[:, c, :], func=AF.Copy), [R_cv[c]], [R_sq[c]])
                P.op("act", lambda e: e.activation(out=sq[:, 2 + c, :], in_=cv[:, c, :], func=AF.Square), [R_cv[c]], [R_sq[2 + c]])
                yield
            b1, b2 = bank(), bank()
            for c in range(2):
                mm(ps[b1][:], ones_bf[:], sq[:, c, :], c == 0, c == 1, [R_sq[c]], [R_ps[b1]])
            for c in range(2):
                mm(ps[b2][:], ones_bf[:], sq[:, 2 + c, :], c == 0, c == 1, [R_sq[2 + c]], [R_ps[b2]])
            tm, rm = tmp()
            tv, rv = tmp()
            P.op("act", lambda e: e.activation(out=tm, in_=ps[b1][:], func=AF.Copy, scale=1.0 / 256), [R_ps[b1]], [rm])
            unbank(b1)
            P.op("pool", lambda e: e.tensor_tensor(out=tv, in0=tm, in1=tm, op=ALU.mult), [rm], [rv])
            P.op("dve", lambda e: e.scalar_tensor_tensor(out=tv, in0=ps[b2][:], scalar=1.0 / 256, in1=tv, op0=ALU.mult, op1=ALU.subtract), [R_ps[b2], rv], [rv])
            unbank(b2)
            P.op("act", lambda e: e.activation(out=tv, in_=tv, func=AF.Ln, bias=EPS, scale=1.0), [rv], [rv])
            P.op("act", lambda e: e.activation(out=tv, in_=tv, func=AF.Exp, scale=-0.5), [rv], [rv])
            for c in range(2):
                P.op("pool", lambda e: e.tensor_tensor(out=cv[:, c, :], in0=cv[:, c, :], in1=tm, op=ALU.subtract), [R_cv[c], rm], [R_cv[c]])
                P.op("pool", lambda e: e.tensor_tensor(out=cv[:, c, :], in0=cv[:, c, :], in1=tv, op=ALU.mult), [R_cv[c], rv], [R_cv[c]])
            yield

            def ln_silu():
                for c in range(2):
                    P.op("act", lambda e: e.activation(out=hid[:, CVO + c, :], in_=cv[:, c, :], func=AF.Silu, bias=pvl[:, PV_LB + c:PV_LB + c + 1],
                                                       scale=pvl[:, PV_LG + c:PV_LG + c + 1]), [R_cv[c]], [R_hid[CVO + c]])
            qk_st = {}

            def qk_PE(cq):
                isq, c = cq < 4, cq % 4
                if cq == 0:
                    qk_st["sl"] = ring_get(l, "inQ")
                if cq == 4:
                    ring_rel(qk_st["sl"][2])
                    qk_st["sl"] = ring_get(l, "inK")
                sl, rs_, si = qk_st["sl"]
                b = bank()
                for k in range(KC):
                    mm(ps[b][:], sl[:, k * 512 + c * 128: k * 512 + c * 128 + 128], hT[:, k, :], k == 0, k == KC - 1, [rs_, R_h[k]], [R_ps[b]])
                tq, rq = tmp()
                sqi = cq % 4
                P.op("act", lambda e: e.activation(out=sq[:, sqi, :], in_=ps[b][:], func=AF.Square), [R_ps[b]], [R_sq[sqi]])
                P.op("act", lambda e: e.activation(out=tq, in_=ps[b][:], func=AF.Copy), [R_ps[b]], [rq])
                unbank(b)
                qk_st[cq] = (tq, rq, sqi)
                if cq == 7:
                    ring_rel(si)

            def qk_BLF(cq):
                isq, c = cq < 4, cq % 4
                tq, rq, sqi = qk_st[cq]
                gcol = PV_QG if isq else PV_KG
                b2 = bank()
                mm(ps[b2][:], bones_bf[:], sq[:, sqi, :], True, True, [R_sq[sqi]], [R_ps[b2]])
                tr, rr = tmp()
                P.op("act", lambda e: e.activation(out=tr, in_=ps[b2][:], func=AF.Ln, bias=EPS, scale=1.0 / 64), [R_ps[b2]], [rr])
                unbank(b2)
                P.op("act", lambda e: e.activation(out=tr, in_=tr, func=AF.Exp, scale=-0.5), [rr], [rr])
                dst = hid[:, QN + c, :] if isq else kbuf[l][:, c, half, :]
                dreg = [R_hid[QN + c]] if isq else [R_k[l][half]]
                P.op("dve", lambda e: e.scalar_tensor_tensor(out=dst, in0=tq, scalar=pvl[:, gcol:gcol + 1], in1=tr,
                                                             op0=ALU.mult, op1=ALU.mult), [rq, rr], dreg)
            for cq in range(9):
                if cq < 8:
                    qk_PE(cq)
                if cq == 2:
                    ln_silu()
                if cq >= 1:
                    qk_BLF(cq - 1)
                yield
            sl, rs_, si = ring_get(l, "inV")
            for tb in range(4):
                b = bank()
                for k in range(KC):
                    mm(ps[b][:], hT[:, k, tb * 128:(tb + 1) * 128], sl[:, k * 512:(k + 1) * 512], k == 0, k == KC - 1, [rs_, R_h[k]], [R_ps[b]])
                P.op("act", lambda e: e.activation(out=vbuf[l][:, half * 4 + tb, :], in_=ps[b][:], func=AF.Copy), [R_ps[b]], [R_v[l][half]])
                unbank(b)
                yield
            ring_rel(si)
            for hc in range(4):
                slM, rM, siM = ring_get(l, "me%d" % hc)
                for _ in attention_hc(hc, slM, rM):
                    yield
                ring_rel(siM)

        slB, rB, siB = ring_get(l, "bbar")
        slC, rC, siC = ring_get(l, "cmat")
        if i == 0:
            P.op("pool", lambda e: e.memset(s5c[l][:, 3:5, :], 0.0), [], R_init[l])
        pend = None
        ysb = None
        assert len(bank_free) == 8
        pool_s5 = [bank_free.pop(0) for _ in range(3)]
        pool_side = list(bank_free)
        del bank_free[:]
        cur_pool[0] = pool_s5
        sgen = side()
        side_total = 18 + 4 * (3 + (8 if i == 0 else 16))
        side_done = 0
        for it in range(9):
            if pend is not None:
                j, (t1, r1), (t2, r2), (t3, r3), (t4, r4), (t5, r5), (t6, r6), cosT, sinT, rT, siT = pend
                xr_i, xi_i = 4 + (j % 2) * 2, 5 + (j % 2) * 2
                P.op("dve", lambda e: e.tensor_tensor(out=t1, in0=t2, in1=cosT, op=ALU.mult), [r2, rT], [r1])
                P.op("dve", lambda e: e.tensor_tensor(out=t3, in0=t4, in1=sinT, op=ALU.mult), [r4, rT], [r3])
                P.op("dve", lambda e: e.tensor_tensor(out=sq[:, xr_i, :], in0=t1, in1=t3, op=ALU.subtract), [r1, r3], [R_sq[xr_i]])
                P.op("dve", lambda e: e.tensor_tensor(out=t5, in0=t4, in1=cosT, op=ALU.mult), [r4, rT], [r5])
                P.op("dve", lambda e: e.tensor_tensor(out=t6, in0=t2, in1=sinT, op=ALU.mult), [r2, rT], [r6])
                P.op("dve", lambda e: e.tensor_tensor(out=sq[:, xi_i, :], in0=t5, in1=t6, op=ALU.add), [r5, r6], [R_sq[xi_i]])
                ring_rel(siT)
            if it < 8:
                j = it
                cc = j // 4
                slT, rT, siT = ring_get(l, "tab%d" % j)
                tabf = slT[:, 0:2048].bitcast(F32)
                cosT, sinT = tabf[:, 0:T], tabf[:, T:2 * T]
                br_, bi_ = bank(), bank()
                mm(ps[br_][:], slB[:, cc * 512 + (j % 4) * 128: cc * 512 + (j % 4) * 128 + 128], hid[:, UT + cc, :], True, True, [rB, R_hid[UT + cc]], [R_ps[br_]])
                mm(ps[bi_][:], slB[:, 1024 + cc * 512 + (j % 4) * 128: 1024 + cc * 512 + (j % 4) * 128 + 128], hid[:, UT + cc, :], True, True, [rB, R_hid[UT + cc]], [R_ps[bi_]])
                tt = [(s5tmp[:, (6 * j + q) % NS5T, :], R_s5t[(6 * j + q) % NS5T]) for q in range(6)]
                (t1, r1), (t2, r2), (t3, r3), (t4, r4), (t5, r5), (t6, r6) = tt
                P.op("dve", lambda e: e.tensor_tensor(out=t1, in0=ps[br_][:], in1=cosT, op=ALU.mult), [R_ps[br_], rT], [r1])
                P.op("dve", lambda e: e.tensor_tensor(out=t4, in0=ps[br_][:], in1=sinT, op=ALU.mult), [R_ps[br_], rT], [r4])
                unbank(br_)
                P.op("dve", lambda e: e.tensor_tensor(out=t2, in0=ps[bi_][:], in1=sinT, op=ALU.mult), [R_ps[bi_], rT], [r2])
                P.op("dve", lambda e: e.tensor_tensor(out=t3, in0=ps[bi_][:], in1=cosT, op=ALU.mult), [R_ps[bi_], rT], [r3])
                unbank(bi_)
                P.op("dve", lambda e: e.tensor_tensor(out=t1, in0=t1, in1=t2, op=ALU.add), [r1, r2], [r1])
                P.op("dve", lambda e: e.tensor_tensor(out=t3, in0=t3, in1=t4, op=ALU.subtract), [r3, r4], [r3])
                rdec = s5c[l][:, 0, j:j + 1].to_broadcast([128, T])
                P.op("dve", lambda e: e.tensor_tensor_scan(out=t2, data0=rdec, data1=t1, initial=s5c[l][:, 3, j:j + 1],
                                                           op0=ALU.mult, op1=ALU.add), [r1, R_init[l][j], R_s5c[l]], [r2])
                P.op("dve", lambda e: e.tensor_tensor_scan(out=t4, data0=rdec, data1=t3, initial=s5c[l][:, 4, j:j + 1],
                                                           op0=ALU.mult, op1=ALU.add), [r3, R_init[l][j], R_s5c[l]], [r4])
                P.op("dve", lambda e: e.tensor_scalar(out=t1[:, 0:1], in0=t4[:, T - 1:T], scalar1=s5c[l][:, 2, j:j + 1], scalar2=None, op0=ALU.mult),
                     [r4, R_s5c[l]], [r1])
                P.op("dve", lambda e: e.tensor_scalar(out=t1[:, 1:2], in0=t2[:, T - 1:T], scalar1=s5c[l][:, 2, j:j + 1], scalar2=None, op0=ALU.mult),
                     [r2, R_s5c[l]], [r1])
                P.op("dve", lambda e: e.scalar_tensor_tensor(out=s5c[l][:, 3, j:j + 1], in0=t2[:, T - 1:T], scalar=s5c[l][:, 1, j:j + 1], in1=t1[:, 0:1],
                                                             op0=ALU.mult, op1=ALU.subtract), [r2, r1, R_s5c[l]], [R_init[l][j]])
                P.op("dve", lambda e: e.scalar_tensor_tensor(out=s5c[l][:, 4, j:j + 1], in0=t4[:, T - 1:T], scalar=s5c[l][:, 1, j:j + 1], in1=t1[:, 1:2],
                                                             op0=ALU.mult, op1=ALU.add), [r4, r1, R_s5c[l]], [R_init[l][j]])
                newpend = (j, (t1, r1), (t2, r2), (t3, r3), (t4, r4), (t5, r5), (t6, r6), cosT, sinT, rT, siT)
            else:
                newpend = None
            if sgen is not None:
                cur_pool[0] = pool_side
                nsteps = -(-(side_total - side_done) // (9 - it))
                for _ in range(nsteps):
                    try:
                        next(sgen)
                        side_done += 1
                    except StopIteration:
                        sgen = None
                        break
                cur_pool[0] = pool_s5
            if pend is not None:
                j = pend[0]
                cc = j // 4
                xr_i, xi_i = 4 + (j % 2) * 2, 5 + (j % 2) * 2
                if j % 4 == 0:
                    ysb = bank()
                mm(ps[ysb][:], slC[:, j * 128:(j + 1) * 128], sq[:, xr_i, :], j % 4 == 0, False, [rC, R_sq[xr_i]], [R_ps[ysb]])
                mm(ps[ysb][:], slC[:, 1024 + j * 128:1024 + (j + 1) * 128], sq[:, xi_i, :], False, j % 4 == 3, [rC, R_sq[xi_i]], [R_ps[ysb]])
                if j % 4 == 3:
                    (ty, ry), (tz, rz) = tmp(), tmp()
                    P.op("dve", lambda e: e.tensor_tensor(out=ty, in0=ps[ysb][:], in1=du[:, cc, :], op=ALU.add), [R_ps[ysb], R_du], [ry])
                    unbank(ysb)
                    P.op("act", lambda e: e.activation(out=tz, in_=ty, func=AF.Square), [ry], [rz])
                    P.op("act", lambda e: e.activation(out=tz, in_=tz, func=AF.Identity, bias=1.0, scale=0.044715), [rz], [rz])
                    P.op("pool", lambda e: e.tensor_tensor(out=tz, in0=tz, in1=ty, op=ALU.mult), [rz, ry], [rz])
                    P.op("act", lambda e: e.activation(out=tz, in_=tz, func=AF.Sigmoid, scale=1.5957691216057308), [rz], [rz])
                    P.op("pool", lambda e: e.tensor_tensor(out=hid[:, MERG + cc, :], in0=ty, in1=tz, op=ALU.mult), [ry, rz], [R_hid[MERG + cc]])
            pend = newpend
        ring_rel(siB)
        ring_rel(siC)
        if sgen is not None:
            cur_pool[0] = pool_side
            for _ in sgen:
                pass
        assert len(pool_s5) + len(pool_side) == 8, (pool_s5, pool_side)
        bank_free[:] = pool_s5 + pool_side
        cur_pool[0] = bank_free
        sl, rs_, si = ring_get(l, "glu")
        for c in range(2):
            bA, bG = bank(), bank()
            for k in range(2):
                mm(ps[bA][:], sl[:, k * 512 + c * 128: k * 512 + c * 128 + 128], hid[:, MERG + k, :], k == 0, k == 1, [rs_, R_hid[MERG + k]], [R_ps[bA]])
            for k in range(2):
                mm(ps[bG][:], sl[:, k * 512 + 256 + c * 128: k * 512 + 256 + c * 128 + 128], hid[:, MERG + k, :], k == 0, k == 1, [rs_, R_hid[MERG + k]], [R_ps[bG]])
            tg, rg = tmp()
            P.op("act", lambda e: e.activation(out=tg, in_=ps[bG][:], func=AF.Sigmoid), [R_ps[bG]], [rg])
            unbank(bG)
            P.op("dve", lambda e: e.tensor_tensor(out=hid[:, S5O + c, :], in0=ps[bA][:], in1=tg, op=ALU.mult), [R_ps[bA], rg], [R_hid[S5O + c]])
            unbank(bA)
        ring_rel(si)

        slS, rS, siS = ring_get(l, "brs")
        slA, rA, siA = ring_get(l, "bra")
        for m in range(8):
            slG, rG, siG = ring_get(l, "g%d" % m)
            acc = None
            bgs = [bank() for _ in range(3)]
            for b in range(3):
                for k in range(KC):
                    mm(ps[bgs[b]][:], slG[:, k * 384 + b * 128: k * 384 + b * 128 + 128], hT[:, k, :], k == 0, k == KC - 1, [rG, R_h[k]], [R_ps[bgs[b]]])
            bys = [bank() for _ in range(3)]
            for b in range(3):
                by = bys[b]
                if b == 0:
                    for k in range(2):
                        mm(ps[by][:], slS[:, k * 1024 + m * 128: k * 1024 + m * 128 + 128], hid[:, S5O + k, :], k == 0, k == 1, [rS, R_hid[S5O + k]], [R_ps[by]])
                elif b == 1:
                    for k in range(4):
                        mm(ps[by][:], slA[:, k * 1024 + m * 128: k * 1024 + m * 128 + 128], hid[:, ATT + k, :], k == 0, k == 3, [rA, R_hid[ATT + k]], [R_ps[by]])
                else:
                    for k in range(2):
                        mm(ps[by][:], slS[:, (2 + k) * 1024 + m * 128: (2 + k) * 1024 + m * 128 + 128], hid[:, CVO + k, :], k == 0, k == 1, [rS, R_hid[CVO + k]], [R_ps[by]])
            for b in range(3):
                bg, by = bgs[b], bys[b]
                tg, rg = tmp()
                P.op("act", lambda e: e.activation(out=tg, in_=ps[bg][:], func=AF.Sigmoid,
                                                   bias=pvl[:, PV_BG + b * 8 + m:PV_BG + b * 8 + m + 1], scale=1.0), [R_ps[bg]], [rg])
                unbank(bg)
                last = (b == max(branches)) and acc is not None
                if b not in branches:
                    unbank(by)
                    continue
                P.op("dve", lambda e: e.tensor_tensor(out=tg, in0=ps[by][:], in1=tg, op=ALU.mult), [R_ps[by], rg], [rg])
                if acc is None:
                    acc = (tg, rg)
                elif last:
                    P.op("pool", lambda e: e.tensor_tensor(out=hid[:, MERG + m, :], in0=acc[0], in1=tg, op=ALU.add), [acc[1], rg], [R_hid[MERG + m]])
                else:
                    P.op("pool", lambda e: e.tensor_tensor(out=acc[0], in0=acc[0], in1=tg, op=ALU.add), [acc[1], rg], [acc[1]])
                unbank(by)
            if len(branches) == 1:
                P.op("pool", lambda e: e.tensor_copy(out=hid[:, MERG + m, :], in_=acc[0]), [acc[1]], [R_hid[MERG + m]])
            ring_rel(siG)
        ring_rel(siS)
        ring_rel(siA)
        for jj in range(2):
            sl, rs_, si = ring_get(l, "wo%d" % jj)
            for mm_ in range(4):
                m = 4 * jj + mm_
                b = bank()
                for k in range(KC):
                    mm(ps[b][:], sl[:, k * 512 + mm_ * 128: k * 512 + mm_ * 128 + 128], hid[:, MERG + k, :], k == 0, k == KC - 1, [rs_, R_hid[MERG + k]], [R_ps[b]])
                P.op("dve", lambda e: e.tensor_tensor(out=xres[:, m, :], in0=ps[b][:], in1=xres[:, m, :], op=ALU.add), [R_ps[b], R_x[m]], [R_x[m]])
                unbank(b)
            ring_rel(si)

    def run_tiles():
        xtok = tmpall[:, 0:8, :].rearrange("p (a b) c -> p a (b c)", b=2)
        stages = dbg.get("stages", ("f1", "mix", "f2"))
        for s_ in range(NSEQ):
            for i in range(NT):
                r0 = s_ * S + i * T
                P.op("sp", lambda e, r0=r0: e.dma_start(out=xtok, in_=x_d[r0:r0 + T, :].rearrange("(a p) c -> p a c", p=128)), [R_xd], R_tmp[0:8], dma_sem="xin")
                for k in range(KC):
                    b = bank()
                    for tb in range(4):
                        P.op("pe", lambda e, b=b, tb=tb, k=k: e.transpose(out=ps[b][:, tb * 128:(tb + 1) * 128], in_=xtok[:, tb, k * 128:(k + 1) * 128], identity=ident[:]),
                             R_tmp[0:8] + [R_ident], [R_ps[b]])
                    P.op("act", lambda e, b=b, k=k: e.activation(out=xres[:, k, :], in_=ps[b][:], func=AF.Copy), [R_ps[b]], [R_x[k]])
                    unbank(b)
                for l in range(nlayers):
                    if "f1" in stages:
                        ffn(l, 0)
                    else:
                        for n_, w_ in PIECES[0:17]:
                            ring_rel(ring_get(l, n_)[2])
                    if "mix" in stages:
                        mixer(l, i)
                    else:
                        for n_, w_ in PIECES[17:NPL - 17]:
                            ring_rel(ring_get(l, n_)[2])
                    if "f2" in stages:
                        ffn(l, 1)
                    else:
                        for n_, w_ in PIECES[NPL - 17:]:
                            ring_rel(ring_get(l, n_)[2])
                for tb in range(4):
                    for g in range(2):
                        b = bank()
                        for kk in range(4):
                            k = 4 * g + kk
                            P.op("pe", lambda e, b=b, tb=tb, k=k, kk=kk: e.transpose(out=ps[b][:, kk * 128:(kk + 1) * 128], in_=xres[:, k, tb * 128:(tb + 1) * 128], identity=ident[:]),
                                 [R_x[k], R_ident], [R_ps[b]])
                        P.op("act", lambda e, b=b, tb=tb, g=g: e.activation(out=xtok[:, tb, g * 512:(g + 1) * 512], in_=ps[b][:], func=AF.Copy), [R_ps[b]], R_tmp[0:8])
                        unbank(b)
                P.op("sp", lambda e, r0=r0: e.dma_start(out=y_d[r0:r0 + T, :].rearrange("(a p) c -> p a c", p=128), in_=xtok), R_tmp[0:8], [R_yd], dma_sem="yout")

    P_real = P
    P = Plan()
    run_tiles()
    for r_ in P.allregs.values():
        r_.w = None
        r_.readers = {}
    P = P_real
    DRY[0] = False
    bank_free[:] = list(range(8))
    tmp_rr[0] = 0
    for grp, members in cast_groups.items():
        fin = cast_fin[grp]
        for (l_, i_) in members:
            R_wscr[l_][i_].w = (grp, fin, "dma:" + grp)
    ring_fill()
    run_tiles()
    P.final_wait("sp", [R_yd])
    for e_ in COMPUTE:
        P.final_wait(e_, [R_yd])
    P.emit(nc)
    es.close()
    return nc


def host_layouts(inp):
    f = lambda a: np.asarray(a, dtype=np.float32)
    pv = np.zeros((L, 128, NPV), np.float32)
    lamR = np.zeros((L, 128, 3, 1024), np.float32)
    braw = np.zeros((L, 128, 2, 2, 512), np.float32)
    craw = np.zeros((L, 128, 2, 8, 128), np.float32)
    btoe = np.zeros((L, 128, 8, 5, 128), np.float32)
    kk = np.arange(128)[:, None, None]
    rel = np.arange(5)[None, :, None]
    qq = np.arange(128)[None, None, :]
    dist = 128 * rel + qq - kk
    idx = np.clip(dist, -128, 128) + 128
    dch = 2 * rel + (qq >= 64) - (kk >= 64)
    mask = ((dch >= 0) & (dch <= 8)).astype(np.float32).reshape(128, 640)
    mask01 = np.concatenate([mask, mask], axis=1)
    for l in range(L):
        fm = lambda v, n: f(v).reshape(n, 128).T
        pv[l, :, PV_N1:PV_N1 + 8] = fm(inp["ffn1_norm"][l], 8)
        pv[l, :, PV_NM:PV_NM + 8] = fm(inp["mix_norm"][l], 8)
        pv[l, :, PV_N2:PV_N2 + 8] = fm(inp["ffn2_norm"][l], 8)
        pv[l, :, PV_BG:PV_BG + 24] = fm(inp["b_gate"][l], 24)
        pv[l, :, PV_D:PV_D + 2] = fm(inp["s5_d"][l], 2)
        pv[l, :, PV_CB:PV_CB + 2] = fm(inp["conv_b_dw"][l], 2)
        pv[l, :, PV_LG:PV_LG + 2] = fm(inp["conv_ln_g"][l], 2)
        pv[l, :, PV_LB:PV_LB + 2] = fm(inp["conv_ln_b"][l], 2)
        cw = f(inp["conv_w_dw"][l])
        for c in range(2):
            pv[l, :, PV_CW + c * 31:PV_CW + c * 31 + 31] = cw[:, c * 128:(c + 1) * 128].T
        pv[l, :, PV_QG] = np.tile(f(inp["attn_q_gain"][l]), 2)
        pv[l, :, PV_KG] = np.tile(f(inp["attn_k_gain"][l]), 2)
        lre, lim, ldt = f(inp["s5_lambda_re"][l]), f(inp["s5_lambda_im"][l]), f(inp["s5_log_dt"][l])
        pv[l, :, PV_LR:PV_LR + 8] = lre.reshape(8, 128).T
        pv[l, :, PV_LI:PV_LI + 8] = lim.reshape(8, 128).T
        pv[l, :, PV_DT:PV_DT + 8] = np.repeat(ldt, 64).reshape(8, 128).T
        lamR[l, :, 0, :] = lre.reshape(1, 1024)
        lamR[l, :, 1, :] = lim.reshape(1, 1024)
        lamR[l, :, 2, :] = np.repeat(ldt, 64).reshape(1, 1024)
        bre, bim = f(inp["s5_b_re"][l]), f(inp["s5_b_im"][l])
        cre, cim = f(inp["s5_c_re"][l]), f(inp["s5_c_im"][l])
        for g in range(16):
            cc, gl = g // 8, g % 8
            braw[l, 16 * gl:16 * gl + 16, 0, cc, 64 * gl:64 * gl + 64] = bre[g].T
            braw[l, 16 * gl:16 * gl + 16, 1, cc, 64 * gl:64 * gl + 64] = bim[g].T
            j, g2 = g // 2, g % 2
            craw[l, 64 * g2:64 * g2 + 64, 0, j, 16 * gl:16 * gl + 16] = cre[g].T
            craw[l, 64 * g2:64 * g2 + 64, 1, j, 16 * gl:16 * gl + 16] = cim[g].T
        rb = f(inp["attn_rel_bias"][l])
        btoe[l] = np.transpose(rb[:, idx], (1, 0, 2, 3))
    return {"pv": pv, "lamR": lamR, "braw": braw.reshape(L, 128, 2048), "craw": craw.reshape(L, 128, 2048),
            "btoe": btoe.reshape(L, 128, 5120), "mask01": mask01, "ident": np.eye(128, dtype=np.float32)}


WKEYS = ["ffn1_w_up", "ffn2_w_up", "ffn1_w_down", "ffn2_w_down", "w_in", "s5_w_glu", "w_br_s5", "w_br_attn", "w_br_conv", "w_out"]


def run(inputs, n_cores, dbg=None):
    x = np.asarray(inputs["x"], dtype=np.float32)
    B, S, _ = x.shape
    nseq = B // n_cores
    nc = build_program(nseq, S, dbg)
    common = host_layouts(inputs)
    for k in WKEYS:
        common[k] = np.ascontiguousarray(np.asarray(inputs[k], dtype=np.float32))
    in_maps = []
    for c in range(n_cores):
        m = dict(common)
        m["x"] = np.ascontiguousarray(x[c * nseq:(c + 1) * nseq].reshape(nseq * S, D))
        in_maps.append(m)
    res = run_bass_kernel_spmd(nc, in_maps, core_ids=list(range(n_cores)))
    out = np.stack([np.asarray(r["y"]).reshape(nseq, S, D) for r in res.results], axis=0)
    return out.reshape(B, S, D).astype(np.float32)


def kernel(**inputs):
    return run(inputs, 8)
```

```python
import math
import numpy as np
from contextlib import ExitStack
import concourse.bass as bass
import concourse.mybir as mybir
from concourse.bass_utils import run_bass_kernel_spmd

F32 = mybir.dt.float32
BF16 = mybir.dt.bfloat16
AF = mybir.ActivationFunctionType
ALU = mybir.AluOpType

COMPUTE = ("pe", "act", "dve", "pool")
ALL_ENG = COMPUTE + ("sp",)


class Reg:
    __slots__ = ("name", "w", "readers", "excl")

    def __init__(self, name, excl=False):
        self.name = name
        self.w = None
        self.readers = {}
        self.excl = excl


class Op:
    __slots__ = ("eng", "fn", "waits", "tok", "needed", "dma")

    def __init__(self, eng, fn, waits, tok, dma):
        self.eng = eng
        self.fn = fn
        self.waits = waits
        self.tok = tok
        self.needed = False
        self.dma = dma


class _Rec:
    def __init__(self):
        self.calls = []

    def __getattr__(self, name):
        def f(*a, **k):
            self.calls.append((name, a, k))
        return f


class Plan:
    def __init__(self):
        self.ops = {e: [] for e in ALL_ENG}
        self.cnt = {e: 0 for e in COMPUTE}
        self.seen = {e: {} for e in ALL_ENG}
        self.dma_cnt = {}
        self.tokmap = {}
        self.allregs = {}
        self.ext = {}

    def _need(self, eng, waits, tok, same_ok):
        if tok is None:
            return
        key, val, teng = tok
        if teng == eng and same_ok and eng == "pe":
            return
        if self.seen[eng].get(key, 0) >= val:
            return
        if waits.get(key, 0) < val:
            waits[key] = val

    def op(self, eng, fn, reads=(), writes=(), dma_sem=None):
        waits = {}
        for r in reads:
            self.allregs[id(r)] = r
        for r in writes:
            self.allregs[id(r)] = r
        for r in reads:
            if r.excl:
                self._need(eng, waits, r.w, True)
                for k, (v, e) in r.readers.items():
                    self._need(eng, waits, (k, v, e), True)
            else:
                self._need(eng, waits, r.w, False)
        for w in writes:
            self._need(eng, waits, w.w, True)
            for k, (v, e) in w.readers.items():
                self._need(eng, waits, (k, v, e), True)
        if dma_sem is not None:
            val = self.dma_cnt.get(dma_sem, 0) + 16
            self.dma_cnt[dma_sem] = val
            tok = (dma_sem, val, "dma:" + dma_sem)
        else:
            self.cnt[eng] += 1
            tok = (eng, self.cnt[eng], eng)
        rec = _Rec()
        fn(rec)
        assert len(rec.calls) == 1
        o = Op(eng, rec.calls[0], waits, tok, dma_sem is not None)
        self.tokmap[(tok[0], tok[1])] = o
        for k, v in waits.items():
            self.seen[eng][k] = v
            t = self.tokmap.get((k, v))
            if t is not None:
                t.needed = True
        self.ops[eng].append(o)
        for r in reads:
            old = r.readers.get(tok[0])
            if old is None or old[0] < tok[1]:
                r.readers[tok[0]] = (tok[1], tok[2])
        for w in writes:
            w.w = tok
            w.readers = {}
        return o

    def final_wait(self, eng, regs):
        waits = {}
        for r in regs:
            self._need(eng, waits, r.w, True)
            for k, (v, e) in r.readers.items():
                self._need(eng, waits, (k, v, e), True)
        o = Op(eng, None, waits, None, False)
        for k, v in waits.items():
            self.seen[eng][k] = max(self.seen[eng].get(k, 0), v)
            t = self.tokmap.get((k, v))
            if t is not None:
                t.needed = True
        self.ops[eng].append(o)

    def barrier(self):
        regs = list(self.allregs.values())
        for e_ in ALL_ENG:
            self.final_wait(e_, regs)

    def emit(self, nc, M=3000):
        remap = {}
        nep = {}
        for e in COMPUTE:
            c = 0
            m = {}
            for o in self.ops[e]:
                if o.tok is not None and not o.dma and o.needed:
                    m[o.tok[1]] = (c // M, c % M + 1)
                    c += 1
            remap[e] = m
            nep[e] = (c + M - 1) // M
        with ExitStack() as es:
            sems = {}
            for e in COMPUTE:
                for k in range(max(1, nep[e])):
                    sems[(e, k)] = es.enter_context(nc.semaphore("s_%s%d" % (e, k)))
            for k, tot in self.dma_cnt.items():
                if k in self.ext:
                    sems[(k, 0)] = self.ext[k]
                    continue
                n = (tot // 16 + M - 1) // M
                for j in range(max(1, n)):
                    sems[(k, j)] = es.enter_context(nc.semaphore("d_%s_%d" % (k, j)))
            self.nsems = len(sems)
            block = es.enter_context(nc.Block())
            plan = self

            def sv(k, v):
                if k in remap:
                    ep, val = remap[k][v]
                    return sems[(k, ep)], val
                n = v // 16 - 1
                return sems[(k, n // M)], (n % M + 1) * 16

            def run(engname):
                def body(eng):
                    for o in plan.ops[engname]:
                        for k, v in o.waits.items():
                            s_, v_ = sv(k, v)
                            eng.wait_ge(s_, v_)
                        if o.fn is None:
                            continue
                        name_, a_, k_ = o.fn
                        ins = getattr(eng, name_)(*a_, **k_)
                        if o.dma:
                            s_, _ = sv(o.tok[0], o.tok[1])
                            ins.then_inc(s_, 16)
                        elif o.needed:
                            s_, _ = sv(o.tok[0], o.tok[1])
                            ins.then_inc(s_, 1)
                return body

            block.tensor(run("pe"))
            block.scalar(run("act"))
            block.vector(run("dve"))
            block.gpsimd(run("pool"))
            block.sync(run("sp"))


D = 1024
KC = 8
DFF = 2816
HC = 22
T = 512
L = 2
EPS = 1e-6
IN_COLS = 5376
NPV = 144
BIAS_PE = False
SLOTW = 4096
PV_N1, PV_NM, PV_N2, PV_BG, PV_D, PV_CB, PV_LG, PV_LB, PV_CW, PV_QG, PV_KG, PV_LR, PV_LI, PV_DT = (
    0, 8, 16, 24, 48, 50, 52, 54, 56, 118, 119, 120, 128, 136)


def layer_pieces():
    p = []
    for f in (1, 2):
        pass
    ffn = lambda f: [("f%du%d" % (f, j), 4096) for j in range(11)] + \
        [("f%dd%d" % (f, j), 4096 if j < 5 else 2048) for j in range(6)]
    mix = [("inA", 4096), ("cw0", 3968), ("cw1", 3968), ("inQ", 4096), ("inK", 4096), ("inV", 4096), ("inU", 2048),
           ("bbar", 2048), ("cmat", 2048)]
    for hc in range(4):
        mix += [("me%d" % hc, 1280), ("tab%d" % (2 * hc), 2048), ("tab%d" % (2 * hc + 1), 2048)]
    mix += [("glu", 1024), ("brs", 4096), ("bra", 4096)] + [("g%d" % j, 3072) for j in range(8)] + \
           [("wo0", 4096), ("wo1", 4096)]
    return ffn(1) + mix + ffn(2)


PIECES = layer_pieces()
PIDX = {n: i for i, (n, w) in enumerate(PIECES)}
NPL = len(PIECES)


def build_program(NSEQ, S, dbg=None):
    dbg = dbg or {}
    NT = S // T
    nlayers = dbg.get("layers", L)
    nc = bass.Bass("TRN2", target_bir_lowering=False)
    dram_in = lambda n, s, d=F32: nc.dram_tensor(n, s, d, kind="ExternalInput").ap()
    x_d = dram_in("x", [NSEQ * S, D])
    y_d = nc.dram_tensor("y", [NSEQ * S, D], F32, kind="ExternalOutput").ap()
    w_up = [dram_in("ffn1_w_up", [L, D, 2 * DFF]), dram_in("ffn2_w_up", [L, D, 2 * DFF])]
    w_dn = [dram_in("ffn1_w_down", [L, DFF, D]), dram_in("ffn2_w_down", [L, DFF, D])]
    w_in = dram_in("w_in", [L, D, IN_COLS])
    w_glu = dram_in("s5_w_glu", [L, 256, 512])
    w_brs = dram_in("w_br_s5", [L, 256, D])
    w_bra = dram_in("w_br_attn", [L, 512, D])
    w_brc = dram_in("w_br_conv", [L, 256, D])
    w_out = dram_in("w_out", [L, D, D])
    pv_d = dram_in("pv", [L, 128, NPV])
    lamR_d = dram_in("lamR", [L, 128, 3, 1024])
    braw_d = dram_in("braw", [L, 128, 2048])
    craw_d = dram_in("craw", [L, 128, 2048])
    btoe_d = dram_in("btoe", [L, 128, 5120])
    mask_d = dram_in("mask01", [128, 1280])
    ident_d = dram_in("ident", [128, 128])
    wscr = nc.dram_tensor("wscr", [L * NPL, 128, SLOTW], BF16, kind="Internal").ap()
    tscr = nc.dram_tensor("tscr", [L * 8, 128, 1024], F32, kind="Internal").ap()

    es = ExitStack()
    sb = lambda n, s, d: es.enter_context(nc.sbuf_tensor("sb_" + n, s, d))
    ident = sb("ident", [128, 128], F32)
    ones_bf = sb("ones_bf", [128, 128], BF16)
    bones_bf = sb("bones_bf", [128, 128], BF16)
    identb = sb("identb", [128, 128], BF16)
    PV = [sb("pv%d" % l, [128, NPV], F32) for l in range(L)]
    s5c = [sb("s5c%d" % l, [128, 5, 8], F32) for l in range(L)]
    R_ident, R_const = Reg("ident"), Reg("const")
    R_pv = [Reg("pv%d" % l) for l in range(L)]
    R_s5c = [Reg("s5c%d" % l) for l in range(L)]
    R_init = [[Reg("init%d_%d" % (l, j)) for j in range(8)] for l in range(L)]
    R_wscr = [[Reg("wscr%d_%d" % (l, i)) for i in range(NPL)] for l in range(L)]

    def scr(l, name, width=None):
        i = PIDX[name]
        w = width or PIECES[i][1]
        return wscr[l * NPL + i, :, 0:w]

    def issue_casts(P):
        cast_groups = {}

        def cast(l, name, dst_sl, src, grp):
            i = PIDX[name]
            P.op("pool", lambda e: e.dma_start(out=dst_sl, in_=src), [], [], dma_sem=grp)
            cast_groups.setdefault(grp, set()).add((l, i))

        def pc(l, name, nk, width):
            return scr(l, name, nk * width).rearrange("p (k c) -> p k c", k=nk)

        for l in range(nlayers):
            for f in range(2):
                g = "w%d_%d" % (l, f * 2)
                for j in range(11):
                    d = pc(l, "f%du%d" % (f + 1, j), 8, 512)
                    for half in range(2):
                        src = w_up[f][l, :, half * DFF + 256 * j: half * DFF + 256 * j + 256].rearrange("(k p) c -> p k c", p=128)
                        cast(l, "f%du%d" % (f + 1, j), d[:, :, half * 256:(half + 1) * 256], src, g)
                for j in range(6):
                    nk = 4 if j < 5 else 2
                    d = pc(l, "f%dd%d" % (f + 1, j), nk, 1024)
                    src = w_dn[f][l, 512 * j: 512 * j + 128 * nk, :].rearrange("(k p) c -> p k c", p=128)
                    cast(l, "f%dd%d" % (f + 1, j), d, src, g)
                if f == 0:
                    g = "w%d_1" % l
                    wi = lambda c0, n_: w_in[l, :, c0:c0 + n_].rearrange("(k p) c -> p k c", p=128)
                    cast(l, "inA", pc(l, "inA", 8, 512), wi(1792, 512), g)
                    cast(l, "inQ", pc(l, "inQ", 8, 512), wi(256, 512), g)
                    cast(l, "inK", pc(l, "inK", 8, 512), wi(768, 512), g)
                    cast(l, "inV", pc(l, "inV", 8, 512), wi(1280, 512), g)
                    cast(l, "inU", pc(l, "inU", 8, 256), wi(0, 256), g)
                    cast(l, "glu", pc(l, "glu", 2, 512), w_glu[l, :, :].rearrange("(k p) c -> p k c", p=128), g)
                    d = pc(l, "brs", 4, 1024)
                    cast(l, "brs", d[:, 0:2, :], w_brs[l, :, :].rearrange("(k p) c -> p k c", p=128), g)
                    cast(l, "brs", d[:, 2:4, :], w_brc[l, :, :].rearrange("(k p) c -> p k c", p=128), g)
                    cast(l, "bra", pc(l, "bra", 4, 1024), w_bra[l, :, :].rearrange("(k p) c -> p k c", p=128), g)
                    for m in range(8):
                        d = pc(l, "g%d" % m, 8, 384)
                        for b in range(3):
                            cast(l, "g%d" % m, d[:, :, b * 128:(b + 1) * 128], wi(2304 + 1024 * b + 128 * m, 128), g)
                    for j in range(2):
                        cast(l, "wo%d" % j, pc(l, "wo%d" % j, 8, 512),
                             w_out[l, :, 512 * j:512 * j + 512].rearrange("(k p) c -> p k c", p=128), g)


        return cast_groups

    PI = math.pi
    P = Plan()
    for l in range(-1, nlayers):
        with ExitStack() as ss:
            st = lambda n, s, d=F32, l=l: ss.enter_context(nc.sbuf_tensor("st%d_%s" % (l + 1, n), s, d))
            regs_all = []

            def RG(n):
                r = Reg(n)
                regs_all.append(r)
                return r
            if l < 0:
                P.op("sp", lambda e: e.dma_start(out=ident[:], in_=ident_d[:, :]), [], [R_ident], dma_sem="c0")
                P.op("pool", lambda e: e.memset(ones_bf[:], 1.0), [], [R_const])
                P.op("pool", lambda e: e.memset(bones_bf[:], 0.0), [], [R_const])
                P.op("pool", lambda e: e.memset(bones_bf[0:64, 0:64], 1.0), [], [R_const])
                P.op("pool", lambda e: e.memset(bones_bf[64:128, 64:128], 1.0), [], [R_const])
                P.op("dve", lambda e: e.tensor_copy(out=identb[:], in_=ident[:]), [R_ident], [R_const])
                cast_groups = issue_casts(P)
                cast_fin = {g_: P.dma_cnt[g_] for g_ in cast_groups}
                continue
            maskt = st("maskt", [128, 1280])
            R_mask = RG("mask")
            P.op("sp", lambda e: e.dma_start(out=maskt[:], in_=mask_d[:, :]), [], [R_mask], dma_sem="c1")

            def s5_params(lr, li, ldt, shape, pfx, R):
                tl = {}

                def t(n):
                    tl[n] = st("%s_%s" % (pfx, n), shape)
                    return tl[n]
                V = lambda fn: P.op("dve", fn, [R], [R])
                A = lambda fn: P.op("act", fn, [R], [R])
                lrc, dt, a, mag, th = t("lrc"), t("dt"), t("a"), t("mag"), t("th")
                V(lambda e: e.tensor_scalar(out=lrc[:], in0=lr, scalar1=-1e-4, scalar2=None, op0=ALU.min))
                A(lambda e: e.activation(out=dt[:], in_=ldt, func=AF.Exp))
                V(lambda e: e.tensor_tensor(out=a[:], in0=lrc[:], in1=dt[:], op=ALU.mult))
                A(lambda e: e.activation(out=mag[:], in_=a[:], func=AF.Exp))
                V(lambda e: e.tensor_tensor(out=th[:], in0=li, in1=dt[:], op=ALU.mult))
                ths, thc0, thc, m = t("ths"), t("thc0"), t("thc"), a
                V(lambda e: e.tensor_copy(out=ths[:], in_=th[:]))
                V(lambda e: e.tensor_scalar(out=thc0[:], in0=th[:], scalar1=PI / 2, scalar2=None, op0=ALU.add))
                V(lambda e: e.tensor_copy(out=thc[:], in_=thc0[:]))
                for kk in range(5):
                    thr = (2 * kk + 1) * PI
                    V(lambda e: e.tensor_scalar(out=m[:], in0=th[:], scalar1=thr, scalar2=-2 * PI, op0=ALU.is_gt, op1=ALU.mult))
                    V(lambda e: e.tensor_tensor(out=ths[:], in0=ths[:], in1=m[:], op=ALU.add))
                    V(lambda e: e.tensor_scalar(out=m[:], in0=thc0[:], scalar1=thr, scalar2=-2 * PI, op0=ALU.is_gt, op1=ALU.mult))
                    V(lambda e: e.tensor_tensor(out=thc[:], in0=thc[:], in1=m[:], op=ALU.add))
                sn, cs = ths, thc
                A(lambda e: e.activation(out=sn[:], in_=ths[:], func=AF.Sin))
                A(lambda e: e.activation(out=cs[:], in_=thc[:], func=AF.Sin))
                n2, t2 = thc0, th
                V(lambda e: e.tensor_tensor(out=n2[:], in0=sn[:], in1=sn[:], op=ALU.mult))
                V(lambda e: e.tensor_tensor(out=t2[:], in0=cs[:], in1=cs[:], op=ALU.mult))
                V(lambda e: e.tensor_tensor(out=n2[:], in0=n2[:], in1=t2[:], op=ALU.add))
                A(lambda e: e.activation(out=n2[:], in_=n2[:], func=AF.Sqrt))
                V(lambda e: e.reciprocal(out=n2[:], in_=n2[:]))
                V(lambda e: e.tensor_tensor(out=sn[:], in0=sn[:], in1=n2[:], op=ALU.mult))
                V(lambda e: e.tensor_tensor(out=cs[:], in0=cs[:], in1=n2[:], op=ALU.mult))
                return {"lrc": lrc, "mag": mag, "sn": sn, "cs": cs, "f1": n2, "f2": t2, "f3": a, "f4": dt}

            P.op("sp", lambda e: e.dma_start(out=PV[l][:], in_=pv_d[l, :, :]), [], [R_pv[l]], dma_sem="c2")
            RS = RG("s5S")
            P.op("dve", lambda e: e.tensor_copy(out=s5c[l][:, 0, :], in_=PV[l][:, PV_LR:PV_LR + 8]), [R_pv[l]], [RS])
            tS = s5_params(PV[l][:, PV_LR:PV_LR + 8], PV[l][:, PV_LI:PV_LI + 8], PV[l][:, PV_DT:PV_DT + 8], [128, 8], "S", RS)
            V = lambda fn: P.op("dve", fn, [RS], [RS])
            V(lambda e: e.tensor_copy(out=s5c[l][:, 0, :], in_=tS["mag"][:]))
            Ct = st("Ctab", [128, 8, T])
            St = st("Stab", [128, 8, T])
            cn = st("cn", [128, 8, 1])
            sn_ = st("snn", [128, 8, 1])
            tA = st("tA", [128, 8, 256])
            tB = st("tB", [128, 8, 256])
            V(lambda e: e.memset(Ct[:, :, 0:1], 1.0))
            V(lambda e: e.memset(St[:, :, 0:1], 0.0))
            V(lambda e: e.tensor_copy(out=cn[:, :, 0], in_=tS["cs"][:]))
            V(lambda e: e.tensor_copy(out=sn_[:, :, 0], in_=tS["sn"][:]))
            n = 1
            while n < T:
                cb = cn[:, :, 0:1].to_broadcast([128, 8, n])
                sbb = sn_[:, :, 0:1].to_broadcast([128, 8, n])
                V(lambda e: e.tensor_tensor(out=tA[:, :, 0:n], in0=Ct[:, :, 0:n], in1=cb, op=ALU.mult))
                V(lambda e: e.tensor_tensor(out=tB[:, :, 0:n], in0=St[:, :, 0:n], in1=sbb, op=ALU.mult))
                V(lambda e: e.tensor_tensor(out=Ct[:, :, n:2 * n], in0=tA[:, :, 0:n], in1=tB[:, :, 0:n], op=ALU.subtract))
                V(lambda e: e.tensor_tensor(out=tA[:, :, 0:n], in0=St[:, :, 0:n], in1=cb, op=ALU.mult))
                V(lambda e: e.tensor_tensor(out=tB[:, :, 0:n], in0=Ct[:, :, 0:n], in1=sbb, op=ALU.mult))
                V(lambda e: e.tensor_tensor(out=St[:, :, n:2 * n], in0=tA[:, :, 0:n], in1=tB[:, :, 0:n], op=ALU.add))
                V(lambda e: e.tensor_tensor(out=tA[:, :, 0:1], in0=cn[:], in1=cn[:], op=ALU.mult))
                V(lambda e: e.tensor_tensor(out=tB[:, :, 0:1], in0=sn_[:], in1=sn_[:], op=ALU.mult))
                V(lambda e: e.tensor_tensor(out=tB[:, :, 1:2], in0=cn[:], in1=sn_[:], op=ALU.mult))
                V(lambda e: e.tensor_tensor(out=cn[:], in0=tA[:, :, 0:1], in1=tB[:, :, 0:1], op=ALU.subtract))
                V(lambda e: e.tensor_scalar(out=sn_[:], in0=tB[:, :, 1:2], scalar1=2.0, scalar2=None, op0=ALU.mult))
                n *= 2
            V(lambda e: e.tensor_copy(out=s5c[l][:, 2, :], in_=sn_[:, :, 0]))
            P.op("dve", lambda e: e.tensor_copy(out=s5c[l][:, 1, :], in_=cn[:, :, 0]), [RS], [RS, R_s5c[l]])
            for j in range(8):
                P.op("sp", lambda e: e.dma_start(out=tscr[l * 8 + j, :, 0:T], in_=Ct[:, j, :]),
                     [RS], [R_wscr[l][PIDX["tab%d" % j]]], dma_sem="tst")
                P.op("sp", lambda e: e.dma_start(out=tscr[l * 8 + j, :, T:2 * T], in_=St[:, j, :]),
                     [RS], [R_wscr[l][PIDX["tab%d" % j]]], dma_sem="tst")
            RR = RG("s5R")
            lam = st("lam", [128, 3, 1024])
            P.op("sp", lambda e: e.dma_start(out=lam[:], in_=lamR_d[l, :, :, :]), [], [RR], dma_sem="c3")
            braw = st("braw", [128, 2, 1024])
            P.op("sp", lambda e: e.dma_start(out=braw[:].rearrange("p a b -> p (a b)"), in_=braw_d[l, :, :]), [], [RR], dma_sem="c4")
            craw = st("craw", [128, 2, 1024])
            P.op("sp", lambda e: e.dma_start(out=craw[:].rearrange("p a b -> p (a b)"), in_=craw_d[l, :, :]), [], [RR], dma_sem="c5")
            bb = st("bb", [128, 2, 1024], BF16)
            V = lambda fn: P.op("dve", fn, [RR], [RR])
            for cc in range(2):
                c0 = cc * 512
                lrA, liA, dtA = lam[:, 0, c0:c0 + 512], lam[:, 1, c0:c0 + 512], lam[:, 2, c0:c0 + 512]
                tR = s5_params(lrA, liA, dtA, [128, 512], "R%d" % cc, RR)
                ar, ai, den, cr, ci, u1 = [st("%s%d" % (n_, cc), [128, 512]) for n_ in ("ar", "ai", "den", "cr", "ci", "u1")]
                lrc = tR["lrc"]
                V(lambda e: e.tensor_tensor(out=ar[:], in0=tR["mag"][:], in1=tR["cs"][:], op=ALU.mult))
                V(lambda e: e.tensor_tensor(out=ai[:], in0=tR["mag"][:], in1=tR["sn"][:], op=ALU.mult))
                V(lambda e: e.tensor_tensor(out=den[:], in0=lrc[:], in1=lrc[:], op=ALU.mult))
                V(lambda e: e.tensor_tensor(out=u1[:], in0=liA, in1=liA, op=ALU.mult))
                V(lambda e: e.tensor_tensor(out=den[:], in0=den[:], in1=u1[:], op=ALU.add))
                V(lambda e: e.reciprocal(out=den[:], in_=den[:]))
                V(lambda e: e.tensor_scalar(out=ar[:], in0=ar[:], scalar1=-1.0, scalar2=None, op0=ALU.add))
                V(lambda e: e.tensor_tensor(out=cr[:], in0=ar[:], in1=lrc[:], op=ALU.mult))
                V(lambda e: e.tensor_tensor(out=u1[:], in0=ai[:], in1=liA, op=ALU.mult))
                V(lambda e: e.tensor_tensor(out=cr[:], in0=cr[:], in1=u1[:], op=ALU.add))
                V(lambda e: e.tensor_tensor(out=cr[:], in0=cr[:], in1=den[:], op=ALU.mult))
                V(lambda e: e.tensor_tensor(out=ci[:], in0=ai[:], in1=lrc[:], op=ALU.mult))
                V(lambda e: e.tensor_tensor(out=u1[:], in0=ar[:], in1=liA, op=ALU.mult))
                V(lambda e: e.tensor_tensor(out=ci[:], in0=ci[:], in1=u1[:], op=ALU.subtract))
                V(lambda e: e.tensor_tensor(out=ci[:], in0=ci[:], in1=den[:], op=ALU.mult))
                bre, bim = braw[:, 0, c0:c0 + 512], braw[:, 1, c0:c0 + 512]
                V(lambda e: e.tensor_tensor(out=u1[:], in0=cr[:], in1=bre, op=ALU.mult))
                V(lambda e: e.tensor_tensor(out=den[:], in0=ci[:], in1=bim, op=ALU.mult))
                V(lambda e: e.tensor_tensor(out=bb[:, 0, c0:c0 + 512], in0=u1[:], in1=den[:], op=ALU.subtract))
                V(lambda e: e.tensor_tensor(out=u1[:], in0=cr[:], in1=bim, op=ALU.mult))
                V(lambda e: e.tensor_tensor(out=den[:], in0=ci[:], in1=bre, op=ALU.mult))
                V(lambda e: e.tensor_tensor(out=bb[:, 1, c0:c0 + 512], in0=u1[:], in1=den[:], op=ALU.add))
            P.op("sp", lambda e: e.dma_start(out=scr(l, "bbar"), in_=bb[:].rearrange("p a b -> p (a b)")),
                 [RR], [R_wscr[l][PIDX["bbar"]]], dma_sem="tst")
            cb_ = st("cb", [128, 2, 1024], BF16)
            V(lambda e: e.tensor_copy(out=cb_[:, 0, :], in_=craw[:, 0, :]))
            V(lambda e: e.tensor_scalar(out=cb_[:, 1, :], in0=craw[:, 1, :], scalar1=-1.0, scalar2=None, op0=ALU.mult))
            P.op("sp", lambda e: e.dma_start(out=scr(l, "cmat"), in_=cb_[:].rearrange("p a b -> p (a b)")),
                 [RR], [R_wscr[l][PIDX["cmat"]]], dma_sem="tst")
            RM = RG("me")
            bt = st("bt", [128, 8, 640])
            mb = st("mb", [128, 8, 640], BF16)
            P.op("sp", lambda e: e.dma_start(out=bt[:].rearrange("p a b -> p (a b)"), in_=btoe_d[l, :, :]), [], [RM], dma_sem="c6")
            if BIAS_PE:
                negm = st("negm", [128, 640])
                P.op("dve", lambda e: e.tensor_scalar(out=negm[:], in0=maskt[:, 0:640], scalar1=-1.0, scalar2=30000.0, op0=ALU.add, op1=ALU.mult), [R_mask], [RM])
                P.op("dve", lambda e: e.scalar_tensor_tensor(out=bt[:], in0=bt[:], scalar=8.0, in1=maskt[:, 0:640].unsqueeze(1).to_broadcast([128, 8, 640]),
                                                             op0=ALU.mult, op1=ALU.mult), [RM, R_mask], [RM])
                P.op("dve", lambda e: e.tensor_tensor(out=mb[:], in0=bt[:], in1=negm[:].unsqueeze(1).to_broadcast([128, 8, 640]), op=ALU.add), [RM], [RM])
            else:
                P.op("act", lambda e: e.activation(out=bt[:], in_=bt[:], func=AF.Exp), [RM], [RM])
                P.op("dve", lambda e: e.tensor_tensor(out=mb[:], in0=bt[:], in1=maskt[:, 0:640].unsqueeze(1).to_broadcast([128, 8, 640]), op=ALU.mult), [RM, R_mask], [RM])
            for hc in range(4):
                P.op("sp", lambda e: e.dma_start(out=scr(l, "me%d" % hc), in_=mb[:, 2 * hc:2 * hc + 2, :].rearrange("p a b -> p (a b)")),
                     [RM], [R_wscr[l][PIDX["me%d" % hc]]], dma_sem="tst")
            RD = RG("cwd")
            for c in range(2):
                dg = st("dg%d" % c, [128, 31, 128], BF16)
                for k in range(31):
                    P.op("dve", lambda e: e.tensor_scalar(out=dg[:, k, :], in0=ident[:], scalar1=PV[l][:, PV_CW + c * 31 + k:PV_CW + c * 31 + k + 1],
                                                                                scalar2=None, op0=ALU.mult), [R_pv[l], R_ident], [RD])
                P.op("sp", lambda e: e.dma_start(out=scr(l, "cw%d" % c), in_=dg[:].rearrange("p a b -> p (a b)")),
                     [RD], [R_wscr[l][PIDX["cw%d" % c]]], dma_sem="tst")
            P.barrier()

    xres = sb("xres", [128, KC, T], F32)
    hT = sb("hT", [128, KC, T], BF16)
    hid = sb("hid", [128, HC, T], BF16)
    sq = sb("sq", [128, KC, T], BF16)
    NTMP = 10
    tmpall = sb("tmpall", [128, NTMP, T], F32)
    du = sb("du", [128, 2, T], F32)
    cv = sb("cv", [128, 2, T], F32)
    NSLOT = 6
    slots = [sb("slot%d" % i, [128, SLOTW], BF16) for i in range(NSLOT)]
    kbuf = [sb("kbuf%d" % l, [128, 4, 2, T], BF16) for l in range(L)]
    vbuf = [sb("vbuf%d" % l, [128, 8, 512], BF16) for l in range(L)]
    hbuf = [sb("hbuf%d" % l, [128, 2, 32 + T], BF16) for l in range(L)]
    ps = [es.enter_context(nc.psum_tensor("ps%d" % i, [128, T], F32)) for i in range(8)]
    R_x = [Reg("x%d" % k) for k in range(KC)]
    R_h = [Reg("h%d" % k) for k in range(KC)]
    R_hid = [Reg("hid%d" % k) for k in range(HC)]
    R_sq = [Reg("sq%d" % k) for k in range(KC)]
    R_tmp = [Reg("tmp%d" % k) for k in range(NTMP)]
    R_du, R_cv = Reg("du"), [Reg("cv0"), Reg("cv1")]
    R_slot = [Reg("slot%d" % i) for i in range(NSLOT)]
    R_k = [[Reg("k%d_%d" % (l, h)) for h in range(2)] for l in range(L)]
    R_v = [[Reg("v%d_%d" % (l, h)) for h in range(2)] for l in range(L)]
    R_hb = [[Reg("hb%d_%d" % (l, c)) for c in range(2)] for l in range(L)]
    R_ps = [Reg("ps%d" % i, excl=True) for i in range(8)]
    R_xd, R_yd = Reg("xd"), Reg("yd")

    bank_free = list(range(8))

    cur_pool = [bank_free]

    def bank():
        return cur_pool[0].pop(0)

    def unbank(b):
        cur_pool[0].append(b)

    tmp_rr = [0]

    def tmp():
        i = tmp_rr[0] % NTMP
        tmp_rr[0] += 1
        return tmpall[:, i, :], R_tmp[i]

    order = []
    DRY = [True]
    ring = {"next_load": 0, "next_use": 0, "free": list(range(NSLOT)), "where": {}}

    def ring_fill():
        if DRY[0]:
            return
        while ring["free"] and ring["next_load"] < len(order):
            idx = ring["next_load"]
            l, n_ = order[idx]
            sl = ring["free"].pop(0)
            i = PIDX[n_]
            w_ = PIECES[i][1]
            if n_.startswith("tab"):
                j = int(n_[3:])
                src = tscr[l * 8 + j, :, :]
                dst = slots[sl][:, 0:2048].bitcast(F32)
            else:
                src = scr(l, n_)
                dst = slots[sl][:, 0:w_]
            P.op("sp", lambda e, dst=dst, src=src: e.dma_start(out=dst, in_=src), [R_wscr[l][i]], [R_slot[sl]], dma_sem="ring%d" % sl)
            ring["where"][idx] = sl
            ring["next_load"] += 1

    def ring_get(l, name):
        if DRY[0]:
            order.append((l, name))
            return slots[0], R_slot[0], 0
        idx = ring["next_use"]
        assert order[idx] == (l, name), (order[idx], l, name)
        assert idx in ring["where"], "ring deadlock at %s" % name
        ring["next_use"] += 1
        sl = ring["where"][idx]
        return slots[sl], R_slot[sl], sl

    def ring_rel(sl):
        if DRY[0]:
            return
        ring["free"].append(sl)
        ring_fill()

    def mm(out, lhsT, rhs, start, stop, reads, writes, **kw):
        P.op("pe", lambda e: e.matmul(out, lhsT=lhsT, rhs=rhs, start=start, stop=stop, **kw), reads, writes)

    def rmsnorm(l, col):
        for k in range(KC):
            if k % 2 == 1:
                P.op("act", lambda e: e.activation(out=sq[:, k, :], in_=xres[:, k, :], func=AF.Square), [R_x[k]], [R_sq[k]])
            else:
                P.op("pool", lambda e: e.tensor_tensor(out=sq[:, k, :], in0=xres[:, k, :], in1=xres[:, k, :], op=ALU.mult), [R_x[k]], [R_sq[k]])
        b = bank()
        for k in range(KC):
            mm(ps[b][:], ones_bf[:], sq[:, k, :], k == 0, k == KC - 1, [R_sq[k]], [R_ps[b]])
        rs, rr = tmp()
        P.op("act", lambda e: e.activation(out=rs, in_=ps[b][:], func=AF.Ln, bias=EPS, scale=1.0 / D), [R_ps[b]], [rr])
        unbank(b)
        P.op("act", lambda e: e.activation(out=rs, in_=rs, func=AF.Exp, scale=-0.5), [rr], [rr])
        for k in range(KC):
            P.op("dve", lambda e, k=k: e.scalar_tensor_tensor(out=hT[:, k, :], in0=xres[:, k, :], scalar=PV[l][:, col + k:col + k + 1],
                                                              in1=rs, op0=ALU.mult, op1=ALU.mult), [R_x[k], rr], [R_h[k]])

    def ffn(l, f):
        rmsnorm(l, PV_N1 if f == 0 else PV_N2)
        for j in range(11):
            sl, rs_, si = ring_get(l, "f%du%d" % (f + 1, j))
            if j == 0:
                bk = {(half, ab): bank() for half in range(2) for ab in range(2)}
                for k in range(KC):
                    for half in range(2):
                        for ab in range(2):
                            mm(ps[bk[(half, ab)]][:], sl[:, k * 512 + ab * 256 + half * 128: k * 512 + ab * 256 + half * 128 + 128], hT[:, k, :],
                               k == 0, k == KC - 1, [rs_, R_h[k]], [R_ps[bk[(half, ab)]]])
                for half in range(2):
                    m = half
                    bA, bB = bk[(half, 0)], bk[(half, 1)]
                    ta, ra = tmp()
                    P.op("act", lambda e: e.activation(out=ta, in_=ps[bA][:], func=AF.Silu), [R_ps[bA]], [ra])
                    unbank(bA)
                    P.op("dve", lambda e: e.tensor_tensor(out=hid[:, m, :], in0=ps[bB][:], in1=ta, op=ALU.mult), [R_ps[bB], ra], [R_hid[m]])
                    unbank(bB)
                ring_rel(si)
                continue
            for half in range(2):
                m = 2 * j + half
                bA, bB = bank(), bank()
                for k in range(KC):
                    mm(ps[bA][:], sl[:, k * 512 + half * 128: k * 512 + half * 128 + 128], hT[:, k, :], k == 0, k == KC - 1,
                       [rs_, R_h[k]], [R_ps[bA]])
                for k in range(KC):
                    mm(ps[bB][:], sl[:, k * 512 + 256 + half * 128: k * 512 + 256 + half * 128 + 128], hT[:, k, :], k == 0, k == KC - 1,
                       [rs_, R_h[k]], [R_ps[bB]])
                ta, ra = tmp()
                P.op("act", lambda e, bA=bA, ta=ta: e.activation(out=ta, in_=ps[bA][:], func=AF.Silu), [R_ps[bA]], [ra])
                unbank(bA)
                P.op("dve", lambda e, bB=bB, ta=ta, m=m: e.tensor_tensor(out=hid[:, m, :], in0=ps[bB][:], in1=ta, op=ALU.mult),
                     [R_ps[bB], ra], [R_hid[m]])
                unbank(bB)
            ring_rel(si)
        bd = [bank() for _ in range(KC)]
        for j in range(6):
            nk = 4 if j < 5 else 2
            sl, rs_, si = ring_get(l, "f%dd%d" % (f + 1, j))
            for m in range(KC):
                for kk in range(nk):
                    k = 4 * j + kk
                    mm(ps[bd[m]][:], sl[:, kk * 1024 + m * 128: kk * 1024 + m * 128 + 128], hid[:, k, :], k == 0, k == HC - 1,
                       [rs_, R_hid[k]], [R_ps[bd[m]]])
            ring_rel(si)
        for m in range(KC):
            P.op("dve", lambda e, m=m: e.scalar_tensor_tensor(out=xres[:, m, :], in0=ps[bd[m]][:], scalar=0.5, in1=xres[:, m, :],
                                                              op0=ALU.mult, op1=ALU.add), [R_ps[bd[m]], R_x[m]], [R_x[m]])
            unbank(bd[m])

    MERG, QN, UT, ATT, S5O, CVO = 0, 8, 12, 14, 18, 20
    branches = dbg.get("branches", (0, 1, 2))
    NS5T = 12
    s5tmp = sb("s5tmp", [128, NS5T, T], F32)
    R_s5t = [Reg("s5t%d" % k) for k in range(NS5T)]
    ebuf = sb("ebuf", [128, 4, T], BF16)
    R_e = [Reg("e%d" % k) for k in range(4)]

    def mixer(l, i):
        half, hhalf = i % 2, 1 - i % 2
        pvl = PV[l]
        rmsnorm(l, PV_NM)
        hb = hbuf[l]
        sl, rs_, si = ring_get(l, "inU")
        for c in range(2):
            b = bank()
            for k in range(KC):
                mm(ps[b][:], sl[:, k * 256 + c * 128: k * 256 + c * 128 + 128], hT[:, k, :], k == 0, k == KC - 1, [rs_, R_h[k]], [R_ps[b]])
            P.op("act", lambda e: e.activation(out=hid[:, UT + c, :], in_=ps[b][:], func=AF.Copy), [R_ps[b]], [R_hid[UT + c]])
            P.op("act", lambda e: e.activation(out=du[:, c, :], in_=ps[b][:], func=AF.Copy, scale=pvl[:, PV_D + c:PV_D + c + 1]), [R_ps[b]], [R_du])
            unbank(b)
        ring_rel(si)

        def attention_hc(hc, slM, rM):
            bnum, bden = bank(), bank()
            steps = []
            for hl in range(2):
                for kbi in range(8):
                    if 4 * i - 4 + kbi >= 0:
                        steps.append((hl, kbi))
            firsts = {0: True, 1: True}
            info = {}
            for n in range(len(steps) + 3):
                if n < len(steps):
                    hl, kbi = steps[n]
                    p0 = 64 * hl
                    gkb = 4 * i - 4 + kbi
                    khalf = half if kbi >= 4 else hhalf
                    kcol = (kbi % 4) * 128
                    q_lo, q_hi = max(0, kbi - 4), min(3, kbi)
                    nq = q_hi - q_lo + 1
                    rel_lo = 4 * i + q_lo - gkb
                    bs = bank()
                    me = slM[:, hl * 640 + rel_lo * 128: hl * 640 + (rel_lo + nq) * 128]
                    ei = n % 4
                    mm(ps[bs][:, 0:nq * 128], kbuf[l][p0:p0 + 64, hc, khalf, kcol:kcol + 128], hid[p0:p0 + 64, QN + hc, q_lo * 128:(q_hi + 1) * 128],
                       True, True, [R_k[l][khalf], R_hid[QN + hc]], [R_ps[bs]])
                    P.op("act", lambda e: e.activation(out=ebuf[:, ei, 0:nq * 128], in_=ps[bs][:, 0:nq * 128], func=AF.Exp, scale=0.125), [R_ps[bs]], [R_e[ei]])
                    P.op("pool", lambda e: e.tensor_tensor(out=ebuf[:, ei, 0:nq * 128], in0=ebuf[:, ei, 0:nq * 128], in1=me, op=ALU.mult), [R_e[ei], rM], [R_e[ei]])
                    unbank(bs)
                    info[n] = (hl, kbi, khalf, q_lo, q_hi, ei)
                if n >= 3:
                    hl, kbi, khalf, q_lo, q_hi, ei = info[n - 3]
                    p0 = 64 * hl
                    h = 2 * hc + hl
                    nq = q_hi - q_lo + 1
                    mm(ps[bnum][p0:p0 + 64, q_lo * 128:(q_hi + 1) * 128], vbuf[l][:, khalf * 4 + kbi % 4, h * 64:(h + 1) * 64], ebuf[:, ei, 0:nq * 128],
                       firsts[hl], False, [R_v[l][khalf], R_e[ei]], [R_ps[bnum]], skip_group_check=True)
                    mm(ps[bden][p0:p0 + 64, q_lo * 128:(q_hi + 1) * 128], ones_bf[:, 0:64], ebuf[:, ei, 0:nq * 128],
                       firsts[hl], False, [R_e[ei]], [R_ps[bden]], skip_group_check=True)
                    firsts[hl] = False
                yield
            td, rd = tmp()
            tn, rn = tmp()
            P.op("act", lambda e: e.activation(out=tn, in_=ps[bnum][:], func=AF.Copy), [R_ps[bnum]], [rn])
            unbank(bnum)
            P.op("act", lambda e: e.activation(out=td, in_=ps[bden][:], func=AF.Ln), [R_ps[bden]], [rd])
            unbank(bden)
            P.op("act", lambda e: e.activation(out=td, in_=td, func=AF.Exp, scale=-1.0), [rd], [rd])
            P.op("pool", lambda e: e.tensor_tensor(out=hid[:, ATT + hc, :], in0=tn, in1=td, op=ALU.mult), [rn, rd], [R_hid[ATT + hc]])

        def side():
            sl, rs_, si = ring_get(l, "inA")
            if i == 0:
                for c in range(2):
                    P.op("pool", lambda e: e.memset(hb[:, c, 0:32], 0.0), [], [R_hb[l][c]])
            for c in range(2):
                bA, bG = bank(), bank()
                for k in range(KC):
                    mm(ps[bA][:], sl[:, k * 512 + c * 128: k * 512 + c * 128 + 128], hT[:, k, :], k == 0, k == KC - 1, [rs_, R_h[k]], [R_ps[bA]])
                for k in range(KC):
                    mm(ps[bG][:], sl[:, k * 512 + 256 + c * 128: k * 512 + 256 + c * 128 + 128], hT[:, k, :], k == 0, k == KC - 1, [rs_, R_h[k]], [R_ps[bG]])
                tg, rg = tmp()
                P.op("act", lambda e: e.activation(out=tg, in_=ps[bG][:], func=AF.Sigmoid), [R_ps[bG]], [rg])
                unbank(bG)
                P.op("dve", lambda e: e.tensor_tensor(out=hb[:, c, 32:32 + T], in0=ps[bA][:], in1=tg, op=ALU.mult), [R_ps[bA], rg], [R_hb[l][c]])
                unbank(bA)
                yield
            ring_rel(si)
            for c in range(2):
                sl, rs_, si = ring_get(l, "cw%d" % c)
                b = bank()
                for k in range(31):
                    mm(ps[b][:], sl[:, k * 128:(k + 1) * 128], hb[:, c, 2 + k:2 + k + T], k == 0, k == 30, [rs_, R_hb[l][c]], [R_ps[b]])
                ring_rel(si)
                P.op("act", lambda e: e.activation(out=cv[:, c, :], in_=ps[b][:], func=AF.Identity, bias=pvl[:, PV_CB + c:PV_CB + c + 1], scale=1.0), [R_ps[b]], [R_cv[c]])
                unbank(b)
                P.op("pool", lambda e: e.tensor_copy(out=hb[:, c, 0:32], in_=hb[:, c, T:T + 32]), [R_hb[l][c]], [R_hb[l][c]])
                P.op("act", lambda e: e.activation(out=sq[:, c, :], in_=cv[:, c, :], func=AF.Copy), [R_cv[c]], [R_sq[c]])
                P.op("act", lambda e: e.activation(out=sq[:, 2 + c, :], in_=cv[:, c, :], func=AF.Square), [R_cv[c]], [R_sq[2 + c]])
                yield
            b1, b2 = bank(), bank()
            for c in range(2):
                mm(ps[b1][:], ones_bf[:], sq[:, c, :], c == 0, c == 1, [R_sq[c]], [R_ps[b1]])
            for c in range(2):
                mm(ps[b2][:], ones_bf[:], sq[:, 2 + c, :], c == 0, c == 1, [R_sq[2 + c]], [R_ps[b2]])
            tm, rm = tmp()
            tv, rv = tmp()
            P.op("act", lambda e: e.activation(out=tm, in_=ps[b1][:], func=AF.Copy, scale=1.0 / 256), [R_ps[b1]], [rm])
            unbank(b1)
            P.op("pool", lambda e: e.tensor_tensor(out=tv, in0=tm, in1=tm, op=ALU.mult), [rm], [rv])
            P.op("dve", lambda e: e.scalar_tensor_tensor(out=tv, in0=ps[b2][:], scalar=1.0 / 256, in1=tv, op0=ALU.mult, op1=ALU.subtract), [R_ps[b2], rv], [rv])
            unbank(b2)
            P.op("act", lambda e: e.activation(out=tv, in_=tv, func=AF.Ln, bias=EPS, scale=1.0), [rv], [rv])
            P.op("act", lambda e: e.activation(out=tv, in_=tv, func=AF.Exp, scale=-0.5), [rv], [rv])
            for c in range(2):
                P.op("pool", lambda e: e.tensor_tensor(out=cv[:, c, :], in0=cv[:, c, :], in1=tm, op=ALU.subtract), [R_cv[c], rm], [R_cv[c]])
                P.op("pool", lambda e: e.tensor_tensor(out=cv[:, c, :], in0=cv[:, c, :], in1=tv, op=ALU.mult), [R_cv[c], rv], [R_cv[c]])
            yield

            def ln_silu():
                for c in range(2):
                    P.op("act", lambda e: e.activation(out=hid[:, CVO + c, :], in_=cv[:, c, :], func=AF.Silu, bias=pvl[:, PV_LB + c:PV_LB + c + 1],
                                                       scale=pvl[:, PV_LG + c:PV_LG + c + 1]), [R_cv[c]], [R_hid[CVO + c]])
            qk_st = {}

            def qk_PE(cq):
                isq, c = cq < 4, cq % 4
                if cq == 0:
                    qk_st["sl"] = ring_get(l, "inQ")
                if cq == 4:
                    ring_rel(qk_st["sl"][2])
                    qk_st["sl"] = ring_get(l, "inK")
                sl, rs_, si = qk_st["sl"]
                b = bank()
                for k in range(KC):
                    mm(ps[b][:], sl[:, k * 512 + c * 128: k * 512 + c * 128 + 128], hT[:, k, :], k == 0, k == KC - 1, [rs_, R_h[k]], [R_ps[b]])
                tq, rq = tmp()
                sqi = cq % 4
                P.op("act", lambda e: e.activation(out=sq[:, sqi, :], in_=ps[b][:], func=AF.Square), [R_ps[b]], [R_sq[sqi]])
                P.op("act", lambda e: e.activation(out=tq, in_=ps[b][:], func=AF.Copy), [R_ps[b]], [rq])
                unbank(b)
                qk_st[cq] = (tq, rq, sqi)
                if cq == 7:
                    ring_rel(si)

            def qk_BLF(cq):
                isq, c = cq < 4, cq % 4
                tq, rq, sqi = qk_st[cq]
                gcol = PV_QG if isq else PV_KG
                b2 = bank()
                mm(ps[b2][:], bones_bf[:], sq[:, sqi, :], True, True, [R_sq[sqi]], [R_ps[b2]])
                tr, rr = tmp()
                P.op("act", lambda e: e.activation(out=tr, in_=ps[b2][:], func=AF.Ln, bias=EPS, scale=1.0 / 64), [R_ps[b2]], [rr])
                unbank(b2)
                P.op("act", lambda e: e.activation(out=tr, in_=tr, func=AF.Exp, scale=-0.5), [rr], [rr])
                dst = hid[:, QN + c, :] if isq else kbuf[l][:, c, half, :]
                dreg = [R_hid[QN + c]] if isq else [R_k[l][half]]
                P.op("dve", lambda e: e.scalar_tensor_tensor(out=dst, in0=tq, scalar=pvl[:, gcol:gcol + 1], in1=tr,
                                                             op0=ALU.mult, op1=ALU.mult), [rq, rr], dreg)
            for cq in range(9):
                if cq < 8:
                    qk_PE(cq)
                if cq == 2:
                    ln_silu()
                if cq >= 1:
                    qk_BLF(cq - 1)
                yield
            sl, rs_, si = ring_get(l, "inV")
            for tb in range(4):
                b = bank()
                for k in range(KC):
                    mm(ps[b][:], hT[:, k, tb * 128:(tb + 1) * 128], sl[:, k * 512:(k + 1) * 512], k == 0, k == KC - 1, [rs_, R_h[k]], [R_ps[b]])
                P.op("act", lambda e: e.activation(out=vbuf[l][:, half * 4 + tb, :], in_=ps[b][:], func=AF.Copy), [R_ps[b]], [R_v[l][half]])
                unbank(b)
                yield
            ring_rel(si)
            for hc in range(4):
                slM, rM, siM = ring_get(l, "me%d" % hc)
                for _ in attention_hc(hc, slM, rM):
                    yield
                ring_rel(siM)

        slB, rB, siB = ring_get(l, "bbar")
        slC, rC, siC = ring_get(l, "cmat")
        if i == 0:
            P.op("pool", lambda e: e.memset(s5c[l][:, 3:5, :], 0.0), [], R_init[l])
        pend = None
        ysb = None
        assert len(bank_free) == 8
        pool_s5 = [bank_free.pop(0) for _ in range(3)]
        pool_side = list(bank_free)
        del bank_free[:]
        cur_pool[0] = pool_s5
        sgen = side()
        side_total = 18 + 4 * (3 + (8 if i == 0 else 16))
        side_done = 0
        for it in range(9):
            if pend is not None:
                j, (t1, r1), (t2, r2), (t3, r3), (t4, r4), (t5, r5), (t6, r6), cosT, sinT, rT, siT = pend
                xr_i, xi_i = 4 + (j % 2) * 2, 5 + (j % 2) * 2
                P.op("dve", lambda e: e.tensor_tensor(out=t1, in0=t2, in1=cosT, op=ALU.mult), [r2, rT], [r1])
                P.op("dve", lambda e: e.tensor_tensor(out=t3, in0=t4, in1=sinT, op=ALU.mult), [r4, rT], [r3])
                P.op("dve", lambda e: e.tensor_tensor(out=sq[:, xr_i, :], in0=t1, in1=t3, op=ALU.subtract), [r1, r3], [R_sq[xr_i]])
                P.op("dve", lambda e: e.tensor_tensor(out=t5, in0=t4, in1=cosT, op=ALU.mult), [r4, rT], [r5])
                P.op("dve", lambda e: e.tensor_tensor(out=t6, in0=t2, in1=sinT, op=ALU.mult), [r2, rT], [r6])
                P.op("dve", lambda e: e.tensor_tensor(out=sq[:, xi_i, :], in0=t5, in1=t6, op=ALU.add), [r5, r6], [R_sq[xi_i]])
                ring_rel(siT)
            if it < 8:
                j = it
                cc = j // 4
                slT, rT, siT = ring_get(l, "tab%d" % j)
                tabf = slT[:, 0:2048].bitcast(F32)
                cosT, sinT = tabf[:, 0:T], tabf[:, T:2 * T]
                br_, bi_ = bank(), bank()
                mm(ps[br_][:], slB[:, cc * 512 + (j % 4) * 128: cc * 512 + (j % 4) * 128 + 128], hid[:, UT + cc, :], True, True, [rB, R_hid[UT + cc]], [R_ps[br_]])
                mm(ps[bi_][:], slB[:, 1024 + cc * 512 + (j % 4) * 128: 1024 + cc * 512 + (j % 4) * 128 + 128], hid[:, UT + cc, :], True, True, [rB, R_hid[UT + cc]], [R_ps[bi_]])
                tt = [(s5tmp[:, (6 * j + q) % NS5T, :], R_s5t[(6 * j + q) % NS5T]) for q in range(6)]
                (t1, r1), (t2, r2), (t3, r3), (t4, r4), (t5, r5), (t6, r6) = tt
                P.op("dve", lambda e: e.tensor_tensor(out=t1, in0=ps[br_][:], in1=cosT, op=ALU.mult), [R_ps[br_], rT], [r1])
                P.op("dve", lambda e: e.tensor_tensor(out=t4, in0=ps[br_][:], in1=sinT, op=ALU.mult), [R_ps[br_], rT], [r4])
                unbank(br_)
                P.op("dve", lambda e: e.tensor_tensor(out=t2, in0=ps[bi_][:], in1=sinT, op=ALU.mult), [R_ps[bi_], rT], [r2])
                P.op("dve", lambda e: e.tensor_tensor(out=t3, in0=ps[bi_][:], in1=cosT, op=ALU.mult), [R_ps[bi_], rT], [r3])
                unbank(bi_)
                P.op("dve", lambda e: e.tensor_tensor(out=t1, in0=t1, in1=t2, op=ALU.add), [r1, r2], [r1])
                P.op("dve", lambda e: e.tensor_tensor(out=t3, in0=t3, in1=t4, op=ALU.subtract), [r3, r4], [r3])
                rdec = s5c[l][:, 0, j:j + 1].to_broadcast([128, T])
                P.op("dve", lambda e: e.tensor_tensor_scan(out=t2, data0=rdec, data1=t1, initial=s5c[l][:, 3, j:j + 1],
                                                           op0=ALU.mult, op1=ALU.add), [r1, R_init[l][j], R_s5c[l]], [r2])
                P.op("dve", lambda e: e.tensor_tensor_scan(out=t4, data0=rdec, data1=t3, initial=s5c[l][:, 4, j:j + 1],
                                                           op0=ALU.mult, op1=ALU.add), [r3, R_init[l][j], R_s5c[l]], [r4])
                P.op("dve", lambda e: e.tensor_scalar(out=t1[:, 0:1], in0=t4[:, T - 1:T], scalar1=s5c[l][:, 2, j:j + 1], scalar2=None, op0=ALU.mult),
                     [r4, R_s5c[l]], [r1])
                P.op("dve", lambda e: e.tensor_scalar(out=t1[:, 1:2], in0=t2[:, T - 1:T], scalar1=s5c[l][:, 2, j:j + 1], scalar2=None, op0=ALU.mult),
                     [r2, R_s5c[l]], [r1])
                P.op("dve", lambda e: e.scalar_tensor_tensor(out=s5c[l][:, 3, j:j + 1], in0=t2[:, T - 1:T], scalar=s5c[l][:, 1, j:j + 1], in1=t1[:, 0:1],
                                                             op0=ALU.mult, op1=ALU.subtract), [r2, r1, R_s5c[l]], [R_init[l][j]])
                P.op("dve", lambda e: e.scalar_tensor_tensor(out=s5c[l][:, 4, j:j + 1], in0=t4[:, T - 1:T], scalar=s5c[l][:, 1, j:j + 1], in1=t1[:, 1:2],
                                                             op0=ALU.mult, op1=ALU.add), [r4, r1, R_s5c[l]], [R_init[l][j]])
                newpend = (j, (t1, r1), (t2, r2), (t3, r3), (t4, r4), (t5, r5), (t6, r6), cosT, sinT, rT, siT)
            else:
                newpend = None
            if sgen is not None:
                cur_pool[0] = pool_side
                nsteps = -(-(side_total - side_done) // (9 - it))
                for _ in range(nsteps):
                    try:
                        next(sgen)
                        side_done += 1
                    except StopIteration:
                        sgen = None
                        break
                cur_pool[0] = pool_s5
            if pend is not None:
                j = pend[0]
                cc = j // 4
                xr_i, xi_i = 4 + (j % 2) * 2, 5 + (j % 2) * 2
                if j % 4 == 0:
                    ysb = bank()
                mm(ps[ysb][:], slC[:, j * 128:(j + 1) * 128], sq[:, xr_i, :], j % 4 == 0, False, [rC, R_sq[xr_i]], [R_ps[ysb]])
                mm(ps[ysb][:], slC[:, 1024 + j * 128:1024 + (j + 1) * 128], sq[:, xi_i, :], False, j % 4 == 3, [rC, R_sq[xi_i]], [R_ps[ysb]])
                if j % 4 == 3:
                    (ty, ry), (tz, rz) = tmp(), tmp()
                    P.op("dve", lambda e: e.tensor_tensor(out=ty, in0=ps[ysb][:], in1=du[:, cc, :], op=ALU.add), [R_ps[ysb], R_du], [ry])
                    unbank(ysb)
                    P.op("act", lambda e: e.activation(out=tz, in_=ty, func=AF.Square), [ry], [rz])
                    P.op("act", lambda e: e.activation(out=tz, in_=tz, func=AF.Identity, bias=1.0, scale=0.044715), [rz], [rz])
                    P.op("pool", lambda e: e.tensor_tensor(out=tz, in0=tz, in1=ty, op=ALU.mult), [rz, ry], [rz])
                    P.op("act", lambda e: e.activation(out=tz, in_=tz, func=AF.Sigmoid, scale=1.5957691216057308), [rz], [rz])
                    P.op("pool", lambda e: e.tensor_tensor(out=hid[:, MERG + cc, :], in0=ty, in1=tz, op=ALU.mult), [ry, rz], [R_hid[MERG + cc]])
            pend = newpend
        ring_rel(siB)
        ring_rel(siC)
        if sgen is not None:
            cur_pool[0] = pool_side
            for _ in sgen:
                pass
        assert len(pool_s5) + len(pool_side) == 8, (pool_s5, pool_side)
        bank_free[:] = pool_s5 + pool_side
        cur_pool[0] = bank_free
        sl, rs_, si = ring_get(l, "glu")
        for c in range(2):
            bA, bG = bank(), bank()
            for k in range(2):
                mm(ps[bA][:], sl[:, k * 512 + c * 128: k * 512 + c * 128 + 128], hid[:, MERG + k, :], k == 0, k == 1, [rs_, R_hid[MERG + k]], [R_ps[bA]])
            for k in range(2):
                mm(ps[bG][:], sl[:, k * 512 + 256 + c * 128: k * 512 + 256 + c * 128 + 128], hid[:, MERG + k, :], k == 0, k == 1, [rs_, R_hid[MERG + k]], [R_ps[bG]])
            tg, rg = tmp()
            P.op("act", lambda e: e.activation(out=tg, in_=ps[bG][:], func=AF.Sigmoid), [R_ps[bG]], [rg])
            unbank(bG)
            P.op("dve", lambda e: e.tensor_tensor(out=hid[:, S5O + c, :], in0=ps[bA][:], in1=tg, op=ALU.mult), [R_ps[bA], rg], [R_hid[S5O + c]])
            unbank(bA)
        ring_rel(si)

        slS, rS, siS = ring_get(l, "brs")
        slA, rA, siA = ring_get(l, "bra")
        for m in range(8):
            slG, rG, siG = ring_get(l, "g%d" % m)
            acc = None
            bgs = [bank() for _ in range(3)]
            for b in range(3):
                for k in range(KC):
                    mm(ps[bgs[b]][:], slG[:, k * 384 + b * 128: k * 384 + b * 128 + 128], hT[:, k, :], k == 0, k == KC - 1, [rG, R_h[k]], [R_ps[bgs[b]]])
            bys = [bank() for _ in range(3)]
            for b in range(3):
                by = bys[b]
                if b == 0:
                    for k in range(2):
                        mm(ps[by][:], slS[:, k * 1024 + m * 128: k * 1024 + m * 128 + 128], hid[:, S5O + k, :], k == 0, k == 1, [rS, R_hid[S5O + k]], [R_ps[by]])
                elif b == 1:
                    for k in range(4):
                        mm(ps[by][:], slA[:, k * 1024 + m * 128: k * 1024 + m * 128 + 128], hid[:, ATT + k, :], k == 0, k == 3, [rA, R_hid[ATT + k]], [R_ps[by]])
                else:
                    for k in range(2):
                        mm(ps[by][:], slS[:, (2 + k) * 1024 + m * 128: (2 + k) * 1024 + m * 128 + 128], hid[:, CVO + k, :], k == 0, k == 1, [rS, R_hid[CVO + k]], [R_ps[by]])
            for b in range(3):
                bg, by = bgs[b], bys[b]
                tg, rg = tmp()
                P.op("act", lambda e: e.activation(out=tg, in_=ps[bg][:], func=AF.Sigmoid,
                                                   bias=pvl[:, PV_BG + b * 8 + m:PV_BG + b * 8 + m + 1], scale=1.0), [R_ps[bg]], [rg])
                unbank(bg)
                last = (b == max(branches)) and acc is not None
                if b not in branches:
                    unbank(by)
                    continue
                P.op("dve", lambda e: e.tensor_tensor(out=tg, in0=ps[by][:], in1=tg, op=ALU.mult), [R_ps[by], rg], [rg])
                if acc is None:
                    acc = (tg, rg)
                elif last:
                    P.op("pool", lambda e: e.tensor_tensor(out=hid[:, MERG + m, :], in0=acc[0], in1=tg, op=ALU.add), [acc[1], rg], [R_hid[MERG + m]])
                else:
                    P.op("pool", lambda e: e.tensor_tensor(out=acc[0], in0=acc[0], in1=tg, op=ALU.add), [acc[1], rg], [acc[1]])
                unbank(by)
            if len(branches) == 1:
                P.op("pool", lambda e: e.tensor_copy(out=hid[:, MERG + m, :], in_=acc[0]), [acc[1]], [R_hid[MERG + m]])
            ring_rel(siG)
        ring_rel(siS)
        ring_rel(siA)
        for jj in range(2):
            sl, rs_, si = ring_get(l, "wo%d" % jj)
            for mm_ in range(4):
                m = 4 * jj + mm_
                b = bank()
                for k in range(KC):
                    mm(ps[b][:], sl[:, k * 512 + mm_ * 128: k * 512 + mm_ * 128 + 128], hid[:, MERG + k, :], k == 0, k == KC - 1, [rs_, R_hid[MERG + k]], [R_ps[b]])
                P.op("dve", lambda e: e.tensor_tensor(out=xres[:, m, :], in0=ps[b][:], in1=xres[:, m, :], op=ALU.add), [R_ps[b], R_x[m]], [R_x[m]])
                unbank(b)
            ring_rel(si)

    def run_tiles():
        xtok = tmpall[:, 0:8, :].rearrange("p (a b) c -> p a (b c)", b=2)
        stages = dbg.get("stages", ("f1", "mix", "f2"))
        xin = s5tmp[:, 0:8, :].rearrange("p (a b) c -> p a (b c)", b=2)
        R_xin = R_s5t[0:8]

        def load_x(r0_):
            P.op("sp", lambda e: e.dma_start(out=xin, in_=x_d[r0_:r0_ + T, :].rearrange("(a p) c -> p a c", p=128)), [R_xd], R_xin, dma_sem="xin")
        tiles = [(s_, i) for s_ in range(NSEQ) for i in range(NT)]
        load_x(0)
        for ti, (s_, i) in enumerate(tiles):
            if True:
                r0 = s_ * S + i * T
                for k in range(KC):
                    b = bank()
                    for tb in range(4):
                        P.op("pe", lambda e, b=b, tb=tb, k=k: e.transpose(out=ps[b][:, tb * 128:(tb + 1) * 128], in_=xin[:, tb, k * 128:(k + 1) * 128], identity=ident[:]),
                             R_xin + [R_ident], [R_ps[b]])
                    P.op("act", lambda e, b=b, k=k: e.activation(out=xres[:, k, :], in_=ps[b][:], func=AF.Copy), [R_ps[b]], [R_x[k]])
                    unbank(b)
                for l in range(nlayers):
                    if "f1" in stages:
                        ffn(l, 0)
                    else:
                        for n_, w_ in PIECES[0:17]:
                            ring_rel(ring_get(l, n_)[2])
                    if "mix" in stages:
                        mixer(l, i)
                    else:
                        for n_, w_ in PIECES[17:NPL - 17]:
                            ring_rel(ring_get(l, n_)[2])
                    if l == nlayers - 1 and ti + 1 < len(tiles):
                        load_x(tiles[ti + 1][0] * S + tiles[ti + 1][1] * T)
                    if "f2" in stages:
                        ffn(l, 1)
                    else:
                        for n_, w_ in PIECES[NPL - 17:]:
                            ring_rel(ring_get(l, n_)[2])
                for tb in range(4):
                    for g in range(2):
                        b = bank()
                        for kk in range(4):
                            k = 4 * g + kk
                            P.op("pe", lambda e, b=b, tb=tb, k=k, kk=kk: e.transpose(out=ps[b][:, kk * 128:(kk + 1) * 128], in_=xres[:, k, tb * 128:(tb + 1) * 128], identity=ident[:]),
                                 [R_x[k], R_ident], [R_ps[b]])
                        P.op("act", lambda e, b=b, tb=tb, g=g: e.activation(out=xtok[:, tb, g * 512:(g + 1) * 512], in_=ps[b][:], func=AF.Copy), [R_ps[b]], R_tmp[0:8])
                        unbank(b)
                P.op("sp", lambda e, r0=r0: e.dma_start(out=y_d[r0:r0 + T, :].rearrange("(a p) c -> p a c", p=128), in_=xtok), R_tmp[0:8], [R_yd], dma_sem="yout")

    P_real = P
    P = Plan()
    run_tiles()
    for r_ in P.allregs.values():
        r_.w = None
        r_.readers = {}
    P = P_real
    DRY[0] = False
    bank_free[:] = list(range(8))
    tmp_rr[0] = 0
    for grp, members in cast_groups.items():
        fin = cast_fin[grp]
        for (l_, i_) in members:
            R_wscr[l_][i_].w = (grp, fin, "dma:" + grp)
    ring_fill()
    run_tiles()
    P.final_wait("sp", [R_yd])
    for e_ in COMPUTE:
        P.final_wait(e_, [R_yd])
    P.emit(nc)
    es.close()
    return nc


def host_layouts(inp):
    f = lambda a: np.asarray(a, dtype=np.float32)
    pv = np.zeros((L, 128, NPV), np.float32)
    lamR = np.zeros((L, 128, 3, 1024), np.float32)
    braw = np.zeros((L, 128, 2, 2, 512), np.float32)
    craw = np.zeros((L, 128, 2, 8, 128), np.float32)
    btoe = np.zeros((L, 128, 8, 5, 128), np.float32)
    kk = np.arange(128)[:, None, None]
    rel = np.arange(5)[None, :, None]
    qq = np.arange(128)[None, None, :]
    dist = 128 * rel + qq - kk
    idx = np.clip(dist, -128, 128) + 128
    dch = 2 * rel + (qq >= 64) - (kk >= 64)
    mask = ((dch >= 0) & (dch <= 8)).astype(np.float32).reshape(128, 640)
    mask01 = np.concatenate([mask, mask], axis=1)
    for l in range(L):
        fm = lambda v, n: f(v).reshape(n, 128).T
        pv[l, :, PV_N1:PV_N1 + 8] = fm(inp["ffn1_norm"][l], 8)
        pv[l, :, PV_NM:PV_NM + 8] = fm(inp["mix_norm"][l], 8)
        pv[l, :, PV_N2:PV_N2 + 8] = fm(inp["ffn2_norm"][l], 8)
        pv[l, :, PV_BG:PV_BG + 24] = fm(inp["b_gate"][l], 24)
        pv[l, :, PV_D:PV_D + 2] = fm(inp["s5_d"][l], 2)
        pv[l, :, PV_CB:PV_CB + 2] = fm(inp["conv_b_dw"][l], 2)
        pv[l, :, PV_LG:PV_LG + 2] = fm(inp["conv_ln_g"][l], 2)
        pv[l, :, PV_LB:PV_LB + 2] = fm(inp["conv_ln_b"][l], 2)
        cw = f(inp["conv_w_dw"][l])
        for c in range(2):
            pv[l, :, PV_CW + c * 31:PV_CW + c * 31 + 31] = cw[:, c * 128:(c + 1) * 128].T
        pv[l, :, PV_QG] = np.tile(f(inp["attn_q_gain"][l]), 2)
        pv[l, :, PV_KG] = np.tile(f(inp["attn_k_gain"][l]), 2)
        lre, lim, ldt = f(inp["s5_lambda_re"][l]), f(inp["s5_lambda_im"][l]), f(inp["s5_log_dt"][l])
        pv[l, :, PV_LR:PV_LR + 8] = lre.reshape(8, 128).T
        pv[l, :, PV_LI:PV_LI + 8] = lim.reshape(8, 128).T
        pv[l, :, PV_DT:PV_DT + 8] = np.repeat(ldt, 64).reshape(8, 128).T
        lamR[l, :, 0, :] = lre.reshape(1, 1024)
        lamR[l, :, 1, :] = lim.reshape(1, 1024)
        lamR[l, :, 2, :] = np.repeat(ldt, 64).reshape(1, 1024)
        bre, bim = f(inp["s5_b_re"][l]), f(inp["s5_b_im"][l])
        cre, cim = f(inp["s5_c_re"][l]), f(inp["s5_c_im"][l])
        for g in range(16):
            cc, gl = g // 8, g % 8
            braw[l, 16 * gl:16 * gl + 16, 0, cc, 64 * gl:64 * gl + 64] = bre[g].T
            braw[l, 16 * gl:16 * gl + 16, 1, cc, 64 * gl:64 * gl + 64] = bim[g].T
            j, g2 = g // 2, g % 2
            craw[l, 64 * g2:64 * g2 + 64, 0, j, 16 * gl:16 * gl + 16] = cre[g].T
            craw[l, 64 * g2:64 * g2 + 64, 1, j, 16 * gl:16 * gl + 16] = cim[g].T
        rb = f(inp["attn_rel_bias"][l])
        btoe[l] = np.transpose(rb[:, idx], (1, 0, 2, 3))
    return {"pv": pv, "lamR": lamR, "braw": braw.reshape(L, 128, 2048), "craw": craw.reshape(L, 128, 2048),
            "btoe": btoe.reshape(L, 128, 5120), "mask01": mask01, "ident": np.eye(128, dtype=np.float32)}


WKEYS = ["ffn1_w_up", "ffn2_w_up", "ffn1_w_down", "ffn2_w_down", "w_in", "s5_w_glu", "w_br_s5", "w_br_attn", "w_br_conv", "w_out"]


def run(inputs, n_cores, dbg=None):
    x = np.asarray(inputs["x"], dtype=np.float32)
    B, S, _ = x.shape
    nseq = B // n_cores
    nc = build_program(nseq, S, dbg)
    common = host_layouts(inputs)
    for k in WKEYS:
        common[k] = np.ascontiguousarray(np.asarray(inputs[k], dtype=np.float32))
    in_maps = []
    for c in range(n_cores):
        m = dict(common)
        m["x"] = np.ascontiguousarray(x[c * nseq:(c + 1) * nseq].reshape(nseq * S, D))
        in_maps.append(m)
    res = run_bass_kernel_spmd(nc, in_maps, core_ids=list(range(n_cores)))
    out = np.stack([np.asarray(r["y"]).reshape(nseq, S, D) for r in res.results], axis=0)
    return out.reshape(B, S, D).astype(np.float32)


def kernel(**inputs):
    return run(inputs, 8)
```
